# Optimizing a Trainium2 kernel written in Bass

```python
import math
import jax, jax.numpy as jnp
from jax import lax
import numpy as np

D_MODEL = 1024
BATCH = 4
SEQ = 4096
DEPTH = 2
DEC_BATCH = 32
DEC_SEQ = 4
PAST_LEN = 8192
PAGE_SIZE = 128

N_A_LAYERS = DEPTH // 2
N_B_LAYERS = DEPTH - N_A_LAYERS
D_INNER = 2 * D_MODEL
SSM_HEAD_DIM = 64
N_SSM_HEADS = D_INNER // SSM_HEAD_DIM
N_SSM_GROUPS = 4
HEADS_PER_GROUP = N_SSM_HEADS // N_SSM_GROUPS
D_STATE = 128
D_CONV = 4
SSD_CHUNK = 128
GN = N_SSM_GROUPS * D_STATE
CONV_DIM = D_INNER + 2 * GN
IN_DIM = D_INNER + CONV_DIM + N_SSM_HEADS
HEAD_DIM = 64
N_HEADS = D_MODEL // HEAD_DIM
N_KV_HEADS = 4
Q_PER_KV = N_HEADS // N_KV_HEADS
KV_DIM = N_KV_HEADS * HEAD_DIM
MOBA_BLOCK = 256
MOBA_TOP_K = 3
Q_CHUNK = 32
ROPE_THETA = 10000.0
D_FF = ((8 * D_MODEL + 3 * 256 - 1) // (3 * 256)) * 256
EPS = 1e-6

kernel_name = 'yoco_ssd_moba_decode_step'


def rmsnorm(x, g):
    xf = x.astype(jnp.float32)
    y = xf * lax.rsqrt(jnp.mean(xf * xf, axis=-1, keepdims=True) + EPS)
    return (y * g.astype(jnp.float32)).astype(x.dtype)


def rope(x, pos):
    half = HEAD_DIM // 2
    inv = ROPE_THETA ** (-jnp.arange(half, dtype=jnp.float32) / half)
    ang = pos.astype(jnp.float32)[:, None] * inv[None, :]
    cos = jnp.cos(ang)[None, :, None, :]
    sin = jnp.sin(ang)[None, :, None, :]
    xf = x.astype(jnp.float32)
    x1, x2 = xf[..., :half], xf[..., half:]
    return jnp.concatenate([x1 * cos - x2 * sin, x2 * cos + x1 * sin], axis=-1).astype(x.dtype)


def swiglu_ffn(h, w_gu, w_down):
    gu = h @ w_gu
    return (jax.nn.silu(gu[..., :D_FF]) * gu[..., D_FF:]) @ w_down


def ssd_scan(x, dt, a, bmat, cmat, h0):
    b, T = x.shape[:2]
    L = min(SSD_CHUNK, T)
    nc = T // L
    G, J, P, N = N_SSM_GROUPS, HEADS_PER_GROUP, SSM_HEAD_DIM, D_STATE
    f32 = jnp.float32
    xr = x.astype(f32).reshape(b, nc, L, G, J, P)
    dtr = dt.reshape(b, nc, L, G, J)
    br = bmat.astype(f32).reshape(b, nc, L, G, N)
    cr = cmat.astype(f32).reshape(b, nc, L, G, N)
    a_cs = jnp.cumsum(dtr * a.reshape(G, J), axis=2)
    xd = xr * dtr[..., None]
    causal = jnp.tril(jnp.ones((L, L), dtype=bool))[:, :, None, None]
    seg = jnp.exp(jnp.where(causal, a_cs[:, :, :, None] - a_cs[:, :, None], -jnp.inf))
    cb = jnp.einsum('bclgn,bcsgn->bclsg', cr, br)
    y_diag = jnp.einsum('bclsgj,bcsgjp->bclgjp', cb[..., None] * seg, xd)
    decay_to_end = jnp.exp(a_cs[:, :, -1:] - a_cs)
    chunk_states = jnp.einsum('bcsgn,bcsgjp->bcgjpn', br, xd * decay_to_end[..., None])
    chunk_decay = jnp.exp(a_cs[:, :, -1])

    def step(h, inp):
        dec, st = inp
        return h * dec[..., None, None] + st, h

    h_final, h_prev = lax.scan(step, h0.astype(f32).reshape(b, G, J, P, N),
                               (jnp.moveaxis(chunk_decay, 1, 0), jnp.moveaxis(chunk_states, 1, 0)))
    h_prev = jnp.moveaxis(h_prev, 0, 1)
    y_off = jnp.einsum('bclgn,bcgjpn->bclgjp', cr, h_prev) * jnp.exp(a_cs)[..., None]
    y = (y_diag + y_off).reshape(b, T, N_SSM_HEADS, P)
    return y, h_final.reshape(b, N_SSM_HEADS, P, N)


def mamba_mixer(h, conv0, ssm0, w_in, conv_w, conv_b, dt_bias, a_log, d_skip, g_norm, w_out):
    b, T, _ = h.shape
    proj = h @ w_in
    z = proj[..., :D_INNER]
    xbc = proj[..., D_INNER:D_INNER + CONV_DIM]
    dt_raw = proj[..., D_INNER + CONV_DIM:]
    padded = jnp.concatenate([conv0.astype(xbc.dtype), xbc], axis=1)
    conv = conv_b
    for i in range(D_CONV):
        conv = conv + padded[:, i:i + T] * conv_w[i]
    new_conv = padded[:, T:]
    xbc = jax.nn.silu(conv)
    xs = xbc[..., :D_INNER].reshape(b, T, N_SSM_HEADS, SSM_HEAD_DIM)
    bm = xbc[..., D_INNER:D_INNER + GN].reshape(b, T, N_SSM_GROUPS, D_STATE)
    cm = xbc[..., D_INNER + GN:].reshape(b, T, N_SSM_GROUPS, D_STATE)
    dt = jax.nn.softplus(dt_raw.astype(jnp.float32) + dt_bias.astype(jnp.float32))
    a = -jnp.exp(a_log.astype(jnp.float32))
    y, new_ssm = ssd_scan(xs, dt, a, bm, cm, ssm0)
    y = y + d_skip.astype(jnp.float32)[:, None] * xs.astype(jnp.float32)
    y = y.reshape(b, T, D_INNER) * jax.nn.silu(z.astype(jnp.float32))
    yg = y.reshape(b, T, N_SSM_GROUPS, D_INNER // N_SSM_GROUPS)
    yg = yg * lax.rsqrt(jnp.mean(yg * yg, axis=-1, keepdims=True) + EPS)
    y = yg.reshape(b, T, D_INNER) * g_norm.astype(jnp.float32)
    return y.astype(h.dtype) @ w_out, new_conv, new_ssm.astype(h.dtype)


def moba_blocks(k, v):
    b, T = k.shape[:2]
    nb = -(-T // MOBA_BLOCK)
    pad = nb * MOBA_BLOCK - T

    def blk(t):
        t = jnp.pad(t, ((0, 0), (0, pad), (0, 0), (0, 0)))
        return t.reshape(b, nb, MOBA_BLOCK, N_KV_HEADS, HEAD_DIM).transpose(0, 3, 1, 2, 4)

    kb, vb = blk(k), blk(v)
    kmean = jnp.mean(kb.astype(jnp.float32), axis=3)
    return kb, vb, kmean


def moba_attend(q, q_pos, kb, vb, kmean):
    b, nq = q.shape[:2]
    nb = kb.shape[2]
    qg = q.reshape(b, nq, N_KV_HEADS, Q_PER_KV, HEAD_DIM)
    n_sel = min(MOBA_TOP_K, nb - 1)
    qc = min(Q_CHUNK, nq)
    nc = nq // qc
    xs = {'q': jnp.swapaxes(qg.reshape(b, nc, qc, N_KV_HEADS, Q_PER_KV, HEAD_DIM), 0, 1),
          'pos': q_pos.reshape(nc, qc)}
    if n_sel > 0:
        q_blk = q_pos // MOBA_BLOCK
        gate = jnp.einsum('bqkgd,bknd->bkgqn', qg.astype(jnp.float32), kmean)
        fully_past = jnp.arange(nb)[None, :] < q_blk[:, None]
        gate = jnp.where(fully_past, gate, -jnp.inf)
        _, idx = lax.top_k(gate, n_sel)
        xs['idx'] = jnp.moveaxis(idx.reshape(b, N_KV_HEADS, Q_PER_KV, nc, qc, n_sel), 3, 0)
    bi = jnp.arange(b)[:, None, None, None, None]
    hi = jnp.arange(N_KV_HEADS)[None, :, None, None, None]
    key_off = jnp.arange(MOBA_BLOCK)
    scale = HEAD_DIM ** -0.5

    def chunk(c):
        qcb, pos = c['q'], c['pos']
        blk = pos // MOBA_BLOCK
        kown = kb[:, :, blk]
        vown = vb[:, :, blk]
        s_own = jnp.einsum('bqkgd,bkqld->bkgql', qcb, kown).astype(jnp.float32) * scale
        own_ok = (blk[:, None] * MOBA_BLOCK + key_off[None, :]) <= pos[:, None]
        s_own = jnp.where(own_ok, s_own, -jnp.inf)
        if n_sel > 0:
            idx = c['idx']
            ksel = kb[bi, hi, idx]
            vsel = vb[bi, hi, idx]
            s_sel = jnp.einsum('bqkgd,bkgqsld->bkgqsl', qcb, ksel).astype(jnp.float32) * scale
            sel_ok = (idx < blk[:, None])[..., None]
            s_sel = jnp.where(sel_ok, s_sel, -jnp.inf).reshape(s_sel.shape[:4] + (n_sel * MOBA_BLOCK,))
            p = jax.nn.softmax(jnp.concatenate([s_sel, s_own], axis=-1), axis=-1).astype(vb.dtype)
            p_sel = p[..., :n_sel * MOBA_BLOCK].reshape(s_own.shape[:4] + (n_sel, MOBA_BLOCK))
            p_own = p[..., n_sel * MOBA_BLOCK:]
            return (jnp.einsum('bkgqsl,bkgqsld->bqkgd', p_sel, vsel)
                    + jnp.einsum('bkgql,bkqld->bqkgd', p_own, vown))
        p_own = jax.nn.softmax(s_own, axis=-1).astype(vb.dtype)
        return jnp.einsum('bkgql,bkqld->bqkgd', p_own, vown)

    out = lax.map(chunk, xs)
    return jnp.swapaxes(out, 0, 1).reshape(b, nq, N_HEADS * HEAD_DIM)


def run_group(x, conv_in, ssm_in, past_k, past_v, p):
    b, T, _ = x.shape
    start = 0 if past_k is None else past_k.shape[1]
    pos = start + jnp.arange(T, dtype=jnp.int32)
    convs, ssms = [], []
    k_new = v_new = kb = vb = kmean = None
    for l in range(DEPTH):
        if l < N_A_LAYERS:
            conv0 = jnp.zeros((b, D_CONV - 1, CONV_DIM), x.dtype) if conv_in is None else conv_in[l]
            ssm0 = (jnp.zeros((b, N_SSM_HEADS, SSM_HEAD_DIM, D_STATE), jnp.float32)
                    if ssm_in is None else ssm_in[l])
            y, c_new, s_new = mamba_mixer(rmsnorm(x, p['norm_mix'][l]), conv0, ssm0,
                                          p['w_in_ssm'][l], p['conv_w'][l], p['conv_b'][l],
                                          p['dt_bias'][l], p['a_log'][l], p['d_skip'][l],
                                          p['norm_ssm'][l], p['w_out_ssm'][l])
            convs.append(c_new)
            ssms.append(s_new)
        else:
            if l == N_A_LAYERS:
                kv = rmsnorm(x, p['norm_kv']) @ p['w_kv']
                k_new = rope(kv[..., :KV_DIM].reshape(b, T, N_KV_HEADS, HEAD_DIM), pos)
                v_new = kv[..., KV_DIM:].reshape(b, T, N_KV_HEADS, HEAD_DIM)
                k_all = k_new if past_k is None else jnp.concatenate([past_k.astype(k_new.dtype), k_new], axis=1)
                v_all = v_new if past_v is None else jnp.concatenate([past_v.astype(v_new.dtype), v_new], axis=1)
                kb, vb, kmean = moba_blocks(k_all, v_all)
            j = l - N_A_LAYERS
            q = rope((rmsnorm(x, p['norm_mix'][l]) @ p['w_q'][j]).reshape(b, T, N_HEADS, HEAD_DIM), pos)
            y = moba_attend(q, pos, kb, vb, kmean) @ p['w_o'][j]
        x = x + y
        x = x + swiglu_ffn(rmsnorm(x, p['norm_ffn'][l]), p['w_gu'][l], p['w_down'][l])
    return rmsnorm(x, p['norm_final']), jnp.stack(convs), jnp.stack(ssms), k_new, v_new


def setup_inputs(seed: int = 0) -> dict:
    key = jax.random.key(seed)
    ks = jax.random.split(key, 26)
    f32 = jnp.float32
    n_pages = PAST_LEN // PAGE_SIZE
    n_used = DEC_BATCH * n_pages
    n_pool = n_used + (n_used + 3) // 4

    def nrm(k, shape, scale):
        return jax.random.normal(k, shape, f32) * scale

    dt0 = jnp.exp(jax.random.uniform(ks[0], (N_A_LAYERS, N_SSM_HEADS), f32, math.log(1e-3), math.log(1e-1)))
    page_table = jax.random.permutation(ks[1], n_pool)[:n_used].reshape(DEC_BATCH, n_pages).astype(jnp.int32)
    return {
        'x_prompt': nrm(ks[2], (BATCH, SEQ, D_MODEL), 1.0),
        'x_sample': nrm(ks[3], (DEC_BATCH, DEC_SEQ, D_MODEL), 1.0),
        'state_conv': nrm(ks[4], (N_A_LAYERS, DEC_BATCH, D_CONV - 1, CONV_DIM), 1.0),
        'state_ssm': nrm(ks[5], (N_A_LAYERS, DEC_BATCH, N_SSM_HEADS, SSM_HEAD_DIM, D_STATE), 0.1),
        'cache_k': nrm(ks[6], (n_pool, PAGE_SIZE, N_KV_HEADS, HEAD_DIM), 1.0),
        'cache_v': nrm(ks[7], (n_pool, PAGE_SIZE, N_KV_HEADS, HEAD_DIM), 1.0),
        'page_table': page_table,
        'norm_mix': 1.0 + nrm(ks[8], (DEPTH, D_MODEL), 0.02),
        'norm_ffn': 1.0 + nrm(ks[9], (DEPTH, D_MODEL), 0.02),
        'w_in_ssm': nrm(ks[10], (N_A_LAYERS, D_MODEL, IN_DIM), D_MODEL ** -0.5),
        'conv_w': nrm(ks[11], (N_A_LAYERS, D_CONV, CONV_DIM), D_CONV ** -0.5),
        'conv_b': nrm(ks[12], (N_A_LAYERS, CONV_DIM), 0.01),
        'dt_bias': dt0 + jnp.log(-jnp.expm1(-dt0)),
        'a_log': jnp.log(jax.random.uniform(ks[13], (N_A_LAYERS, N_SSM_HEADS), f32, 1.0, 16.0)),
        'd_skip': 1.0 + nrm(ks[14], (N_A_LAYERS, N_SSM_HEADS), 0.1),
        'norm_ssm': 1.0 + nrm(ks[15], (N_A_LAYERS, D_INNER), 0.02),
        'w_out_ssm': nrm(ks[16], (N_A_LAYERS, D_INNER, D_MODEL), D_INNER ** -0.5),
        'norm_kv': 1.0 + nrm(ks[17], (D_MODEL,), 0.02),
        'w_kv': nrm(ks[18], (D_MODEL, 2 * KV_DIM), D_MODEL ** -0.5),
        'w_q': nrm(ks[19], (N_B_LAYERS, D_MODEL, N_HEADS * HEAD_DIM), D_MODEL ** -0.5),
        'w_o': nrm(ks[20], (N_B_LAYERS, N_HEADS * HEAD_DIM, D_MODEL), (N_HEADS * HEAD_DIM) ** -0.5),
        'w_gu': nrm(ks[21], (DEPTH, D_MODEL, 2 * D_FF), D_MODEL ** -0.5),
        'w_down': nrm(ks[22], (DEPTH, D_FF, D_MODEL), D_FF ** -0.5),
        'norm_final': 1.0 + nrm(ks[23], (D_MODEL,), 0.02),
    }


def reference(x_prompt, x_sample, state_conv, state_ssm, cache_k, cache_v, page_table,
              norm_mix, norm_ffn, w_in_ssm, conv_w, conv_b, dt_bias, a_log, d_skip, norm_ssm,
              w_out_ssm, norm_kv, w_kv, w_q, w_o, w_gu, w_down, norm_final):
    p = dict(norm_mix=norm_mix, norm_ffn=norm_ffn, w_in_ssm=w_in_ssm, conv_w=conv_w, conv_b=conv_b,
             dt_bias=dt_bias, a_log=a_log, d_skip=d_skip, norm_ssm=norm_ssm, w_out_ssm=w_out_ssm,
             norm_kv=norm_kv, w_kv=w_kv, w_q=w_q, w_o=w_o, w_gu=w_gu, w_down=w_down,
             norm_final=norm_final)
    y_prompt, conv_prompt, ssm_prompt, k_prompt, v_prompt = run_group(x_prompt, None, None, None, None, p)
    n_seq = page_table.shape[0]
    past_k = cache_k[page_table].reshape(n_seq, -1, N_KV_HEADS, HEAD_DIM)
    past_v = cache_v[page_table].reshape(n_seq, -1, N_KV_HEADS, HEAD_DIM)
    y_sample, conv_sample, ssm_sample, k_sample, v_sample = run_group(
        x_sample, state_conv, state_ssm, past_k, past_v, p)
    return (y_prompt, y_sample, conv_prompt, ssm_prompt, k_prompt, v_prompt,
            conv_sample, ssm_sample, k_sample, v_sample)
```

```python
import numpy as np
from contextlib import ExitStack
import concourse.bass as bass
import concourse.mybir as mybir
from concourse.bass_utils import run_bass_kernel_spmd
import ml_dtypes

F32 = mybir.dt.float32
BF16 = mybir.dt.bfloat16
I32 = mybir.dt.int32
AF = mybir.ActivationFunctionType
ALU = mybir.AluOpType
AX = mybir.AxisListType

NDS = 6
SEM_LIMIT = 30000


class Buf:
    __slots__ = ("t", "name", "lw", "rd")

    def __init__(self, t, name):
        self.t = t
        self.name = name
        self.lw = None
        self.rd = {}

    def __getitem__(self, k):
        return self.t[k]


class Sched:
    COMPUTE = ("pe", "act", "dve", "pool")
    QUEUES = ("pe", "act", "dve", "pool", "sp")

    def __init__(self, nc, es):
        self.nc, self.es = nc, es
        self.q = {e: [] for e in self.QUEUES}
        self.known = {e: {} for e in self.QUEUES}
        self.csem, self.ccnt = {}, {}
        self.nsem = 0
        self.allsems = []
        for e in self.COMPUTE:
            self._newsem(e)
        self.dsem = {}
        for qn in ("sp", "pool", "act"):
            self.dsem[qn] = []
            for i in range(NDS):
                s = es.enter_context(nc.semaphore(f"d_{qn}_{i}"))
                self.dsem[qn].append([s, 0])
        self.drr = {qn: 0 for qn in self.dsem}
        self.n_ops = 0

    def _newsem(self, e):
        self.nsem += 1
        s = self.es.enter_context(self.nc.semaphore(f"c_{e}_{self.nsem}"))
        if e in self.csem:
            self.allsems.append((self.csem[e], self.ccnt[e]))
        self.csem[e] = s
        self.ccnt[e] = 0

    @staticmethod
    def _deps(R, W):
        deps = {}
        for b in R:
            if b.lw is not None:
                s, v = b.lw
                if deps.get(s, 0) < v:
                    deps[s] = v
        for b in W:
            if b.lw is not None:
                s, v = b.lw
                if deps.get(s, 0) < v:
                    deps[s] = v
            for s, v in b.rd.items():
                if deps.get(s, 0) < v:
                    deps[s] = v
        return deps

    def _wait(self, e, deps, skip=None):
        kn = self.known[e]
        for s, v in deps.items():
            if s is skip:
                continue
            if kn.get(s, 0) >= v:
                continue
            kn[s] = v
            self.q[e].append(("w", s, v))

    @staticmethod
    def _mark(ev, R, W):
        s, v = ev
        for b in W:
            b.lw = ev
            b.rd = {}
        for b in R:
            if any(b is w for w in W):
                continue
            if b.rd.get(s, 0) < v:
                b.rd[s] = v

    def op(self, e, fn, R=(), W=()):
        deps = self._deps(R, W)
        self._wait(e, deps, skip=self.csem[e] if e == "pe" else None)
        if self.ccnt[e] >= SEM_LIMIT:
            self._newsem(e)
        self.ccnt[e] += 1
        ev = (self.csem[e], self.ccnt[e])
        self.q[e].append(("o", fn, ev[0]))
        self._mark(ev, R, W)
        self.n_ops += 1

    def dma(self, qn, fn, R=(), W=()):
        deps = self._deps(R, W)
        self._wait(qn, deps)
        i = self.drr[qn]
        self.drr[qn] = (i + 1) % NDS
        ent = self.dsem[qn][i]
        sem, c = ent
        if c > 0:
            self._wait(qn, {sem: 16 * c})
        ent[1] = c + 1
        ev = (sem, 16 * (c + 1))
        self.q[qn].append(("d", fn, sem))
        self._mark(ev, R, W)
        self.n_ops += 1

    def barrier(self):
        deps = {}
        for e in self.COMPUTE:
            if self.ccnt[e] > 0:
                deps[self.csem[e]] = self.ccnt[e]
        for s, c in self.allsems:
            deps[s] = c
        for qn in self.dsem:
            for s, c in self.dsem[qn]:
                if c > 0:
                    deps[s] = 16 * c
        for e in self.QUEUES:
            self._wait(e, dict(deps), skip=None)

    def finish(self):
        self.barrier()
        nc = self.nc
        q = self.q

        def run(eng, items):
            for it in items:
                if it[0] == "w":
                    eng.wait_ge(it[1], it[2])
                elif it[0] == "o":
                    it[1](eng).then_inc(it[2], 1)
                else:
                    it[1](eng).then_inc(it[2], 16)

        with nc.Block() as block:
            @block.tensor
            def _(e):
                run(e, q["pe"])

            @block.scalar
            def _(e):
                run(e, q["act"])

            @block.vector
            def _(e):
                run(e, q["dve"])

            @block.gpsimd
            def _(e):
                run(e, q["pool"])

            @block.sync
            def _(e):
                run(e, q["sp"])


D = 1024
DIN = 2048
NH = 32
HP = 64
NG = 4
NST = 128
CD = 3072
DFF = 2816
NQH = 16
NKV = 4
HD = 64
EPS = 1e-6
NEG = -30000.0
TOPK = 3
DEBUG = False


class KB:
    def __init__(self, nc, es):
        self.nc, self.es = nc, es
        self.S = Sched(nc, es)
        self.psum = [Buf(es.enter_context(nc.psum_tensor(f"ps{i}", [128, 512], F32)), f"ps{i}") for i in range(8)]
        self.pi = 0
        self.NROT = 6

    def ps(self):
        b = self.psum[self.pi]
        self.pi = (self.pi + 1) % self.NROT
        return b

    def sb(self, name, shape, dt):
        return Buf(self.es.enter_context(self.nc.sbuf_tensor("s_" + name, list(shape), dt)), name)

    def mm(self, out, lhsT, rhs, st, sp, R, W):
        self.S.op("pe", lambda e: e.matmul(out, lhsT, rhs, start=st, stop=sp), R, W)

    def tr(self, out, in_, ident, R, W):
        self.S.op("pe", lambda e: e.transpose(out, in_, ident), R, W)

    def act(self, out, in_, func, R, W, bias=None, scale=None, accum=None):
        kw = {}
        if bias is not None:
            kw["bias"] = bias
        if scale is not None:
            kw["scale"] = scale
        if accum is not None:
            kw["accum_out"] = accum
        self.S.op("act", lambda e: e.activation(out, in_, func, **kw), R, W)

    def tt(self, eng, out, a, b, op, R, W):
        self.S.op(eng, lambda e: e.tensor_tensor(out, a, b, op), R, W)

    def ts(self, eng, out, a, s1, s2, op0, op1, R, W):
        if op1 is None:
            self.S.op(eng, lambda e: e.tensor_scalar(out, a, s1, None, op0), R, W)
        else:
            self.S.op(eng, lambda e: e.tensor_scalar(out, a, s1, s2, op0, op1), R, W)

    def stt(self, eng, out, a, sc, b, op0, op1, R, W):
        self.S.op(eng, lambda e: e.scalar_tensor_tensor(out, a, sc, b, op0, op1), R, W)

    def cp(self, eng, out, in_, R, W):
        if eng == "act":
            self.S.op("act", lambda e: e.activation(out, in_, AF.Copy), R, W)
        else:
            self.S.op(eng, lambda e: e.tensor_copy(out, in_), R, W)

    def memset(self, eng, out, val, W):
        self.S.op(eng, lambda e: e.memset(out, val), (), W)

    def dma(self, qn, out, in_, R=(), W=()):
        self.S.dma(qn, lambda e: e.dma_start(out=out, in_=in_), R, W)


class WStream:
    def __init__(self, kb, nbuf=3, elems=4096):
        self.kb = kb
        self.bufs = [kb.sb(f"wbuf{i}", [128, elems], BF16) for i in range(nbuf)]
        self.plan = []
        self.issued = 0
        self.taken = 0

    def add(self, ap, parts, elems, tag):
        self.plan.append((ap, parts, elems, tag))

    def _issue(self):
        ap, parts, elems, tag = self.plan[self.issued]
        b = self.bufs[self.issued % len(self.bufs)]
        self.kb.dma("pool", b[0:parts, 0:elems], ap, W=[b])
        self.issued += 1

    def next(self, tag):
        n = len(self.bufs)
        while self.issued < len(self.plan) and self.issued < self.taken + n - 1:
            self._issue()
        ap, parts, elems, t = self.plan[self.taken]
        assert t == tag, (t, tag)
        b = self.bufs[self.taken % n]
        self.taken += 1
        return b


def build(NT, NBP, with_sample=False, NPAST=8192, NPOOLROWS=2560 * 128):
    NTILES = NT // 512
    NSUB = NT // 128
    NB = NT // 256
    NSEL = min(TOPK, NB - 1)
    nc = bass.Bass("TRN2", target_bir_lowering=False)
    es = ExitStack()

    def din(name, shape, dt=F32):
        return nc.dram_tensor(name, list(shape), dt, kind="ExternalInput").ap()

    def dout(name, shape, dt=F32):
        return nc.dram_tensor(name, list(shape), dt, kind="ExternalOutput").ap()

    xp = din("xp", [NT, D])
    c_identf = din("c_identf", [128, 128])
    c_tri = din("c_tri", [128, 128])
    c_ones = din("c_ones", [128, 128])
    c_mneg = din("c_mneg", [128, 128], BF16)
    c_identb = din("c_identb", [128, 128], BF16)
    c_trimask = din("c_trimask", [128, 512], BF16)
    c_onehot = din("c_onehot", [NBP, NT], BF16)
    c_rope = din("c_rope", [NT, 64])
    c_ropeq = din("c_ropeq", [NT // 2, 64])
    NSTEP = NSUB // 2
    NT3 = NT // 2
    c_gmask = din("c_gmask", [128, NSTEP * 3 * NBP])
    c_x2idx = din("c_x2idx", [128, NSTEP], mybir.dt.uint32)
    c_masks = din("c_masks", [NSTEP, 128, 2 * 512], BF16)
    g_all = din("g_all", [6, 128, D])
    g_allT = din("g_allT", [128, 6 * 8])
    g_ssm = din("g_ssm", [128, DIN])
    cw_d = din("cw", [128, 24, 4])
    cb_d = din("cb", [128, 24])
    dtb_d = din("dtb", [128, NH])
    alog_d = din("alog", [128, NH])
    dsk_d = din("dsk", [128, NH])
    w_z = din("w_z", [4, 128, 4096])
    w_x = din("w_x", [6, 128, 4096])
    w_dt = din("w_dt", [128, 8 * 32])
    w_out = din("w_out", [4, 128, 4096])
    w_gu = din("w_gu", [2, 11, 128, 4096])
    w_dn = din("w_dn", [2, 6, 128, 4096])
    w_kv = din("w_kv", [1, 128, 4096])
    w_q = din("w_q", [2, 128, 4096])
    w_o = din("w_o", [2, 128, 4096])

    yp = dout("yp", [NT // 2, D])
    kp = dout("kp", [NT, 256])
    vp = dout("vp", [NT, 256])
    convp = dout("convp", [3, CD])
    ssmp = dout("ssmp", [DIN, NST])
    x2d_t = nc.dram_tensor("x2d", [NT, D], F32, kind="Internal").ap()
    if with_sample:
        NBPS = 8 * ((NPAST // 256 + 1 + 7) // 8)
        xs = din("xs", [16, D])
        sconv = din("sconv", [12, CD])
        sssm = din("sssm", [4, DIN, NST])
        cache_k = din("cache_k", [NPOOLROWS, 256])
        cache_v = din("cache_v", [NPOOLROWS, 256])
        ptab = din("ptab", [4, NPAST // 128], I32)
        c_rope_s = din("c_rope_s", [16, 64])
        c_ropeq_s = din("c_ropeq_s", [16, 64])
        c_i16 = din("c_i16", [128, 256])
        c_sel = din("c_sel", [16, 16 * 128])
        c_onehot_s = din("c_onehot_s", [NBPS, NPAST + 128], BF16)
        c_gmask_s = din("c_gmask_s", [128, 3 * NBPS])
        c_trimask_s = din("c_trimask_s", [128, 16], BF16)
        c_pidx = din("c_pidx", [128, 1])
        ys = dout("ys", [16, D])
        convs = dout("convs", [12, CD])
        ssms = dout("ssms", [4, DIN, NST])
        ks = dout("ks", [16, 256])
        vs = dout("vs", [16, 256])
    dbg_at = dout("dbg_at", [NTILES // 2, 128, 8, 512], BF16) if DEBUG else None

    with es:
        kb = KB(nc, es)
        S = kb.S
        sb = kb.sb
        kpBs = [Buf(kp, f"kp_dram{i}") for i in range(NSUB)]
        vpBs = [Buf(vp, f"vp_dram{i}") for i in range(NSUB)]
        x2B = Buf(x2d_t, "x2_dram")

        identf = sb("identf", [128, 128], F32)
        tri = sb("tri", [128, 128], F32)
        ones = sb("ones", [128, 128], F32)
        identb = sb("identb", [128, 128], BF16)
        gT = sb("gT", [128, 6, 8], F32)
        kb.dma("sp", identf[:], c_identf, W=[identf])
        kb.dma("sp", tri[:], c_tri, W=[tri])
        kb.dma("sp", ones[:], c_ones, W=[ones])
        mneg = sb("mneg", [128, 128], BF16)
        kb.dma("sp", mneg[:], c_mneg, W=[mneg])
        kb.dma("sp", identb[:], c_identb, W=[identb])
        kb.dma("sp", gT[:].rearrange("p a b -> p (a b)"), g_allT, W=[gT])

        ws = WStream(kb)
        for j in range(NTILES):
            for b in range(4):
                ws.add(w_z[b], 128, 4096, "z")
            for b in range(6):
                ws.add(w_x[b], 128, 4096, "x")
            ws.add(w_dt, 128, 256, "dt")
            if j == NTILES - 1:
                for b in range(6):
                    ws.add(w_x[b], 128, 4096, "xc")
            for b in range(4):
                ws.add(w_out[b], 128, 4096, "out")
            for b in range(11):
                ws.add(w_gu[0, b], 128, 4096, "gu")
            for b in range(6):
                ws.add(w_dn[0, b][:, 0:(4096 if b % 3 < 2 else 3072)], 128, 4096 if b % 3 < 2 else 3072, "dn")
            ws.add(w_kv[0], 128, 4096, "kv")
        for j in range(NTILES // 2):
            for b in range(2):
                ws.add(w_q[b], 128, 4096, "q")
            for b in range(2):
                ws.add(w_o[b], 128, 4096, "o")
            for b in range(11):
                ws.add(w_gu[1, b], 128, 4096, "gu")
            for b in range(6):
                ws.add(w_dn[1, b][:, 0:(4096 if b % 3 < 2 else 3072)], 128, 4096 if b % 3 < 2 else 3072, "dn")

        if with_sample:
            for b in range(4):
                ws.add(w_z[b], 128, 4096, "z")
            for b in range(6):
                ws.add(w_x[b], 128, 4096, "x")
            ws.add(w_dt, 128, 256, "dt")
            for b in range(4):
                ws.add(w_out[b], 128, 4096, "out")
            for b in range(11):
                ws.add(w_gu[0, b], 128, 4096, "gu")
            for b in range(6):
                ws.add(w_dn[0, b][:, 0:(4096 if b % 3 < 2 else 3072)], 128, 4096 if b % 3 < 2 else 3072, "dn")
            ws.add(w_kv[0], 128, 4096, "kv")
            for b in range(2):
                ws.add(w_q[b], 128, 4096, "q")
            for b in range(2):
                ws.add(w_o[b], 128, 4096, "o")
            for b in range(11):
                ws.add(w_gu[1, b], 128, 4096, "gu")
            for b in range(6):
                ws.add(w_dn[1, b][:, 0:(4096 if b % 3 < 2 else 3072)], 128, 4096 if b % 3 < 2 else 3072, "dn")
        esP = ExitStack()

        def sbP(name, shape, dt):
            return Buf(esP.enter_context(nc.sbuf_tensor("sP_" + name, list(shape), dt)), name)
        xt = sbP("xt", [128, 4, D], F32)
        xs_ = [Buf(xt.t, f"xt_sub{i}") for i in range(4)]
        xnb = [sbP(f"xnb{i}", [128, D], BF16) for i in range(2)]
        xnT = sbP("xnT", [128, 8, 512], BF16)
        junk = sbP("junk", [128, 512], BF16)
        ms = sbP("ms", [128, 8], F32)
        rs = sbP("rs", [128, 8], F32)
        hT = sbP("hT", [128, 22, 512], BF16)
        cst = [sbP(f"cst{i}", [128, 515], F32) for i in range(2)]
        cacc = [sbP(f"cacc{i}", [128, 512], F32) for i in range(2)]

        def norm_T(gidx, nsub=4, ntok=128):
            for s_ in range(nsub):
                jb = xnb[(s_ + 1) % len(xnb)]
                kb.act(jb[0:ntok, :], xt[0:ntok, s_, :], AF.Square, R=[xs_[s_]], W=[jb, ms],
                       scale=1.0 / 32.0, accum=ms[0:ntok, s_:s_ + 1])
            kb.act(rs[0:ntok, 0:nsub], ms[0:ntok, 0:nsub], AF.Ln, R=[ms], W=[rs], bias=EPS)
            kb.act(rs[0:ntok, 0:nsub], rs[0:ntok, 0:nsub], AF.Exp, R=[rs], W=[rs], scale=-0.5)
            for s_ in range(nsub):
                xb_ = xnb[s_ % len(xnb)]
                kb.ts("dve", xb_[0:ntok, :], xt[0:ntok, s_, :], rs[0:ntok, s_:s_ + 1], None, ALU.mult, None,
                      R=[xs_[s_], rs], W=[xb_])
                p = kb.ps()
                pbv = p.t[:].bitcast(BF16)
                for kc in range(8):
                    kb.tr(pbv[:, kc * 128:kc * 128 + ntok], xb_[0:ntok, kc * 128:(kc + 1) * 128],
                          identb[0:ntok, 0:ntok], R=[xb_, identb], W=[p])
                kb.tt("dve", xnT[:, :, s_ * 128:s_ * 128 + ntok],
                      pbv[:, 0:1024].rearrange("p (k t) -> p k t", k=8)[:, :, 0:ntok],
                      gT[:, gidx, :].unsqueeze(2).to_broadcast([128, 8, ntok]), ALU.mult, R=[p, gT], W=[xnT])

        def proj_tok(xT_, KC, wv, cols, s_, ntok=128, p=None, c0=0):
            p = p or kb.ps()
            for kc in range(KC):
                kb.mm(p.t[0:ntok, c0:c0 + cols], xT_[:, kc, s_ * 128:s_ * 128 + ntok], wv[:, kc, 0:cols],
                      kc == 0, kc == KC - 1, R=[xT_, wv_buf[0]], W=[p])
            return p

        def proj_feat(xT_, KC, wv, col0, ntot):
            p = kb.ps()
            for kc in range(KC):
                kb.mm(p.t[:, 0:ntot], wv[:, kc, col0:col0 + 128], xT_[:, kc, 0:ntot],
                      kc == 0, kc == KC - 1, R=[xT_, wv_buf[0]], W=[p])
            return p

        wv_buf = [None]

        def wnext(tag, KC, cols, parts=128):
            b = ws.next(tag)
            wv_buf[0] = b
            return b.t[0:parts, 0:KC * cols].rearrange("p (k c) -> p k c", k=KC)

        def ffn(layer, gidx, nsub=4, ntok=128):
            ntot = (nsub - 1) * 128 + ntok
            norm_T(gidx, nsub, ntok)
            for b in range(11):
                wv = wnext("gu", 8, 512)
                for t2 in range(2):
                    fft = b * 2 + t2
                    pg = proj_feat(xnT, 8, wv, t2 * 128, ntot)
                    pu = proj_feat(xnT, 8, wv, 256 + t2 * 128, ntot)
                    sg_ = cacc[fft % 2]
                    kb.act(sg_[:, 0:ntot], pg.t[:, 0:ntot], AF.Silu, R=[pg], W=[sg_])
                    kb.tt("dve", hT[:, fft, 0:ntot], sg_[:, 0:ntot], pu.t[:, 0:ntot], ALU.mult, R=[sg_, pu], W=[hT])
            for hf in range(2):
                pp = [kb.ps() for _ in range(nsub)]
                for kbk in range(3):
                    nk = 8 if kbk < 2 else 6
                    wv = wnext("dn", nk, 512)
                    for s_ in range(nsub):
                        for kc in range(nk):
                            kcg = kbk * 8 + kc
                            kb.mm(pp[s_].t[0:ntok, 0:512], hT[:, kcg, s_ * 128:s_ * 128 + ntok], wv[:, kc, :],
                                  kcg == 0, kcg == 21, R=[hT, wv_buf[0]], W=[pp[s_]])
                for s_ in range(nsub):
                    kb.tt("dve", xt[0:ntok, s_, hf * 512:(hf + 1) * 512], xt[0:ntok, s_, hf * 512:(hf + 1) * 512],
                          pp[s_].t[0:ntok, 0:512], ALU.add, R=[xs_[s_], pp[s_]], W=[xs_[s_]])


        def rope(dstB, dst, srcB, src, H, tab):
            n = src.shape[0]
            rt1, rt2 = cst[0], cst[1]
            sv = src.rearrange("p (h two d) -> p h two d", two=2, d=32)
            dv = dst.rearrange("p (h two d) -> p h two d", two=2, d=32)
            cosb = tab[0:n, 0:32].unsqueeze(1).to_broadcast([n, H, 32])
            sinb = tab[0:n, 32:64].unsqueeze(1).to_broadcast([n, H, 32])
            a = rt1[0:n, 0:H * 32].rearrange("p (h d) -> p h d", d=32)
            b_ = rt2[0:n, 0:H * 32].rearrange("p (h d) -> p h d", d=32)
            x1, x2 = sv[:, :, 0, :], sv[:, :, 1, :]
            kb.tt("dve", a, x1, cosb, ALU.mult, R=[srcB, tab], W=[rt1])
            kb.tt("dve", b_, x2, sinb, ALU.mult, R=[srcB, tab], W=[rt2])
            kb.tt("dve", dv[:, :, 0, :], a, b_, ALU.subtract, R=[rt1, rt2], W=[dstB])
            kb.tt("dve", a, x2, cosb, ALU.mult, R=[srcB, tab, dstB], W=[rt1])
            kb.tt("dve", b_, x1, sinb, ALU.mult, R=[srcB, tab, dstB], W=[rt2])
            kb.tt("dve", dv[:, :, 1, :], a, b_, ALU.add, R=[rt1, rt2], W=[dstB])

        es2 = ExitStack()
        with es2:
            def sb2(name, shape, dt):
                return Buf(es2.enter_context(nc.sbuf_tensor("s2_" + name, list(shape), dt)), name)
            gssm = sb2("gssm", [128, DIN], F32)
            cw = sb2("cw", [128, 24, 4], F32)
            cbias = sb2("cbias", [128, 24], F32)
            dtb = sb2("dtb", [128, NH], F32)
            arep = sb2("arep", [128, NH], F32)
            dsk = sb2("dsk", [128, NH], F32)
            DI = sb2("DI", [128, NH, 128], BF16)
            kb.dma("sp", gssm[:], g_ssm, W=[gssm])
            kb.dma("sp", cw[:], cw_d, W=[cw])
            kb.dma("sp", cbias[:], cb_d, W=[cbias])
            kb.dma("sp", dtb[:], dtb_d, W=[dtb])
            kb.dma("sp", arep[:], alog_d, W=[arep])
            kb.dma("sp", dsk[:], dsk_d, W=[dsk])
            kb.act(arep[:], arep[:], AF.Exp, R=[arep], W=[arep])
            kb.ts("dve", arep[:], arep[:], -1.0, None, ALU.mult, None, R=[arep], W=[arep])
            kb.tt("dve", DI[:], identf[:].unsqueeze(1).to_broadcast([128, NH, 128]),
                  dsk[:].unsqueeze(2).to_broadcast([128, NH, 128]), ALU.mult, R=[identf, dsk], W=[DI])

            halo = sb2("halo", [128, 24, 3], F32)
            kb.memset("pool", halo[:], 0.0, W=[halo])
            cob = [sb2(f"cob{i}", [128, 512], BF16) for i in range(2)]
            x_tok = sb2("x_tok", [128, 4, DIN], BF16)
            B_tok = sb2("B_tok", [128, 4, 512], BF16)
            BT = sb2("BT", [128, 4, 512], BF16)
            CT = sb2("CT", [128, 4, 512], BF16)
            dtt = sb2("dtt", [128, 4, NH], F32)
            dta = sb2("dta", [128, 4, NH], F32)
            acs = sb2("acs", [128, 64], F32)
            nacs = sb2("nacs", [128, NH], F32)
            eacs = sb2("eacs", [128, NH], F32)
            cdec = sb2("cdec", [128, NH], F32)
            dte = sb2("dte", [128, NH], F32)
            w1 = sb2("w1", [128, NH], F32)
            xd = sb2("xd", [128, NH, HP], BF16)
            xdw = sb2("xdw", [128, NH, HP], BF16)
            Gm = sb2("Gm", [128, 4, 128], BF16)
            Dc = [sb2(f"Dc{i}", [128, 512], F32) for i in range(2)]
            Eb = [sb2(f"Eb{i}", [128, 1024], BF16) for i in range(2)]
            Mb = [sb2(f"Mb{i}", [128, 8, 128], BF16) for i in range(2)]
            t1b = [sb2(f"t1b{i}", [128, 512], F32) for i in range(2)]
            y32 = [sb2(f"y32{i}", [128, 512], F32) for i in range(2)]
            yg = sb2("yg", [128, DIN], F32)
            gms = sb2("gms", [128, 4], F32)
            grs = sb2("grs", [128, 4], F32)
            yn = [sb2(f"yn{i}", [128, 1024], BF16) for i in range(1)]
            ynT = sb2("ynT", [128, 16, 512], BF16)
            h32 = sb2("h32", [128, DIN], F32)
            hbf = sb2("hbf", [128, DIN], BF16)
            kv32 = [t1b[0], y32[0]]
            kro = cacc
            rtab = [sb2(f"rtab{i}", [128, 64], F32) for i in range(4)]
            kb.memset("pool", h32[:], 0.0, W=[h32])
            kb.memset("pool", hbf[:], 0.0, W=[hbf])
            zs = hT.t[:].rearrange("p k t -> p (k t)")[:, 0:8192].rearrange("p (s c) -> p s c", s=4)
            print("phase2 sbuf remaining", nc.sbuf_bytes_remaining)

            for j in range(NTILES):
                for s4 in range(4):
                    kb.dma("sp", xt[:, s4, :], xp[j * 512 + s4 * 128:j * 512 + (s4 + 1) * 128, :], W=[xs_[s4]])
                norm_T(0)
                for b in range(4):
                    wv = wnext("z", 8, 512)
                    for s_ in range(4):
                        p = proj_tok(xnT, 8, wv, 512, s_)
                        kb.act(zs[:, s_, b * 512:(b + 1) * 512], p.t[:, 0:512], AF.Silu, R=[p], W=[hT])
                conv_wv = [None]

                def conv_front(ci):
                    if ci % 4 == 0:
                        conv_wv[0] = wnext("x", 8, 512)
                    p = proj_feat(xnT, 8, conv_wv[0], (ci % 4) * 128, 512)
                    st_ = cst[ci % 2]
                    ca = cacc[ci % 2]
                    kb.cp("act", st_[:, 0:3], halo[:, ci, :], R=[halo], W=[st_])
                    kb.cp("act", st_[:, 3:515], p.t[:, 0:512], R=[p], W=[st_])
                    kb.cp("act", halo[:, ci, :], st_[:, 512:515], R=[st_], W=[halo])
                    kb.act(ca[:], p.t[:, 0:512], AF.Identity, R=[p, cw, cbias], W=[ca],
                           scale=cw[:, ci, 3:4], bias=cbias[:, ci:ci + 1])
                    for tap in (2, 1, 0):
                        kb.stt("dve", ca[:], st_[:, tap:tap + 512], cw[:, ci, tap:tap + 1], ca[:],
                               ALU.mult, ALU.add, R=[st_, cw, ca], W=[ca])

                def conv_back(ci):
                    ca = cacc[ci % 2]
                    if ci < 16:
                        co = cob[ci % 2]
                        kb.act(co[:], ca[:], AF.Silu, R=[ca], W=[co])
                        pt = kb.ps()
                        pbv = pt.t[:].bitcast(BF16)
                        for s_ in range(4):
                            kb.tr(pbv[:, s_ * 128:(s_ + 1) * 128], co[:, s_ * 128:(s_ + 1) * 128], identb[:],
                                  R=[co, identb], W=[pt])
                        kb.cp("dve", x_tok[:, :, ci * 128:(ci + 1) * 128],
                              pbv[:, 0:512].rearrange("p (s c) -> p s c", s=4), R=[pt], W=[x_tok])
                    elif ci < 20:
                        g = ci - 16
                        kb.act(BT[:, g, :], ca[:], AF.Silu, R=[ca], W=[BT])
                        pt = kb.ps()
                        pbv = pt.t[:].bitcast(BF16)
                        for s_ in range(4):
                            kb.tr(pbv[:, s_ * 128:(s_ + 1) * 128], BT[:, g, s_ * 128:(s_ + 1) * 128], identb[:],
                                  R=[BT, identb], W=[pt])
                        kb.cp("dve", B_tok[:, :, g * 128:(g + 1) * 128],
                              pbv[:, 0:512].rearrange("p (s c) -> p s c", s=4), R=[pt], W=[B_tok])
                    else:
                        g = ci - 20
                        kb.act(CT[:, g, :], ca[:], AF.Silu, R=[ca], W=[CT])
                conv_front(0)
                for ci in range(24):
                    if ci + 1 < 24:
                        conv_front(ci + 1)
                    conv_back(ci)
                wv = wnext("dt", 8, 32)
                p = kb.ps()
                for s_ in range(4):
                    proj_tok(xnT, 8, wv, 32, s_, p=p, c0=s_ * 32)
                kb.tt("dve", dtt[:], p.t[:, 0:128].rearrange("p (s h) -> p s h", s=4),
                      dtb[:].unsqueeze(1).to_broadcast([128, 4, NH]), ALU.add, R=[p, dtb], W=[dtt])
                kb.act(dtt[:], dtt[:], AF.Exp, R=[dtt], W=[dtt])
                kb.act(dtt[:], dtt[:], AF.Ln, R=[dtt], W=[dtt], bias=1.0)
                kb.tt("dve", dta[:], dtt[:], arep[:].unsqueeze(1).to_broadcast([128, 4, NH]), ALU.mult,
                      R=[dtt, arep], W=[dta])
                if j == NTILES - 1:
                    for b in range(6):
                        wv = wnext("xc", 8, 512)
                        p = kb.ps()
                        for kc in range(8):
                            kb.mm(p.t[0:3, 0:512], xnT[:, kc, 509:512], wv[:, kc, 0:512], kc == 0, kc == 7,
                                  R=[xnT, wv_buf[0]], W=[p])
                        c3 = cacc[b % 2]
                        kb.cp("act", c3[0:3, :], p.t[0:3, 0:512], R=[p], W=[c3])
                        kb.dma("sp", convp[:, b * 512:(b + 1) * 512], c3[0:3, :], R=[c3])
                for s_ in range(4):
                    gi = j * 4 + s_
                    tsl = slice(s_ * 128, (s_ + 1) * 128)
                    p1 = kb.ps()
                    kb.mm(p1.t[:, 0:32], tri[:], dta[:, s_, :], True, True, R=[tri, dta], W=[p1])
                    kb.mm(p1.t[:, 32:64], ones[:], dta[:, s_, :], True, True, R=[ones, dta], W=[p1])
                    kb.cp("dve", acs[:], p1.t[:, 0:64], R=[p1], W=[acs])
                    kb.ts("dve", nacs[:], acs[:, 0:32], -1.0, None, ALU.mult, None, R=[acs], W=[nacs])
                    kb.act(eacs[:], acs[:, 0:32], AF.Exp, R=[acs], W=[eacs])
                    kb.act(cdec[:], acs[:, 32:64], AF.Exp, R=[acs], W=[cdec])
                    kb.tt("dve", dte[:], acs[:, 32:64], acs[:, 0:32], ALU.subtract, R=[acs], W=[dte])
                    kb.act(dte[:], dte[:], AF.Exp, R=[dte], W=[dte])
                    kb.tt("dve", w1[:], dte[:], dtt[:, s_, :], ALU.mult, R=[dte, dtt], W=[w1])
                    xv = x_tok[:, s_, :].rearrange("p (h d) -> p h d", d=HP)
                    kb.tt("dve", xd[:], xv, dtt[:, s_, :].unsqueeze(2).to_broadcast([128, NH, HP]), ALU.mult,
                          R=[x_tok, dtt], W=[xd])
                    kb.tt("pool", xdw[:], xv, w1[:].unsqueeze(2).to_broadcast([128, NH, HP]), ALU.mult,
                          R=[x_tok, w1], W=[xdw])
                    pG = kb.ps()
                    for g in range(4):
                        kb.mm(pG.t[:, g * 128:(g + 1) * 128], BT[:, g, tsl], CT[:, g, tsl], True, True,
                              R=[BT, CT], W=[pG])
                    kb.tt("dve", Gm[:], pG.t[:, 0:512].rearrange("p (g l) -> p g l", g=4),
                          tri[:].unsqueeze(1).to_broadcast([128, 4, 128]), ALU.mult, R=[pG, tri], W=[Gm])
                    def ssd_front(g):
                        ebb, mbb = Eb[g % 2], Mb[g % 2]
                        for half in range(2):
                            dcb = Dc[half]
                            pS = kb.ps()
                            for hh in range(4):
                                h = g * 8 + half * 4 + hh
                                kb.mm(pS.t[:, hh * 128:(hh + 1) * 128], dta[:, s_, h:h + 1].to_broadcast([128, 128]),
                                      tri[:], True, False, R=[dta, tri], W=[pS])
                                kb.mm(pS.t[:, hh * 128:(hh + 1) * 128], identb[:], mneg[:], False, True,
                                      R=[identb, mneg], W=[pS])
                            h0 = g * 8 + half * 4
                            kb.tt("dve", dcb[:].rearrange("p (h l) -> p h l", h=4),
                                  pS.t[:, 0:512].rearrange("p (h l) -> p h l", h=4),
                                  nacs[:, h0:h0 + 4].unsqueeze(2).to_broadcast([128, 4, 128]), ALU.add,
                                  R=[pS, nacs], W=[dcb])
                            kb.act(ebb[:, half * 512:(half + 1) * 512], dcb[:], AF.Exp, R=[dcb], W=[ebb])
                        kb.tt("dve", mbb[:], ebb[:].rearrange("p (h l) -> p h l", h=8),
                              Gm[:, g, :].unsqueeze(1).to_broadcast([128, 8, 128]), ALU.mult, R=[ebb, Gm], W=[mbb])

                    def ssd_back(g):
                        mbb = Mb[g % 2]
                        pO = kb.ps()
                        kb.mm(pO.t[:, 0:512], CT[:, g, tsl], hbf[:, g * 512:(g + 1) * 512], True, True,
                              R=[CT, hbf], W=[pO])
                        pY = kb.ps()
                        for hh in range(8):
                            h = g * 8 + hh
                            kb.mm(pY.t[:, hh * 64:(hh + 1) * 64], mbb[:, hh, :], xd[:, h, :], True, False,
                                  R=[mbb, xd], W=[pY])
                            kb.mm(pY.t[:, hh * 64:(hh + 1) * 64], DI[:, h, :], x_tok[:, s_, h * 64:(h + 1) * 64],
                                  False, True, R=[DI, x_tok], W=[pY])
                        t1 = t1b[g % 2]
                        yy = y32[g % 2]
                        kb.tt("dve", t1[:].rearrange("p (h d) -> p h d", d=HP),
                              pO.t[:, 0:512].rearrange("p (h d) -> p h d", d=HP),
                              eacs[:, g * 8:(g + 1) * 8].unsqueeze(2).to_broadcast([128, 8, HP]), ALU.mult,
                              R=[pO, eacs], W=[t1])
                        kb.tt("dve", yy[:], pY.t[:, 0:512], t1[:], ALU.add, R=[pY, t1], W=[yy])
                        kb.tt("dve", yg[:, g * 512:(g + 1) * 512], yy[:], zs[:, s_, g * 512:(g + 1) * 512], ALU.mult,
                              R=[yy, hT], W=[yg])
                        kb.act(junk[:, 0:512], yg[:, g * 512:(g + 1) * 512], AF.Square, R=[yg], W=[junk, gms],
                               scale=float(512 ** -0.5), accum=gms[:, g:g + 1])
                    ssd_front(0)
                    for g in range(4):
                        if g + 1 < 4:
                            ssd_front(g + 1)
                        ssd_back(g)
                    kb.act(grs[:], gms[:], AF.Ln, R=[gms], W=[grs], bias=EPS)
                    kb.act(grs[:], grs[:], AF.Exp, R=[grs], W=[grs], scale=-0.5)
                    ynb = yn[0]
                    for half in range(2):
                        for g2 in range(2):
                            g = half * 2 + g2
                            kb.stt("dve", ynb[:, g2 * 512:(g2 + 1) * 512], yg[:, g * 512:(g + 1) * 512], grs[:, g:g + 1],
                                   gssm[:, g * 512:(g + 1) * 512], ALU.mult, ALU.mult, R=[yg, grs, gssm], W=[ynb])
                        pt = kb.ps()
                        pbv = pt.t[:].bitcast(BF16)
                        for k8 in range(8):
                            kb.tr(pbv[:, k8 * 128:(k8 + 1) * 128], ynb[:, k8 * 128:(k8 + 1) * 128], identb[:],
                                  R=[ynb, identb], W=[pt])
                        kb.cp("act", ynT[:, half * 8:(half + 1) * 8, tsl],
                              pbv[:, 0:1024].rearrange("p (k t) -> p k t", k=8), R=[pt], W=[ynT])
                    for g in range(4):
                        pT_ = kb.ps()
                        kb.mm(pT_.t[:, 0:512], B_tok[:, s_, g * 128:(g + 1) * 128],
                              xdw[:, g * 8:(g + 1) * 8, :], True, True, R=[B_tok, xdw], W=[pT_])
                        hv = h32[:, g * 512:(g + 1) * 512].rearrange("p (h d) -> p h d", d=HP)
                        if g == 0:
                            hall = h32[:].rearrange("p (h d) -> p h d", d=HP)
                            kb.tt("dve", hall, hall, cdec[:].unsqueeze(2).to_broadcast([128, NH, HP]),
                                  ALU.mult, R=[h32, cdec], W=[h32])
                        kb.tt("dve", hv, hv, pT_.t[:, 0:512].rearrange("p (h d) -> p h d", d=HP), ALU.add,
                              R=[h32, pT_], W=[h32])
                    kb.cp("act", hbf[:], h32[:], R=[h32], W=[hbf])
                if j == NTILES - 1:
                    for q4 in range(4):
                        pt = kb.ps()
                        for k4 in range(4):
                            k = q4 * 4 + k4
                            kb.tr(pt.t[:, k4 * 128:(k4 + 1) * 128], h32[:, k * 128:(k + 1) * 128], identf[:],
                                  R=[h32, identf], W=[pt])
                        kb.cp("act", yg[:, q4 * 512:(q4 + 1) * 512], pt.t[:, 0:512], R=[pt], W=[yg])
                    kb.dma("sp", ssmp.rearrange("(k p) n -> p k n", p=128),
                           yg[:].rearrange("p (k n) -> p k n", k=16), R=[yg])
                for hf in range(2):
                    pp = [kb.ps() for _ in range(4)]
                    for kbk in range(2):
                        wv = wnext("out", 8, 512)
                        for s_ in range(4):
                            for kc in range(8):
                                kcg = kbk * 8 + kc
                                kb.mm(pp[s_].t[:, 0:512], ynT[:, kcg, s_ * 128:(s_ + 1) * 128], wv[:, kc, :],
                                      kcg == 0, kcg == 15, R=[ynT, wv_buf[0]], W=[pp[s_]])
                    for s_ in range(4):
                        kb.tt("dve", xt[:, s_, hf * 512:(hf + 1) * 512], xt[:, s_, hf * 512:(hf + 1) * 512],
                              pp[s_].t[:, 0:512], ALU.add, R=[xs_[s_], pp[s_]], W=[xs_[s_]])
                ffn(0, 1)
                kb.dma("sp", x2d_t[j * 512:(j + 1) * 512, :].rearrange("(s p) d -> p s d", p=128), xt[:],
                       R=xs_, W=[x2B])
                norm_T(2)
                wv = wnext("kv", 8, 512)
                for s_ in range(4):
                    kb.dma("sp", rtab[s_][:], c_rope[(j * 4 + s_) * 128:(j * 4 + s_ + 1) * 128, :], W=[rtab[s_]])
                for s_ in range(4):
                    gi = j * 4 + s_
                    p = proj_tok(xnT, 8, wv, 512, s_)
                    kvb, krb, tb = kv32[s_ % 2], kro[s_ % 2], rtab[s_]
                    kb.cp("act", kvb[:], p.t[:, 0:512], R=[p], W=[kvb])
                    rope(krb, krb[:, 0:256], kvb, kvb[:, 0:256], 4, tb)
                    kb.dma("sp", kp[gi * 128:(gi + 1) * 128, :], krb[:, 0:256], R=[krb], W=[kpBs[gi]])
                    kb.dma("sp", vp[gi * 128:(gi + 1) * 128, :], kvb[:, 256:512], R=[kvb], W=[vpBs[gi]])
            S.barrier()
        S.barrier()

        KR = 64 + NBP
        es3 = ExitStack()
        with es3:
            def sb3(name, shape, dt):
                return Buf(es3.enter_context(nc.sbuf_tensor("s3_" + name, list(shape), dt)), name)
            gfin = sb3("gfin", [128, D], F32)
            kb.dma("sp", gfin[:], g_all[5], W=[gfin])
            KT = sb3("KT", [128, 4, NT], BF16)
            Vaug = sb3("Vaug", [128, NSUB, 4, 128], BF16)
            rd = sb3("rd", [64, 512], F32)
            kmT = sb3("kmT", [64, 4, NBP], F32)
            gmask = sb3("gmask", [128, NSTEP * 3 * NBP], F32)
            x2idx = sb3("x2idx", [128, NSTEP], mybir.dt.uint32)
            mk = [sb3(f"mk{i}", [128, 2 * 512], BF16) for i in range(2)]
            trimask = sb3("trimask", [128, 512], BF16)
            kin2 = [sb3(f"kin2_{i}", [128, 2, 512], F32) for i in range(2)]
            kbb = [sb3(f"kbb{i}", [128, 256], BF16) for i in range(2)]
            qr = sb3("qr", [128, D], F32)
            Qaug = sb3("Qaug", [128, NQH, KR], BF16)
            QT32 = sb3("QT32", [64, 4, 128], F32)
            QTaug = sb3("QTaug", [128, NQH, 128], BF16)
            gm = sb3("gm", [128, NQH, NBP], F32)
            m8 = sb3("m8", [128, NQH, 8], F32)
            sel = sb3("sel", [128, NQH, NBP], F32)
            PT = [sb3(f"PT{i}", [128, 512], BF16) for i in range(3)]
            AT = sb3("AT", [128, 8, 512], BF16)
            rtq = [sb3(f"rtq{i}", [128, 64], F32) for i in range(2)]
            print("phase3 sbuf remaining", nc.sbuf_bytes_remaining)
            kb.dma("sp", gmask[:], c_gmask, W=[gmask])
            kb.dma("sp", trimask[:], c_trimask, W=[trimask])
            kb.dma("sp", x2idx[:], c_x2idx, W=[x2idx])
            kb.memset("pool", kmT[:], 0.0, W=[kmT])
            kb.memset("pool", KT[:], 0.0, W=[KT])
            kb.memset("pool", QTaug[:], 0.0, W=[QTaug])
            kb.memset("pool", Vaug[:], 1.0, W=[Vaug])
            kb.NROT = 4
            for kv in range(4):
                kb.dma("sp", KT[64:64 + NBP, kv, :], c_onehot, W=[KT])
            pi_ = 0
            for n in range(NB):
                kin = kin2[n % 2]
                kb.dma("sp", kin[:, :, 0:256], kp[n * 256:(n + 1) * 256, :].rearrange("(s p) d -> p s d", p=128),
                       R=[kpBs[2 * n], kpBs[2 * n + 1]], W=[kin])
                kb.dma("sp", kin[:, :, 256:512], vp[n * 256:(n + 1) * 256, :].rearrange("(s p) d -> p s d", p=128),
                       R=[vpBs[2 * n], vpBs[2 * n + 1]], W=[kin])
                pm = kb.ps()
                for h in range(4):
                    for s2 in range(2):
                        kb.mm(pm.t[0:64, h:h + 1], kin[:, s2, h * 64:(h + 1) * 64], ones[:, 0:1], s2 == 0, s2 == 1,
                              R=[kin, ones], W=[pm])
                kb.act(kmT[:, :, n], pm.t[0:64, 0:4], AF.Copy, R=[pm], W=[kmT], scale=1.0 / 256.0)
                for s2 in range(2):
                    kt = n * 2 + s2
                    kb_ = kbb[kt % 2]
                    kb.cp("dve", kb_[:], kin[:, s2, 0:256], R=[kin], W=[kb_])
                    pt = kb.ps()
                    pbv = pt.t[:].bitcast(BF16)
                    for h in range(4):
                        kb.tr(pbv[0:64, h * 128:(h + 1) * 128], kb_[:, h * 64:(h + 1) * 64], identb[:],
                              R=[kb_, identb], W=[pt])
                    kb.cp("act", KT[0:64, :, kt * 128:(kt + 1) * 128],
                          pbv[0:64, 0:512].rearrange("p (h t) -> p h t", h=4), R=[pt], W=[KT])
                    kb.cp("dve", Vaug[:, kt, :, 0:64], kin[:, s2, 256:512].rearrange("p (h d) -> p h d", h=4),
                          R=[kin], W=[Vaug])

            for j in range(NTILES // 2):
                for s_ in range(4):
                    S.dma("pool", (lambda o_, c_: (lambda e: e.indirect_dma_start(
                        out=o_, out_offset=None, in_=x2d_t,
                        in_offset=bass.IndirectOffsetOnAxis(ap=x2idx[:, c_:c_ + 1], axis=0))))(xt[:, s_, :], j * 4 + s_),
                        R=[x2B, x2idx], W=[xs_[s_]])
                norm_T(3)
                wq = []
                for b in range(2):
                    wv = wnext("q", 8, 512)
                    wq.append((wv, wv_buf[0]))
                def q_front(s_):
                    gi_ = j * 4 + s_
                    tb = rtq[s_ % 2]
                    kb.dma("sp", tb[:], c_ropeq[gi_ * 128:(gi_ + 1) * 128, :], W=[tb])
                    qraw = kin2[s_ % 2]
                    for b in range(2):
                        wv_buf[0] = wq[b][1]
                        p = proj_tok(xnT, 8, wq[b][0], 512, s_)
                        kb.cp("act", qraw[:, b, :], p.t[:, 0:512], R=[p], W=[qraw])
                    return qraw, tb
                qf = q_front(0)
                for s_ in range(4):
                    gi = j * 4 + s_
                    tsl = slice(s_ * 128, (s_ + 1) * 128)
                    qraw, tb = qf
                    rope(qr, qr[:], qraw, qraw[:].rearrange("p b c -> p (b c)"), NQH, tb)
                    kb.cp("act", Qaug[:, :, 0:64], qr[:].rearrange("p (h d) -> p h d", d=64), R=[qr], W=[Qaug])
                    pg = kb.psum[4]
                    for q4 in range(4):
                        pt = kb.ps()
                        for k4 in range(4):
                            h = q4 * 4 + k4
                            kb.tr(pt.t[0:64, k4 * 128:(k4 + 1) * 128], qr[:, h * 64:(h + 1) * 64], identf[:],
                                  R=[qr, identf], W=[pt])
                        kb.cp("dve", QT32[:], pt.t[0:64, 0:512].rearrange("p (h t) -> p h t", h=4), R=[pt], W=[QT32])
                        for k4 in range(4):
                            h = q4 * 4 + k4
                            kb.mm(pg.t[:, h * NBP:(h + 1) * NBP], QT32[:, k4, :], kmT[:, h // 4, :], True, True,
                                  R=[QT32, kmT], W=[pg])
                    mo = gi * 3 * NBP
                    kb.tt("dve", gm[:], pg.t[:, 0:NQH * NBP].rearrange("p (h n) -> p h n", h=NQH),
                          gmask[:, mo:mo + NBP].unsqueeze(1).to_broadcast([128, NQH, NBP]), ALU.add,
                          R=[pg, gmask], W=[gm])
                    for h in range(NQH):
                        S.op("dve", (lambda h_: (lambda e: e.max(m8[:, h_, :], gm[:, h_, :])))(h), R=[gm], W=[m8])
                    kb.tt("dve", sel[:], gm[:], m8[:, :, NSEL - 1:NSEL].to_broadcast([128, NQH, NBP]), ALU.is_ge,
                          R=[gm, m8], W=[sel])
                    kb.tt("dve", sel[:], sel[:], gmask[:, mo + NBP:mo + 2 * NBP].unsqueeze(1).to_broadcast([128, NQH, NBP]),
                          ALU.mult, R=[sel, gmask], W=[sel])
                    kb.tt("dve", sel[:], sel[:], gmask[:, mo + 2 * NBP:mo + 3 * NBP].unsqueeze(1).to_broadcast([128, NQH, NBP]),
                          ALU.add, R=[sel, gmask], W=[sel])
                    kb.ts("dve", Qaug[:, :, 64:KR], sel[:], -1.0, -NEG, ALU.add, ALU.mult, R=[sel], W=[Qaug])
                    for kv in range(4):
                        pt = kb.ps()
                        pbv = pt.t[:].bitcast(BF16)
                        for g in range(4):
                            h = kv * 4 + g
                            kb.tr(pbv[0:KR, g * 128:(g + 1) * 128], Qaug[:, h, :], identb[:], R=[Qaug, identb], W=[pt])
                        kb.cp("act", QTaug[0:KR, kv * 4:(kv + 1) * 4, :],
                              pbv[0:KR, 0:512].rearrange("p (g t) -> p g t", g=4), R=[pt], W=[QTaug])
                    if s_ + 1 < 4:
                        qf = q_front(s_ + 1)
                    nkt = 2 * gi + 2
                    mkb = mk[gi % 2]
                    if gi == 0:
                        kb.dma("sp", mkb[:], c_masks[0], W=[mkb])
                    if gi + 1 < NSTEP:
                        kb.dma("sp", mk[(gi + 1) % 2][:], c_masks[gi + 1], W=[mk[(gi + 1) % 2]])
                    for kv in range(4):
                        pO = kb.psum[6 + (kv % 2)]
                        rq = QTaug[:, kv * 4:(kv + 1) * 4, :]

                        def s_mm(kt):
                            pS_ = kb.ps()
                            msk = kt >= 2 * gi
                            kb.mm(pS_.t[:, 0:512], KT[:, kv, kt * 128:(kt + 1) * 128], rq, True, not msk,
                                  R=[KT, QTaug], W=[pS_])
                            if msk:
                                m_ = kt - 2 * gi
                                kb.mm(pS_.t[:, 0:512], identb[:], mkb[:, m_ * 512:(m_ + 1) * 512], False, True,
                                      R=[identb, mkb], W=[pS_])
                            return pS_
                        pS_cur = s_mm(0)
                        for kt in range(nkt):
                            pS_next = s_mm(kt + 1) if kt + 1 < nkt else None
                            ptb = PT[pi_ % 3]
                            pi_ += 1
                            kb.act(ptb[:], pS_cur.t[:, 0:512], AF.Exp, R=[pS_cur], W=[ptb])
                            kb.mm(pO.t[:, 0:512], Vaug[:, kt, kv, :], ptb[:], kt == 0, kt == nkt - 1,
                                  R=[Vaug, ptb], W=[pO])
                            pS_cur = pS_next
                        S.op("dve", lambda e, pO=pO: e.reciprocal(rd[:], pO.t[64:128, 0:512]), R=[pO], W=[rd])
                        pov = pO.t[0:64, 0:512].rearrange("p (gp two t) -> p gp two t", two=2, t=128)
                        rdv = rd[:].rearrange("p (gp two t) -> p gp two t", two=2, t=128)
                        for two in range(2):
                            kb.tt("dve", AT[two * 64:(two + 1) * 64, kv * 2:kv * 2 + 2, tsl], pov[:, :, two, :],
                                  rdv[:, :, two, :], ALU.mult, R=[pO, rd], W=[AT])
                if DEBUG:
                    kb.dma("sp", dbg_at[j], AT[:], R=[AT])
                for b in range(2):
                    wv = wnext("o", 8, 512)
                    for s_ in range(4):
                        p = proj_tok(AT, 8, wv, 512, s_)
                        kb.tt("dve", xt[:, s_, b * 512:(b + 1) * 512], xt[:, s_, b * 512:(b + 1) * 512],
                              p.t[:, 0:512], ALU.add, R=[xs_[s_], p], W=[xs_[s_]])
                ffn(1, 4)
                for s_ in range(4):
                    kb.act(xnb[s_ % 2][:], xt[:, s_, :], AF.Square, R=[xs_[s_]], W=[xnb[s_ % 2], ms], scale=1.0 / 32.0,
                           accum=ms[:, s_:s_ + 1])
                kb.act(rs[:, 0:4], ms[:, 0:4], AF.Ln, R=[ms], W=[rs], bias=EPS)
                kb.act(rs[:, 0:4], rs[:, 0:4], AF.Exp, R=[rs], W=[rs], scale=-0.5)
                for s_ in range(4):
                    gi = j * 4 + s_
                    yb = kin2[s_ % 2]
                    ybv = yb[:].rearrange("p b c -> p (b c)")
                    kb.stt("dve", ybv, xt[:, s_, :], rs[:, s_:s_ + 1], gfin[:], ALU.mult, ALU.mult,
                           R=[xs_[s_], rs, gfin], W=[yb])
                    kb.dma("sp", yp[gi * 128:(gi + 1) * 128, :], ybv, R=[yb])
            S.barrier()
        esP.close()
        if with_sample:
            esS = ExitStack()

            def sbS(name, shape, dt):
                return Buf(esS.enter_context(nc.sbuf_tensor("sS_" + name, list(shape), dt)), name)
            xt = sbS("xt", [16, 1, D], F32)
            xs_ = [xt]
            xnb = [sbS("xnb0", [16, D], BF16)]
            xnT = sbS("xnT", [128, 8, 16], BF16)
            junk = sbS("junk", [16, D], BF16)
            ms = sbS("ms", [16, 8], F32)
            rs = sbS("rs", [16, 8], F32)
            hT = sbS("hT", [128, 22, 16], BF16)
            cst = [sbS(f"cst{i}", [16, 515], F32) for i in range(2)]
            cacc = [sbS(f"cacc{i}", [128, 16], F32) for i in range(2)]
            NPG = NPAST // 128
            NPB = NPAST // 256
            KRS = 64 + NBPS
            NKS = NPAST + 128
            U32 = mybir.dt.uint32
            kb.NROT = 4
            ksB = Buf(ks, "ks_dram")
            vsB = Buf(vs, "vs_dram")
            esA = ExitStack()
            with esA:
                def sbA(name, shape, dt):
                    return Buf(esA.enter_context(nc.sbuf_tensor("sA_" + name, list(shape), dt)), name)
                gssm = sbA("gssm", [16, DIN], F32)
                cw = sbA("cw", [128, 24, 4], F32)
                cbias = sbA("cbias", [128, 24], F32)
                dtb = sbA("dtb", [16, NH], F32)
                arep = sbA("arep", [16, NH], F32)
                dsk = sbA("dsk", [16, NH], F32)
                i16 = sbA("i16", [128, 256], F32)
                selm = sbA("selm", [16, 16 * 128], F32)
                kb.dma("sp", gssm[:], g_ssm[0:16, :], W=[gssm])
                kb.dma("sp", cw[:], cw_d, W=[cw])
                kb.dma("sp", cbias[:], cb_d, W=[cbias])
                kb.dma("sp", dtb[:], dtb_d[0:16, :], W=[dtb])
                kb.dma("sp", arep[:], alog_d[0:16, :], W=[arep])
                kb.dma("sp", dsk[:], dsk_d[0:16, :], W=[dsk])
                kb.dma("sp", i16[:], c_i16, W=[i16])
                kb.dma("sp", selm[:], c_sel, W=[selm])
                kb.act(arep[:], arep[:], AF.Exp, R=[arep], W=[arep])
                kb.ts("dve", arep[:], arep[:], -1.0, None, ALU.mult, None, R=[arep], W=[arep])
                zs_s = sbA("zs_s", [16, DIN], F32)
                sconv_sb = sbA("sconv_sb", [12, CD], F32)
                c16 = [sbA(f"c16_{i}", [16, 512], F32) for i in range(2)]
                stg = [sbA(f"stg{i}", [128, 4, 7], F32) for i in range(2)]
                ca4 = [sbA(f"ca4_{i}", [128, 4, 4], F32) for i in range(2)]
                xf = [sbA(f"xf{i}", [128, 16], F32) for i in range(2)]
                x_tok_s = sbA("x_tok_s", [16, DIN], F32)
                xdt_tok = sbA("xdt_tok", [16, DIN], F32)
                BT_s = sbA("BT_s", [128, 4, 16], F32)
                CT_s = sbA("CT_s", [128, 4, 16], F32)
                CTm = sbA("CTm", [128, 4, 16, 16], F32)
                dtt_s = sbA("dtt_s", [16, NH], F32)
                dA = sbA("dA", [16, NH], F32)
                decb = sbA("decb", [128, NH], F32)
                hs_nat = sbA("hs_nat", [128, 16, 128], F32)
                h32_s = sbA("h32_s", [128, DIN], F32)
                y_s = sbA("y_s", [16, DIN], F32)
                yn_s = sbA("yn_s", [16, DIN], BF16)
                gms_s = sbA("gms_s", [16, 4], F32)
                kvs = sbA("kvs", [16, 512], F32)
                krs = sbA("krs", [16, 256], F32)
                rts = sbA("rts", [16, 64], F32)
                print("sampleA sbuf remaining", nc.sbuf_bytes_remaining)
                kb.dma("sp", xt[0:16, 0, :], xs, W=[xs_[0]])
                kb.dma("sp", sconv_sb[:], sconv, W=[sconv_sb])
                norm_T(0, 1, 16)
                for b in range(4):
                    wv = wnext("z", 8, 512)
                    p = proj_tok(xnT, 8, wv, 512, 0, 16)
                    kb.act(zs_s[:, b * 512:(b + 1) * 512], p.t[0:16, 0:512], AF.Silu, R=[p], W=[zs_s])
                for b in range(6):
                    wv = wnext("x", 8, 512)
                    p = proj_tok(xnT, 8, wv, 512, 0, 16)
                    cc = c16[b % 2]
                    kb.cp("act", cc[:], p.t[0:16, 0:512], R=[p], W=[cc])
                    for sq in range(4):
                        kb.dma("sp", convs[sq * 3:(sq + 1) * 3, b * 512:(b + 1) * 512], cc[sq * 4 + 1:sq * 4 + 4, :], R=[cc])
                    for ct in range(4):
                        ci = b * 4 + ct
                        p = proj_feat(xnT, 8, wv, ct * 128, 16)
                        pst = kb.ps()
                        kb.tr(pst.t[:, 0:12], sconv_sb[0:12, ci * 128:(ci + 1) * 128], identf[0:12, 0:12],
                              R=[sconv_sb, identf], W=[pst])
                        st_ = stg[ci % 2]
                        ca = ca4[ci % 2]
                        kb.cp("dve", st_[:, :, 0:3], pst.t[:, 0:12].rearrange("p (s r) -> p s r", s=4), R=[pst], W=[st_])
                        kb.cp("act", st_[:, :, 3:7], p.t[:, 0:16].rearrange("p (s r) -> p s r", s=4), R=[p], W=[st_])
                        kb.ts("dve", ca[:], st_[:, :, 3:7], cw[:, ci, 3:4], cbias[:, ci:ci + 1], ALU.mult, ALU.add,
                              R=[st_, cw, cbias], W=[ca])
                        for tap in (2, 1, 0):
                            kb.stt("dve", ca[:], st_[:, :, tap:tap + 4], cw[:, ci, tap:tap + 1], ca[:],
                                   ALU.mult, ALU.add, R=[st_, cw, ca], W=[ca])
                        cav = ca[:].rearrange("p s r -> p (s r)")
                        if ci < 16:
                            xf_ = xf[ci % 2]
                            kb.act(xf_[:], cav, AF.Silu, R=[ca], W=[xf_])
                            pt = kb.ps()
                            kb.tr(pt.t[0:16, 0:128], xf_[:], identf[:], R=[xf_, identf], W=[pt])
                            kb.cp("dve", x_tok_s[:, ci * 128:(ci + 1) * 128], pt.t[0:16, 0:128], R=[pt], W=[x_tok_s])
                        elif ci < 20:
                            kb.act(BT_s[:, ci - 16, :], cav, AF.Silu, R=[ca], W=[BT_s])
                        else:
                            kb.act(CT_s[:, ci - 20, :], cav, AF.Silu, R=[ca], W=[CT_s])
                wv = wnext("dt", 8, 32)
                p = proj_tok(xnT, 8, wv, 32, 0, 16)
                kb.tt("dve", dtt_s[:], p.t[0:16, 0:32], dtb[:], ALU.add, R=[p, dtb], W=[dtt_s])
                kb.act(dtt_s[:], dtt_s[:], AF.Exp, R=[dtt_s], W=[dtt_s])
                kb.act(dtt_s[:], dtt_s[:], AF.Ln, R=[dtt_s], W=[dtt_s], bias=1.0)
                kb.tt("dve", dA[:], dtt_s[:], arep[:], ALU.mult, R=[dtt_s, arep], W=[dA])
                kb.tt("dve", xdt_tok[:].rearrange("p (h d) -> p h d", d=HP), x_tok_s[:].rearrange("p (h d) -> p h d", d=HP),
                      dtt_s[:].unsqueeze(2).to_broadcast([16, NH, HP]), ALU.mult, R=[x_tok_s, dtt_s], W=[xdt_tok])
                kb.tt("dve", CTm[:], CT_s[:].unsqueeze(3).to_broadcast([128, 4, 16, 16]),
                      i16[:].rearrange("p (a b) -> p a b", a=16).unsqueeze(1).to_broadcast([128, 4, 16, 16]), ALU.mult,
                      R=[CT_s, i16], W=[CTm])
                pY = [kb.psum[4 + g] for g in range(4)]
                for sq in range(4):
                    kb.dma("sp", hs_nat[:], sssm[sq].rearrange("(k p) n -> p k n", p=128), W=[hs_nat])
                    for q4 in range(4):
                        pt = kb.ps()
                        for k4 in range(4):
                            k = q4 * 4 + k4
                            kb.tr(pt.t[:, k4 * 128:(k4 + 1) * 128], hs_nat[:, k, :], identf[:], R=[hs_nat, identf], W=[pt])
                        kb.cp("act", h32_s[:, q4 * 512:(q4 + 1) * 512], pt.t[:, 0:512], R=[pt], W=[h32_s])
                    for t in range(4):
                        tk = sq * 4 + t
                        sel_tk = selm[:, tk * 128:(tk + 1) * 128]
                        pd = kb.ps()
                        kb.mm(pd.t[:, 0:32], sel_tk, dA[:], True, True, R=[selm, dA], W=[pd])
                        kb.act(decb[:], pd.t[:, 0:32], AF.Exp, R=[pd], W=[decb])
                        for g in range(4):
                            px = kb.ps()
                            kb.mm(px.t[:, 0:512], sel_tk, xdt_tok[:, g * 512:(g + 1) * 512], True, True,
                                  R=[selm, xdt_tok], W=[px])
                            hg = h32_s[:, g * 512:(g + 1) * 512]
                            hv = hg.rearrange("p (h d) -> p h d", d=HP)
                            kb.tt("dve", hv, hv, decb[:, g * 8:(g + 1) * 8].unsqueeze(2).to_broadcast([128, 8, HP]),
                                  ALU.mult, R=[h32_s, decb], W=[h32_s])
                            kb.stt("dve", hg, px.t[:, 0:512], BT_s[:, g, tk:tk + 1], hg, ALU.mult, ALU.add,
                                   R=[px, BT_s, h32_s], W=[h32_s])
                            kb.mm(pY[g].t[0:16, 0:512], CTm[:, g, tk, :], hg, tk == 0, tk == 15,
                                  R=[CTm, h32_s], W=[pY[g]])
                    for q4 in range(4):
                        pt = kb.ps()
                        for k4 in range(4):
                            k = q4 * 4 + k4
                            kb.tr(pt.t[:, k4 * 128:(k4 + 1) * 128], h32_s[:, k * 128:(k + 1) * 128], identf[:],
                                  R=[h32_s, identf], W=[pt])
                        kb.cp("act", hs_nat[:, q4 * 4:(q4 + 1) * 4, :], pt.t[:, 0:512].rearrange("p (k n) -> p k n", k=4),
                              R=[pt], W=[hs_nat])
                    kb.dma("sp", ssms[sq].rearrange("(k p) n -> p k n", p=128), hs_nat[:], R=[hs_nat])
                for g in range(4):
                    gsl = slice(g * 512, (g + 1) * 512)
                    kb.tt("dve", y_s[:, gsl].rearrange("p (h d) -> p h d", d=HP),
                          x_tok_s[:, gsl].rearrange("p (h d) -> p h d", d=HP),
                          dsk[:, g * 8:(g + 1) * 8].unsqueeze(2).to_broadcast([16, 8, HP]), ALU.mult,
                          R=[x_tok_s, dsk], W=[y_s])
                    kb.tt("dve", y_s[:, gsl], y_s[:, gsl], pY[g].t[0:16, 0:512], ALU.add, R=[y_s, pY[g]], W=[y_s])
                    kb.tt("dve", y_s[:, gsl], y_s[:, gsl], zs_s[:, gsl], ALU.mult, R=[y_s, zs_s], W=[y_s])
                    kb.act(junk[0:16, 0:512], y_s[:, gsl], AF.Square, R=[y_s], W=[junk, gms_s],
                           scale=float(512 ** -0.5), accum=gms_s[:, g:g + 1])
                kb.act(gms_s[:], gms_s[:], AF.Ln, R=[gms_s], W=[gms_s], bias=EPS)
                kb.act(gms_s[:], gms_s[:], AF.Exp, R=[gms_s], W=[gms_s], scale=-0.5)
                for g in range(4):
                    gsl = slice(g * 512, (g + 1) * 512)
                    kb.stt("dve", yn_s[:, gsl], y_s[:, gsl], gms_s[:, g:g + 1], gssm[:, gsl], ALU.mult, ALU.mult,
                           R=[y_s, gms_s, gssm], W=[yn_s])
                ynT_s = hT
                for half in range(2):
                    pt = kb.ps()
                    pbv = pt.t[:].bitcast(BF16)
                    for k8 in range(8):
                        kc = half * 8 + k8
                        kb.tr(pbv[:, k8 * 128:k8 * 128 + 16], yn_s[:, kc * 128:(kc + 1) * 128], identb[0:16, 0:16],
                              R=[yn_s, identb], W=[pt])
                    kb.cp("act", ynT_s[:, half * 8:(half + 1) * 8, 0:16],
                          pbv[:, 0:1024].rearrange("p (k t) -> p k t", k=8)[:, :, 0:16], R=[pt], W=[hT])
                for hf in range(2):
                    p = kb.ps()
                    for kbk in range(2):
                        wv = wnext("out", 8, 512)
                        for kc in range(8):
                            kcg = kbk * 8 + kc
                            kb.mm(p.t[0:16, 0:512], ynT_s[:, kcg, 0:16], wv[:, kc, :], kcg == 0, kcg == 15,
                                  R=[hT, wv_buf[0]], W=[p])
                    kb.tt("dve", xt[0:16, 0, hf * 512:(hf + 1) * 512], xt[0:16, 0, hf * 512:(hf + 1) * 512],
                          p.t[0:16, 0:512], ALU.add, R=[xs_[0], p], W=[xs_[0]])
                ffn(0, 1, 1, 16)
                norm_T(2, 1, 16)
                wv = wnext("kv", 8, 512)
                p = proj_tok(xnT, 8, wv, 512, 0, 16)
                kb.dma("sp", rts[:], c_rope_s, W=[rts])
                kb.cp("act", kvs[:], p.t[0:16, 0:512], R=[p], W=[kvs])
                rope(krs, krs[:], kvs, kvs[:, 0:256], 4, rts)
                kb.dma("sp", ks, krs[:], R=[krs], W=[ksB])
                kb.dma("sp", vs, kvs[:, 256:512], R=[kvs], W=[vsB])
                S.barrier()
            S.barrier()
            esB = ExitStack()
            with esB:
                def sbB(name, shape, dt):
                    return Buf(esB.enter_context(nc.sbuf_tensor("sB_" + name, list(shape), dt)), name)
                gfin_s = sbB("gfin_s", [16, D], F32)
                kb.dma("sp", gfin_s[:], g_all[5, 0:16, :], W=[gfin_s])
                KT_s = sbB("KT_s", [KRS, 4, NKS], BF16)
                Vaug_s = sbB("Vaug_s", [128, NPG + 1, 4, 65], BF16)
                kmT_s = sbB("kmT_s", [64, 4, NBPS], F32)
                gmask_s = sbB("gmask_s", [128, 3 * NBPS], F32)
                trimask_s = sbB("trimask_s", [128, 16], BF16)
                ptab_i = sbB("ptab_i", [128, 4 * NPG], I32)
                pidx = sbB("pidx", [128, 1], F32)
                idxf = sbB("idxf", [128, 4 * NPG], F32)
                idxu = sbB("idxu", [128, 4 * NPG], U32)
                kpg = [[sbB(f"kpg{i}_{j}", [128, 256], F32) for j in range(2)] for i in range(4)]
                vpg = [[sbB(f"vpg{i}_{j}", [128, 256], F32) for j in range(2)] for i in range(4)]
                kbb_s = [sbB(f"kbb_s{i}", [128, 256], BF16) for i in range(2)]
                qraw_s = sbB("qraw_s", [16, D], F32)
                qr_s = sbB("qr_s", [16, D], F32)
                rtq_s = sbB("rtq_s", [16, 64], F32)
                q4 = sbB("q4", [4, D], F32)
                knew = sbB("knew", [4, 512], F32)
                knb = sbB("knb", [4, 256], BF16)
                Qaug_s = sbB("Qaug_s", [4, NQH, KRS], BF16)
                QT32_s = sbB("QT32_s", [64, 4, 4], F32)
                QTaug_s = sbB("QTaug_s", [KRS, NQH, 4], BF16)
                gm_s = sbB("gm_s", [4, NQH, NBPS], F32)
                m8_s = sbB("m8_s", [4, NQH, 8], F32)
                sel_s = sbB("sel_s", [4, NQH, NBPS], F32)
                PT_s = [sbB(f"PT_s{i}", [128, 512], BF16) for i in range(2)]
                on_s = sbB("on_s", [64, 16], F32)
                rden_s = sbB("rden_s", [65, 16], F32)
                AT_s = sbB("AT_s", [128, 8, 16], BF16)
                yo_s = sbB("yo_s", [16, D], F32)
                print("sampleB sbuf remaining", nc.sbuf_bytes_remaining)
                kb.dma("sp", gmask_s[:], c_gmask_s, W=[gmask_s])
                kb.dma("sp", trimask_s[:], c_trimask_s, W=[trimask_s])
                kb.dma("sp", pidx[:], c_pidx, W=[pidx])
                kb.dma("sp", ptab_i[:], ptab.rearrange("b j -> (b j)").partition_broadcast(128), W=[ptab_i])
                kb.ts("dve", idxf[:], ptab_i[:], 128.0, pidx[:, 0:1], ALU.mult, ALU.add, R=[ptab_i, pidx], W=[idxf])
                kb.cp("dve", idxu[:], idxf[:], R=[idxf], W=[idxu])
                kb.memset("pool", kmT_s[:], 0.0, W=[kmT_s])
                kb.memset("pool", Vaug_s[:], 1.0, W=[Vaug_s])
                for kv in range(4):
                    kb.dma("sp", KT_s[64:64 + NBPS, kv, :], c_onehot_s, W=[KT_s])
                norm_T(3, 1, 16)
                for b in range(2):
                    wv = wnext("q", 8, 512)
                    p = proj_tok(xnT, 8, wv, 512, 0, 16)
                    kb.cp("act", qraw_s[:, b * 512:(b + 1) * 512], p.t[0:16, 0:512], R=[p], W=[qraw_s])
                kb.dma("sp", rtq_s[:], c_ropeq_s, W=[rtq_s])
                rope(qr_s, qr_s[:], qraw_s, qraw_s[:], NQH, rtq_s)
                pi2 = 0
                for sq in range(4):
                    for n in range(NPB):
                        kin = kpg[(sq * NPB + n) % 4]
                        vin = vpg[(sq * NPB + n) % 4]
                        for s2 in range(2):
                            col = sq * NPG + n * 2 + s2
                            S.dma("pool", (lambda o_, c_: (lambda e: e.indirect_dma_start(
                                out=o_, out_offset=None, in_=cache_k,
                                in_offset=bass.IndirectOffsetOnAxis(ap=idxu[:, c_:c_ + 1], axis=0))))(kin[s2][:], col),
                                R=[idxu], W=[kin[s2]])
                            S.dma("pool", (lambda o_, c_: (lambda e: e.indirect_dma_start(
                                out=o_, out_offset=None, in_=cache_v,
                                in_offset=bass.IndirectOffsetOnAxis(ap=idxu[:, c_:c_ + 1], axis=0))))(vin[s2][:], col),
                                R=[idxu], W=[vin[s2]])
                        pm = kb.ps()
                        for h in range(4):
                            for s2 in range(2):
                                kb.mm(pm.t[0:64, h:h + 1], kin[s2][:, h * 64:(h + 1) * 64], ones[:, 0:1], s2 == 0, s2 == 1,
                                      R=[kin[s2], ones], W=[pm])
                        kb.act(kmT_s[:, :, n], pm.t[0:64, 0:4], AF.Copy, R=[pm], W=[kmT_s], scale=1.0 / 256.0)
                        for s2 in range(2):
                            kt = n * 2 + s2
                            kb_ = kbb_s[kt % 2]
                            kb.cp("dve", kb_[:], kin[s2][:], R=[kin[s2]], W=[kb_])
                            pt = kb.ps()
                            pbv = pt.t[:].bitcast(BF16)
                            for h in range(4):
                                kb.tr(pbv[0:64, h * 128:(h + 1) * 128], kb_[:, h * 64:(h + 1) * 64], identb[:],
                                      R=[kb_, identb], W=[pt])
                            kb.cp("act", KT_s[0:64, :, kt * 128:(kt + 1) * 128],
                                  pbv[0:64, 0:512].rearrange("p (h t) -> p h t", h=4), R=[pt], W=[KT_s])
                            kb.cp("dve", Vaug_s[:, kt, :, 0:64], vin[s2][:].rearrange("p (h d) -> p h d", h=4),
                                  R=[vin[s2]], W=[Vaug_s])
                    kb.dma("sp", knew[:, 0:256], ks[sq * 4:(sq + 1) * 4, :], R=[ksB], W=[knew])
                    kb.dma("sp", knew[:, 256:512], vs[sq * 4:(sq + 1) * 4, :], R=[vsB], W=[knew])
                    kb.cp("dve", knb[:], knew[:, 0:256], R=[knew], W=[knb])
                    pt = kb.ps()
                    pbv = pt.t[:].bitcast(BF16)
                    for h in range(4):
                        kb.tr(pbv[0:64, h * 128:h * 128 + 4], knb[:, h * 64:(h + 1) * 64], identb[0:4, 0:4],
                              R=[knb, identb], W=[pt])
                    kb.cp("act", KT_s[0:64, :, NPAST:NPAST + 4],
                          pbv[0:64, 0:512].rearrange("p (h t) -> p h t", h=4)[:, :, 0:4], R=[pt], W=[KT_s])
                    kb.cp("dve", Vaug_s[0:4, NPG, :, 0:64], knew[:, 256:512].rearrange("p (h d) -> p h d", h=4),
                          R=[knew], W=[Vaug_s])
                    kb.dma("sp", q4[:], qr_s[sq * 4:(sq + 1) * 4, :], R=[qr_s], W=[q4])
                    kb.cp("act", Qaug_s[:, :, 0:64], q4[:].rearrange("p (h d) -> p h d", d=64), R=[q4], W=[Qaug_s])
                    pgs = [kb.psum[4], kb.psum[5]]
                    for q4i in range(4):
                        pt = kb.ps()
                        for k4 in range(4):
                            h = q4i * 4 + k4
                            kb.tr(pt.t[0:64, k4 * 4:(k4 + 1) * 4], q4[:, h * 64:(h + 1) * 64], identf[0:4, 0:4],
                                  R=[q4, identf], W=[pt])
                        kb.cp("dve", QT32_s[:], pt.t[0:64, 0:16].rearrange("p (h t) -> p h t", h=4), R=[pt], W=[QT32_s])
                        for k4 in range(4):
                            h = q4i * 4 + k4
                            pg = pgs[h // 8]
                            kb.mm(pg.t[0:4, (h % 8) * NBPS:(h % 8 + 1) * NBPS], QT32_s[:, k4, :], kmT_s[:, h // 4, :],
                                  True, True, R=[QT32_s, kmT_s], W=[pg])
                    for hf in range(2):
                        kb.tt("dve", gm_s[:, hf * 8:(hf + 1) * 8, :],
                              pgs[hf].t[0:4, 0:8 * NBPS].rearrange("p (h n) -> p h n", h=8),
                              gmask_s[0:4, 0:NBPS].unsqueeze(1).to_broadcast([4, 8, NBPS]), ALU.add,
                              R=[pgs[hf], gmask_s], W=[gm_s])
                    for h in range(NQH):
                        S.op("dve", (lambda h_: (lambda e: e.max(m8_s[:, h_, :], gm_s[:, h_, :])))(h), R=[gm_s], W=[m8_s])
                    kb.tt("dve", sel_s[:], gm_s[:], m8_s[:, :, TOPK - 1:TOPK].to_broadcast([4, NQH, NBPS]), ALU.is_ge,
                          R=[gm_s, m8_s], W=[sel_s])
                    kb.tt("dve", sel_s[:], sel_s[:], gmask_s[0:4, NBPS:2 * NBPS].unsqueeze(1).to_broadcast([4, NQH, NBPS]),
                          ALU.mult, R=[sel_s, gmask_s], W=[sel_s])
                    kb.tt("dve", sel_s[:], sel_s[:], gmask_s[0:4, 2 * NBPS:3 * NBPS].unsqueeze(1).to_broadcast([4, NQH, NBPS]),
                          ALU.add, R=[sel_s, gmask_s], W=[sel_s])
                    kb.ts("dve", Qaug_s[:, :, 64:KRS], sel_s[:], -1.0, -NEG, ALU.add, ALU.mult, R=[sel_s], W=[Qaug_s])
                    pt = kb.ps()
                    pbv = pt.t[:].bitcast(BF16)
                    for h in range(NQH):
                        kb.tr(pbv[0:KRS, h * 4:(h + 1) * 4], Qaug_s[:, h, :], identb[0:4, 0:4], R=[Qaug_s, identb], W=[pt])
                    kb.cp("act", QTaug_s[:], pbv[0:KRS, 0:64].rearrange("p (h t) -> p h t", h=NQH), R=[pt], W=[QTaug_s])
                    for kv in range(4):
                        pO = kb.psum[6 + (kv % 2)]
                        rq = QTaug_s[:, kv * 4:(kv + 1) * 4, :]
                        kt0 = 0
                        while kt0 < NPG:
                            nk = min(32, NPG - kt0)
                            pS = kb.ps()
                            for k_ in range(nk):
                                kt = kt0 + k_
                                kb.mm(pS.t[:, k_ * 16:(k_ + 1) * 16], KT_s[:, kv, kt * 128:(kt + 1) * 128], rq, True, True,
                                      R=[KT_s, QTaug_s], W=[pS])
                            ptb = PT_s[pi2 % 2]
                            pi2 += 1
                            kb.act(ptb[:, 0:nk * 16], pS.t[:, 0:nk * 16], AF.Exp, R=[pS], W=[ptb])
                            for k_ in range(nk):
                                kt = kt0 + k_
                                kb.mm(pO.t[0:65, 0:16], Vaug_s[:, kt, kv, :], ptb[:, k_ * 16:(k_ + 1) * 16], kt == 0, False,
                                      R=[Vaug_s, ptb], W=[pO])
                            kt0 += nk
                        pS = kb.ps()
                        kb.mm(pS.t[0:4, 0:16], KT_s[:, kv, NPAST:NPAST + 4], rq, True, False, R=[KT_s, QTaug_s], W=[pS])
                        kb.mm(pS.t[0:4, 0:16], identb[0:4, 0:4], trimask_s[0:4, :], False, True, R=[identb, trimask_s], W=[pS])
                        ptb = PT_s[pi2 % 2]
                        pi2 += 1
                        kb.act(ptb[0:4, 0:16], pS.t[0:4, 0:16], AF.Exp, R=[pS], W=[ptb])
                        kb.mm(pO.t[0:65, 0:16], Vaug_s[0:4, NPG, kv, :], ptb[0:4, 0:16], False, True, R=[Vaug_s, ptb], W=[pO])
                        kb.cp("act", on_s[:], pO.t[0:64, 0:16], R=[pO], W=[on_s])
                        S.op("dve", lambda e, pO=pO: e.reciprocal(rden_s[64:65, :], pO.t[64:65, 0:16]), R=[pO], W=[rden_s])
                        pB = kb.ps()
                        kb.mm(pB.t[0:64, 0:16], ones[64:65, 0:64], rden_s[64:65, :], True, True, R=[ones, rden_s], W=[pB])
                        onv = on_s[:].rearrange("p (gp two t) -> p gp two t", two=2, t=4)
                        pbv2 = pB.t[0:64, 0:16].rearrange("p (gp two t) -> p gp two t", two=2, t=4)
                        for two in range(2):
                            kb.tt("dve", AT_s[two * 64:(two + 1) * 64, kv * 2:kv * 2 + 2, sq * 4:(sq + 1) * 4],
                                  onv[:, :, two, :], pbv2[:, :, two, :], ALU.mult, R=[on_s, pB], W=[AT_s])
                for b in range(2):
                    wv = wnext("o", 8, 512)
                    p = proj_tok(AT_s, 8, wv, 512, 0, 16)
                    kb.tt("dve", xt[0:16, 0, b * 512:(b + 1) * 512], xt[0:16, 0, b * 512:(b + 1) * 512],
                          p.t[0:16, 0:512], ALU.add, R=[xs_[0], p], W=[xs_[0]])
                ffn(1, 4, 1, 16)
                kb.act(xnb[0][0:16, :], xt[0:16, 0, :], AF.Square, R=[xs_[0]], W=[xnb[0], ms], scale=1.0 / 32.0, accum=ms[0:16, 0:1])
                kb.act(rs[0:16, 0:1], ms[0:16, 0:1], AF.Ln, R=[ms], W=[rs], bias=EPS)
                kb.act(rs[0:16, 0:1], rs[0:16, 0:1], AF.Exp, R=[rs], W=[rs], scale=-0.5)
                kb.stt("dve", yo_s[:], xt[0:16, 0, :], rs[0:16, 0:1], gfin_s[:], ALU.mult, ALU.mult,
                       R=[xs_[0], rs, gfin_s], W=[yo_s])
                kb.dma("sp", ys, yo_s[:], R=[yo_s])
                S.barrier()
            esS.close()
        S.finish()
    return nc


def _blockify(W, col_starts, cols, kp=128):
    K = W.shape[0]
    KC = K // kp
    out = np.empty((len(col_starts), kp, KC * cols), np.float32)
    for b, c0 in enumerate(col_starts):
        blk = W[:, c0:c0 + cols].reshape(KC, kp, cols).transpose(1, 0, 2)
        out[b] = blk.reshape(kp, KC * cols)
    return out


def _rep(v, n=128):
    return np.ascontiguousarray(np.broadcast_to(np.asarray(v, np.float32)[None], (n,) + tuple(np.shape(v))))


def _sub_of(step, role):
    lo, hi = 2 * step, 2 * step + 1
    a = lo if step % 2 == 0 else hi
    return a if role == 0 else (lo + hi - a)


def _consts(NT, NBP, role=0, pos0=0):
    NSUB = NT // 128
    NSTEP = NSUB // 2
    NB = NT // 256
    bf = ml_dtypes.bfloat16
    c = {}
    c["c_identf"] = np.eye(128, dtype=np.float32)
    r = np.arange(128)
    c["c_tri"] = (r[:, None] <= r[None, :]).astype(np.float32)
    c["c_ones"] = np.ones((128, 128), np.float32)
    c["c_mneg"] = np.where(r[None, :] < r[:, None], -1.0e6, 0.0).astype(np.float32).astype(bf)
    c["c_identb"] = np.eye(128).astype(bf)
    tm = np.where(r[:, None] > r[None, :], NEG, 0.0).astype(np.float32)
    c["c_trimask"] = np.tile(tm, (1, 4)).astype(bf)
    oh = np.zeros((NBP, NT), np.float32)
    for n in range(NB):
        oh[n, n * 256:(n + 1) * 256] = 1.0
    c["c_onehot"] = oh.astype(bf)
    half = 32
    inv = (np.float32(10000.0) ** (-np.arange(half, dtype=np.float32) / np.float32(half))).astype(np.float32)
    pos = (pos0 + np.arange(NT)).astype(np.float32)
    ang = (pos[:, None] * inv[None, :]).astype(np.float32)
    tab = np.concatenate([np.cos(ang), np.sin(ang)], axis=1).astype(np.float32)
    c["c_rope"] = tab
    subs = [_sub_of(i, role) for i in range(NSTEP)]
    rows = np.concatenate([np.arange(sb_ * 128, (sb_ + 1) * 128) for sb_ in subs])
    c["c_ropeq"] = np.ascontiguousarray((tab * np.float32(0.125)).astype(np.float32)[rows])
    c["c_x2idx"] = np.ascontiguousarray(rows.reshape(NSTEP, 128).T.astype(np.uint32))
    gmk = np.zeros((NSTEP, 3, NBP), np.float32)
    nn = np.arange(NBP)
    for i in range(NSTEP):
        qb = i
        gmk[i, 0] = np.where(nn < qb, 0.0, -1e30)
        gmk[i, 1] = (nn < qb).astype(np.float32)
        gmk[i, 2] = (nn == qb).astype(np.float32)
    c["c_gmask"] = _rep(gmk.reshape(-1))
    tri4 = np.tile(tm, (1, 4))
    negf = np.full((128, 512), NEG, np.float32)
    zero = np.zeros((128, 512), np.float32)
    mks = np.empty((NSTEP, 128, 1024), np.float32)
    for i in range(NSTEP):
        if subs[i] == 2 * i:
            mks[i, :, 0:512], mks[i, :, 512:1024] = tri4, negf
        else:
            mks[i, :, 0:512], mks[i, :, 512:1024] = zero, tri4
    c["c_masks"] = mks.astype(bf)
    return c


def _weights(inp):
    w = {}
    w_in = np.asarray(inp["w_in_ssm"][0], np.float32)
    w["w_z"] = _blockify(w_in, [0, 512, 1024, 1536], 512)
    w["w_x"] = _blockify(w_in, [2048 + 512 * b for b in range(6)], 512)
    w["w_dt"] = _blockify(w_in, [5120], 32)[0]
    Wout = np.asarray(inp["w_out_ssm"][0], np.float32)
    wo_ = np.empty((4, 128, 4096), np.float32)
    for hf in range(2):
        for kbk in range(2):
            wo_[hf * 2 + kbk] = _blockify(Wout[kbk * 1024:(kbk + 1) * 1024], [hf * 512], 512)[0]
    w["w_out"] = wo_
    gu = np.empty((2, 11, 128, 4096), np.float32)
    dn = np.zeros((2, 6, 128, 4096), np.float32)
    for l in range(2):
        W = np.asarray(inp["w_gu"][l], np.float32)
        for b in range(11):
            blk = np.concatenate([W[:, 256 * b:256 * b + 256], W[:, DFF + 256 * b:DFF + 256 * b + 256]], axis=1)
            gu[l, b] = _blockify(blk, [0], 512)[0]
        Wd = np.asarray(inp["w_down"][l], np.float32)
        for hf in range(2):
            for kbk in range(3):
                nk = 8 if kbk < 2 else 6
                blk = _blockify(Wd[kbk * 1024:kbk * 1024 + nk * 128], [hf * 512], 512)[0]
                dn[l, hf * 3 + kbk, :, 0:nk * 512] = blk
    w["w_gu"] = gu
    w["w_dn"] = dn
    w["w_kv"] = _blockify(np.asarray(inp["w_kv"], np.float32), [0], 512)
    w["w_q"] = _blockify(np.asarray(inp["w_q"][0], np.float32), [0, 512], 512)
    w["w_o"] = _blockify(np.asarray(inp["w_o"][0], np.float32), [0, 512], 512)
    g = np.stack([inp["norm_mix"][0], inp["norm_ffn"][0], inp["norm_kv"], inp["norm_mix"][1],
                  inp["norm_ffn"][1], inp["norm_final"]]).astype(np.float32)
    w["g_all"] = np.ascontiguousarray(np.broadcast_to(g[:, None, :], (6, 128, D)))
    w["g_allT"] = np.ascontiguousarray(g.reshape(6, 8, 128).transpose(2, 0, 1).reshape(128, 48))
    w["g_ssm"] = _rep(inp["norm_ssm"][0])
    cwv = np.asarray(inp["conv_w"][0], np.float32)
    w["cw"] = np.ascontiguousarray(cwv.reshape(4, 24, 128).transpose(2, 1, 0))
    w["cb"] = np.ascontiguousarray(np.asarray(inp["conv_b"][0], np.float32).reshape(24, 128).T)
    w["dtb"] = _rep(inp["dt_bias"][0])
    w["alog"] = _rep(inp["a_log"][0])
    w["dsk"] = _rep(inp["d_skip"][0])
    return w


_NC_CACHE = {}


def _consts_sample(NPAST):
    bf = ml_dtypes.bfloat16
    NPB = NPAST // 256
    NBPS = 8 * ((NPB + 1 + 7) // 8)
    c = {}
    half = 32
    inv = (np.float32(10000.0) ** (-np.arange(half, dtype=np.float32) / np.float32(half))).astype(np.float32)
    pos = (NPAST + (np.arange(16) % 4)).astype(np.float32)
    ang = (pos[:, None] * inv[None, :]).astype(np.float32)
    tab = np.concatenate([np.cos(ang), np.sin(ang)], axis=1).astype(np.float32)
    c["c_rope_s"] = tab
    c["c_ropeq_s"] = (tab * np.float32(0.125)).astype(np.float32)
    c["c_i16"] = np.ascontiguousarray(np.tile(np.eye(16, dtype=np.float32).reshape(1, -1), (128, 1)))
    sel = np.zeros((16, 16, 128), np.float32)
    for t in range(16):
        sel[t, t, :] = 1.0
    c["c_sel"] = sel.reshape(16, 16 * 128)
    oh = np.zeros((NBPS, NPAST + 128), np.float32)
    for n in range(NPB):
        oh[n, n * 256:(n + 1) * 256] = 1.0
    oh[NPB, NPAST:NPAST + 128] = 1.0
    c["c_onehot_s"] = oh.astype(bf)
    nn = np.arange(NBPS)
    gmk = np.stack([np.where(nn < NPB, 0.0, -1e30), (nn < NPB).astype(np.float64), (nn == NPB).astype(np.float64)])
    c["c_gmask_s"] = _rep(gmk.astype(np.float32).reshape(-1))
    r = np.arange(128)
    q = np.arange(16) % 4
    c["c_trimask_s"] = np.where(r[:, None] > q[None, :], NEG, 0.0).astype(np.float32).astype(bf)
    c["c_pidx"] = np.arange(128, dtype=np.float32).reshape(128, 1)
    return c


def run_all(inp, NT, NPAST, n_cores=8):
    NB = NT // 256
    NBP = max(8, NB)
    ck = np.ascontiguousarray(np.asarray(inp["cache_k"], np.float32).reshape(-1, 256))
    cv = np.ascontiguousarray(np.asarray(inp["cache_v"], np.float32).reshape(-1, 256))
    key = (NT, NBP, NPAST, ck.shape[0])
    if key not in _NC_CACHE:
        _NC_CACHE[key] = build(NT, NBP, with_sample=True, NPAST=NPAST, NPOOLROWS=ck.shape[0])
    nc = _NC_CACHE[key]
    shared = {}
    role_c = [_consts(NT, NBP, 0), _consts(NT, NBP, 1)]
    shared.update(_consts_sample(NPAST))
    shared.update(_weights(inp))
    shared["cache_k"] = ck
    shared["cache_v"] = cv
    xpr = np.asarray(inp["x_prompt"], np.float32)
    xsm = np.asarray(inp["x_sample"], np.float32)
    sc = np.asarray(inp["state_conv"], np.float32)
    ssm = np.asarray(inp["state_ssm"], np.float32)
    pt = np.asarray(inp["page_table"], np.int32)
    B = xpr.shape[0]
    in_maps = []
    for c in range(n_cores):
        m = dict(shared)
        m.update(role_c[c // B])
        m["xp"] = np.ascontiguousarray(xpr[c % B])
        sl = slice(4 * c, 4 * c + 4)
        m["xs"] = np.ascontiguousarray(xsm[sl].reshape(16, D))
        m["sconv"] = np.ascontiguousarray(sc[0, sl].reshape(12, CD))
        m["sssm"] = np.ascontiguousarray(ssm[0, sl].reshape(4, DIN, NST))
        m["ptab"] = np.ascontiguousarray(pt[sl])
        in_maps.append(m)
    res = run_bass_kernel_spmd(nc, in_maps, core_ids=list(range(n_cores))).results
    y_prompt = np.empty((B, NT, D), np.float32)
    for b in range(B):
        for role in range(2):
            yc = res[b + B * role]["yp"]
            for i in range(NT // 256):
                sb_ = _sub_of(i, role)
                y_prompt[b, sb_ * 128:(sb_ + 1) * 128] = yc[i * 128:(i + 1) * 128]
    k_prompt = np.stack([res[b]["kp"] for b in range(B)]).reshape(B, NT, 4, 64).astype(np.float32)
    v_prompt = np.stack([res[b]["vp"] for b in range(B)]).reshape(B, NT, 4, 64).astype(np.float32)
    conv_prompt = np.stack([res[b]["convp"] for b in range(B)])[None].astype(np.float32)
    ssm_prompt = np.stack([res[b]["ssmp"] for b in range(B)]).reshape(1, B, NH, HP, NST).astype(np.float32)
    nS = 4 * n_cores
    y_sample = np.concatenate([res[c]["ys"] for c in range(n_cores)]).reshape(nS, 4, D).astype(np.float32)
    conv_sample = np.concatenate([res[c]["convs"] for c in range(n_cores)]).reshape(1, nS, 3, CD).astype(np.float32)
    ssm_sample = np.concatenate([res[c]["ssms"] for c in range(n_cores)]).reshape(1, nS, NH, HP, NST).astype(np.float32)
    k_sample = np.concatenate([res[c]["ks"] for c in range(n_cores)]).reshape(nS, 4, 4, 64).astype(np.float32)
    v_sample = np.concatenate([res[c]["vs"] for c in range(n_cores)]).reshape(nS, 4, 4, 64).astype(np.float32)
    return (y_prompt, y_sample, conv_prompt, ssm_prompt, k_prompt, v_prompt,
            conv_sample, ssm_sample, k_sample, v_sample)


def kernel(**inputs):
    return run_all(inputs, 4096, 8192)
```

```python
import numpy as np
from contextlib import ExitStack
import concourse.bass as bass
import concourse.mybir as mybir
from concourse.bass_utils import run_bass_kernel_spmd
import ml_dtypes

F32 = mybir.dt.float32
BF16 = mybir.dt.bfloat16
I32 = mybir.dt.int32
AF = mybir.ActivationFunctionType
ALU = mybir.AluOpType
AX = mybir.AxisListType

NDS = 6
SEM_LIMIT = 30000


class Buf:
    __slots__ = ("t", "name", "lw", "rd")

    def __init__(self, t, name):
        self.t = t
        self.name = name
        self.lw = None
        self.rd = {}

    def __getitem__(self, k):
        return self.t[k]


class Sched:
    COMPUTE = ("pe", "act", "dve", "pool")
    QUEUES = ("pe", "act", "dve", "pool", "sp")

    def __init__(self, nc, es):
        self.nc, self.es = nc, es
        self.q = {e: [] for e in self.QUEUES}
        self.known = {e: {} for e in self.QUEUES}
        self.csem, self.ccnt = {}, {}
        self.nsem = 0
        self.allsems = []
        for e in self.COMPUTE:
            self._newsem(e)
        self.dsem = {}
        for qn in ("sp", "pool", "act"):
            self.dsem[qn] = []
            for i in range(NDS):
                s = es.enter_context(nc.semaphore(f"d_{qn}_{i}"))
                self.dsem[qn].append([s, 0])
        self.drr = {qn: 0 for qn in self.dsem}
        self.n_ops = 0

    def _newsem(self, e):
        self.nsem += 1
        s = self.es.enter_context(self.nc.semaphore(f"c_{e}_{self.nsem}"))
        if e in self.csem:
            self.allsems.append((self.csem[e], self.ccnt[e]))
        self.csem[e] = s
        self.ccnt[e] = 0

    @staticmethod
    def _deps(R, W):
        deps = {}
        for b in R:
            if b.lw is not None:
                s, v = b.lw
                if deps.get(s, 0) < v:
                    deps[s] = v
        for b in W:
            if b.lw is not None:
                s, v = b.lw
                if deps.get(s, 0) < v:
                    deps[s] = v
            for s, v in b.rd.items():
                if deps.get(s, 0) < v:
                    deps[s] = v
        return deps

    def _wait(self, e, deps, skip=None):
        kn = self.known[e]
        for s, v in deps.items():
            if s is skip:
                continue
            if kn.get(s, 0) >= v:
                continue
            kn[s] = v
            self.q[e].append(("w", s, v))

    @staticmethod
    def _mark(ev, R, W):
        s, v = ev
        for b in W:
            b.lw = ev
            b.rd = {}
        for b in R:
            if any(b is w for w in W):
                continue
            if b.rd.get(s, 0) < v:
                b.rd[s] = v

    def op(self, e, fn, R=(), W=()):
        deps = self._deps(R, W)
        self._wait(e, deps, skip=self.csem[e] if e == "pe" else None)
        if self.ccnt[e] >= SEM_LIMIT:
            self._newsem(e)
        self.ccnt[e] += 1
        ev = (self.csem[e], self.ccnt[e])
        self.q[e].append(("o", fn, ev[0]))
        self._mark(ev, R, W)
        self.n_ops += 1

    def dma(self, qn, fn, R=(), W=()):
        deps = self._deps(R, W)
        self._wait(qn, deps)
        i = self.drr[qn]
        self.drr[qn] = (i + 1) % NDS
        ent = self.dsem[qn][i]
        sem, c = ent
        if c > 0:
            self._wait(qn, {sem: 16 * c})
        ent[1] = c + 1
        ev = (sem, 16 * (c + 1))
        self.q[qn].append(("d", fn, sem))
        self._mark(ev, R, W)
        self.n_ops += 1

    def barrier(self):
        deps = {}
        for e in self.COMPUTE:
            if self.ccnt[e] > 0:
                deps[self.csem[e]] = self.ccnt[e]
        for s, c in self.allsems:
            deps[s] = c
        for qn in self.dsem:
            for s, c in self.dsem[qn]:
                if c > 0:
                    deps[s] = 16 * c
        for e in self.QUEUES:
            self._wait(e, dict(deps), skip=None)

    def finish(self):
        self.barrier()
        nc = self.nc
        q = self.q

        def run(eng, items):
            for it in items:
                if it[0] == "w":
                    eng.wait_ge(it[1], it[2])
                elif it[0] == "o":
                    it[1](eng).then_inc(it[2], 1)
                else:
                    it[1](eng).then_inc(it[2], 16)

        with nc.Block() as block:
            @block.tensor
            def _(e):
                run(e, q["pe"])

            @block.scalar
            def _(e):
                run(e, q["act"])

            @block.vector
            def _(e):
                run(e, q["dve"])

            @block.gpsimd
            def _(e):
                run(e, q["pool"])

            @block.sync
            def _(e):
                run(e, q["sp"])


D = 1024
DIN = 2048
NH = 32
HP = 64
NG = 4
NST = 128
CD = 3072
DFF = 2816
NQH = 16
NKV = 4
HD = 64
EPS = 1e-6
NEG = -30000.0
TOPK = 3
DEBUG = False


class KB:
    def __init__(self, nc, es):
        self.nc, self.es = nc, es
        self.S = Sched(nc, es)
        self.psum = [Buf(es.enter_context(nc.psum_tensor(f"ps{i}", [128, 512], F32)), f"ps{i}") for i in range(8)]
        self.pi = 0
        self.NROT = 6

    def ps(self):
        b = self.psum[self.pi]
        self.pi = (self.pi + 1) % self.NROT
        return b

    def sb(self, name, shape, dt):
        return Buf(self.es.enter_context(self.nc.sbuf_tensor("s_" + name, list(shape), dt)), name)

    def mm(self, out, lhsT, rhs, st, sp, R, W):
        self.S.op("pe", lambda e: e.matmul(out, lhsT, rhs, start=st, stop=sp), R, W)

    def tr(self, out, in_, ident, R, W):
        self.S.op("pe", lambda e: e.transpose(out, in_, ident), R, W)

    def act(self, out, in_, func, R, W, bias=None, scale=None, accum=None):
        kw = {}
        if bias is not None:
            kw["bias"] = bias
        if scale is not None:
            kw["scale"] = scale
        if accum is not None:
            kw["accum_out"] = accum
        self.S.op("act", lambda e: e.activation(out, in_, func, **kw), R, W)

    def tt(self, eng, out, a, b, op, R, W):
        self.S.op(eng, lambda e: e.tensor_tensor(out, a, b, op), R, W)

    def ts(self, eng, out, a, s1, s2, op0, op1, R, W):
        if op1 is None:
            self.S.op(eng, lambda e: e.tensor_scalar(out, a, s1, None, op0), R, W)
        else:
            self.S.op(eng, lambda e: e.tensor_scalar(out, a, s1, s2, op0, op1), R, W)

    def stt(self, eng, out, a, sc, b, op0, op1, R, W):
        self.S.op(eng, lambda e: e.scalar_tensor_tensor(out, a, sc, b, op0, op1), R, W)

    def cp(self, eng, out, in_, R, W):
        if eng == "act":
            self.S.op("act", lambda e: e.activation(out, in_, AF.Copy), R, W)
        else:
            self.S.op(eng, lambda e: e.tensor_copy(out, in_), R, W)

    def memset(self, eng, out, val, W):
        self.S.op(eng, lambda e: e.memset(out, val), (), W)

    def dma(self, qn, out, in_, R=(), W=()):
        self.S.dma(qn, lambda e: e.dma_start(out=out, in_=in_), R, W)


class WStream:
    def __init__(self, kb, nbuf=3, elems=4096):
        self.kb = kb
        self.bufs = [kb.sb(f"wbuf{i}", [128, elems], BF16) for i in range(nbuf)]
        self.plan = []
        self.issued = 0
        self.taken = 0

    def add(self, ap, parts, elems, tag):
        self.plan.append((ap, parts, elems, tag))

    def _issue(self):
        ap, parts, elems, tag = self.plan[self.issued]
        b = self.bufs[self.issued % len(self.bufs)]
        self.kb.dma("pool", b[0:parts, 0:elems], ap, W=[b])
        self.issued += 1

    def next(self, tag):
        n = len(self.bufs)
        while self.issued < len(self.plan) and self.issued < self.taken + n - 1:
            self._issue()
        ap, parts, elems, t = self.plan[self.taken]
        assert t == tag, (t, tag)
        b = self.bufs[self.taken % n]
        self.taken += 1
        return b


def build(NT, NBP, with_sample=False, NPAST=8192, NPOOLROWS=2560 * 128):
    NTILES = NT // 512
    NSUB = NT // 128
    NB = NT // 256
    NSEL = min(TOPK, NB - 1)
    nc = bass.Bass("TRN2", target_bir_lowering=False)
    es = ExitStack()

    def din(name, shape, dt=F32):
        return nc.dram_tensor(name, list(shape), dt, kind="ExternalInput").ap()

    def dout(name, shape, dt=F32):
        return nc.dram_tensor(name, list(shape), dt, kind="ExternalOutput").ap()

    xp = din("xp", [NT, D])
    c_identf = din("c_identf", [128, 128])
    c_tri = din("c_tri", [128, 128])
    c_ones = din("c_ones", [128, 128])
    c_mneg = din("c_mneg", [128, 128], BF16)
    c_identb = din("c_identb", [128, 128], BF16)
    c_trimask = din("c_trimask", [128, 512], BF16)
    c_onehot = din("c_onehot", [NBP, NT], BF16)
    c_rope = din("c_rope", [NT, 64])
    c_ropeq = din("c_ropeq", [NT // 2, 64])
    NSTEP = NSUB // 2
    NT3 = NT // 2
    c_gmask = din("c_gmask", [128, NSTEP * 3 * NBP])
    c_x2idx = din("c_x2idx", [128, NSTEP], mybir.dt.uint32)
    c_masks = din("c_masks", [NSTEP, 128, 2 * 512], BF16)
    g_all = din("g_all", [6, 128, D])
    g_allT = din("g_allT", [128, 6 * 8])
    g_ssm = din("g_ssm", [128, DIN])
    cw_d = din("cw", [128, 24, 4])
    cb_d = din("cb", [128, 24])
    dtb_d = din("dtb", [128, NH])
    alog_d = din("alog", [128, NH])
    dsk_d = din("dsk", [128, NH])
    w_z = din("w_z", [4, 128, 4096])
    w_x = din("w_x", [6, 128, 4096])
    w_dt = din("w_dt", [128, 8 * 32])
    w_out = din("w_out", [4, 128, 4096])
    w_gu = din("w_gu", [2, 11, 128, 4096])
    w_dn = din("w_dn", [2, 6, 128, 4096])
    w_kv = din("w_kv", [1, 128, 4096])
    w_q = din("w_q", [2, 128, 4096])
    w_o = din("w_o", [2, 128, 4096])

    yp = dout("yp", [NT // 2, D])
    kp = dout("kp", [NT, 256])
    vp = dout("vp", [NT, 256])
    convp = dout("convp", [3, CD])
    ssmp = dout("ssmp", [DIN, NST])
    x2d_t = nc.dram_tensor("x2d", [NT, D], F32, kind="Internal").ap()
    if with_sample:
        NBPS = 8 * ((NPAST // 256 + 1 + 7) // 8)
        xs = din("xs", [16, D])
        sconv = din("sconv", [12, CD])
        sssm = din("sssm", [4, DIN, NST])
        cache_k = din("cache_k", [NPOOLROWS, 256])
        cache_v = din("cache_v", [NPOOLROWS, 256])
        ptab = din("ptab", [4, NPAST // 128], I32)
        c_rope_s = din("c_rope_s", [16, 64])
        c_ropeq_s = din("c_ropeq_s", [16, 64])
        c_i16 = din("c_i16", [128, 256])
        c_sel = din("c_sel", [16, 16 * 128])
        c_onehot_s = din("c_onehot_s", [NBPS, NPAST + 128], BF16)
        c_gmask_s = din("c_gmask_s", [128, 3 * NBPS])
        c_trimask_s = din("c_trimask_s", [128, 16], BF16)
        c_pidx = din("c_pidx", [128, 1])
        ys = dout("ys", [16, D])
        convs = dout("convs", [12, CD])
        ssms = dout("ssms", [4, DIN, NST])
        ks = dout("ks", [16, 256])
        vs = dout("vs", [16, 256])
    dbg_at = dout("dbg_at", [NTILES // 2, 128, 8, 512], BF16) if DEBUG else None

    with es:
        kb = KB(nc, es)
        S = kb.S
        sb = kb.sb
        kpBs = [Buf(kp, f"kp_dram{i}") for i in range(NSUB)]
        vpBs = [Buf(vp, f"vp_dram{i}") for i in range(NSUB)]
        x2B = Buf(x2d_t, "x2_dram")

        identf = sb("identf", [128, 128], F32)
        tri = sb("tri", [128, 128], F32)
        ones = sb("ones", [128, 128], F32)
        identb = sb("identb", [128, 128], BF16)
        gT = sb("gT", [128, 6, 8], F32)
        kb.dma("sp", identf[:], c_identf, W=[identf])
        kb.dma("sp", tri[:], c_tri, W=[tri])
        kb.dma("sp", ones[:], c_ones, W=[ones])
        mneg = sb("mneg", [128, 128], BF16)
        kb.dma("sp", mneg[:], c_mneg, W=[mneg])
        kb.dma("sp", identb[:], c_identb, W=[identb])
        kb.dma("sp", gT[:].rearrange("p a b -> p (a b)"), g_allT, W=[gT])

        ws = WStream(kb)
        for j in range(NTILES):
            for b in range(4):
                ws.add(w_z[b], 128, 4096, "z")
            for b in range(6):
                ws.add(w_x[b], 128, 4096, "x")
            ws.add(w_dt, 128, 256, "dt")
            if j == NTILES - 1:
                for b in range(6):
                    ws.add(w_x[b], 128, 4096, "xc")
            for b in range(4):
                ws.add(w_out[b], 128, 4096, "out")
            for b in range(11):
                ws.add(w_gu[0, b], 128, 4096, "gu")
            for b in range(6):
                ws.add(w_dn[0, b][:, 0:(4096 if b % 3 < 2 else 3072)], 128, 4096 if b % 3 < 2 else 3072, "dn")
            ws.add(w_kv[0], 128, 4096, "kv")
        for j in range(NTILES // 2):
            for b in range(2):
                ws.add(w_q[b], 128, 4096, "q")
            for b in range(2):
                ws.add(w_o[b], 128, 4096, "o")
            for b in range(11):
                ws.add(w_gu[1, b], 128, 4096, "gu")
            for b in range(6):
                ws.add(w_dn[1, b][:, 0:(4096 if b % 3 < 2 else 3072)], 128, 4096 if b % 3 < 2 else 3072, "dn")

        if with_sample:
            for b in range(4):
                ws.add(w_z[b], 128, 4096, "z")
            for b in range(6):
                ws.add(w_x[b], 128, 4096, "x")
            ws.add(w_dt, 128, 256, "dt")
            for b in range(4):
                ws.add(w_out[b], 128, 4096, "out")
            for b in range(11):
                ws.add(w_gu[0, b], 128, 4096, "gu")
            for b in range(6):
                ws.add(w_dn[0, b][:, 0:(4096 if b % 3 < 2 else 3072)], 128, 4096 if b % 3 < 2 else 3072, "dn")
            ws.add(w_kv[0], 128, 4096, "kv")
            for b in range(2):
                ws.add(w_q[b], 128, 4096, "q")
            for b in range(2):
                ws.add(w_o[b], 128, 4096, "o")
            for b in range(11):
                ws.add(w_gu[1, b], 128, 4096, "gu")
            for b in range(6):
                ws.add(w_dn[1, b][:, 0:(4096 if b % 3 < 2 else 3072)], 128, 4096 if b % 3 < 2 else 3072, "dn")
        esP = ExitStack()

        def sbP(name, shape, dt):
            return Buf(esP.enter_context(nc.sbuf_tensor("sP_" + name, list(shape), dt)), name)
        xt = sbP("xt", [128, 4, D], F32)
        xs_ = [Buf(xt.t, f"xt_sub{i}") for i in range(4)]
        xnb = [sbP(f"xnb{i}", [128, D], BF16) for i in range(2)]
        xnT = sbP("xnT", [128, 8, 512], BF16)
        junk = sbP("junk", [128, 512], BF16)
        ms = sbP("ms", [128, 8], F32)
        rs = sbP("rs", [128, 8], F32)
        hT = sbP("hT", [128, 22, 512], BF16)
        cst = [sbP(f"cst{i}", [128, 515], F32) for i in range(2)]
        cacc = [sbP(f"cacc{i}", [128, 512], F32) for i in range(2)]

        def norm_T(gidx, nsub=4, ntok=128):
            for s_ in range(nsub):
                jb = xnb[(s_ + 1) % len(xnb)]
                kb.act(jb[0:ntok, :], xt[0:ntok, s_, :], AF.Square, R=[xs_[s_]], W=[jb, ms],
                       scale=1.0 / 32.0, accum=ms[0:ntok, s_:s_ + 1])
            kb.act(rs[0:ntok, 0:nsub], ms[0:ntok, 0:nsub], AF.Ln, R=[ms], W=[rs], bias=EPS)
            kb.act(rs[0:ntok, 0:nsub], rs[0:ntok, 0:nsub], AF.Exp, R=[rs], W=[rs], scale=-0.5)
            for s_ in range(nsub):
                xb_ = xnb[s_ % len(xnb)]
                kb.ts("dve", xb_[0:ntok, :], xt[0:ntok, s_, :], rs[0:ntok, s_:s_ + 1], None, ALU.mult, None,
                      R=[xs_[s_], rs], W=[xb_])
                p = kb.ps()
                pbv = p.t[:].bitcast(BF16)
                for kc in range(8):
                    kb.tr(pbv[:, kc * 128:kc * 128 + ntok], xb_[0:ntok, kc * 128:(kc + 1) * 128],
                          identb[0:ntok, 0:ntok], R=[xb_, identb], W=[p])
                kb.tt("dve", xnT[:, :, s_ * 128:s_ * 128 + ntok],
                      pbv[:, 0:1024].rearrange("p (k t) -> p k t", k=8)[:, :, 0:ntok],
                      gT[:, gidx, :].unsqueeze(2).to_broadcast([128, 8, ntok]), ALU.mult, R=[p, gT], W=[xnT])

        def proj_tok(xT_, KC, wv, cols, s_, ntok=128, p=None, c0=0):
            p = p or kb.ps()
            for kc in range(KC):
                kb.mm(p.t[0:ntok, c0:c0 + cols], xT_[:, kc, s_ * 128:s_ * 128 + ntok], wv[:, kc, 0:cols],
                      kc == 0, kc == KC - 1, R=[xT_, wv_buf[0]], W=[p])
            return p

        def proj_feat(xT_, KC, wv, col0, ntot):
            p = kb.ps()
            for kc in range(KC):
                kb.mm(p.t[:, 0:ntot], wv[:, kc, col0:col0 + 128], xT_[:, kc, 0:ntot],
                      kc == 0, kc == KC - 1, R=[xT_, wv_buf[0]], W=[p])
            return p

        wv_buf = [None]

        def wnext(tag, KC, cols, parts=128):
            b = ws.next(tag)
            wv_buf[0] = b
            return b.t[0:parts, 0:KC * cols].rearrange("p (k c) -> p k c", k=KC)

        def ffn(layer, gidx, nsub=4, ntok=128):
            ntot = (nsub - 1) * 128 + ntok
            norm_T(gidx, nsub, ntok)
            for b in range(11):
                wv = wnext("gu", 8, 512)
                for t2 in range(2):
                    fft = b * 2 + t2
                    pg = proj_feat(xnT, 8, wv, t2 * 128, ntot)
                    pu = proj_feat(xnT, 8, wv, 256 + t2 * 128, ntot)
                    sg_ = cacc[fft % 2]
                    kb.act(sg_[:, 0:ntot], pg.t[:, 0:ntot], AF.Silu, R=[pg], W=[sg_])
                    kb.tt("dve", hT[:, fft, 0:ntot], sg_[:, 0:ntot], pu.t[:, 0:ntot], ALU.mult, R=[sg_, pu], W=[hT])
            for hf in range(2):
                pp = [kb.ps() for _ in range(nsub)]
                for kbk in range(3):
                    nk = 8 if kbk < 2 else 6
                    wv = wnext("dn", nk, 512)
                    for s_ in range(nsub):
                        for kc in range(nk):
                            kcg = kbk * 8 + kc
                            kb.mm(pp[s_].t[0:ntok, 0:512], hT[:, kcg, s_ * 128:s_ * 128 + ntok], wv[:, kc, :],
                                  kcg == 0, kcg == 21, R=[hT, wv_buf[0]], W=[pp[s_]])
                for s_ in range(nsub):
                    kb.tt("dve", xt[0:ntok, s_, hf * 512:(hf + 1) * 512], xt[0:ntok, s_, hf * 512:(hf + 1) * 512],
                          pp[s_].t[0:ntok, 0:512], ALU.add, R=[xs_[s_], pp[s_]], W=[xs_[s_]])


        def rope(dstB, dst, srcB, src, H, tab):
            n = src.shape[0]
            rt1, rt2 = cst[0], cst[1]
            sv = src.rearrange("p (h two d) -> p h two d", two=2, d=32)
            dv = dst.rearrange("p (h two d) -> p h two d", two=2, d=32)
            cosb = tab[0:n, 0:32].unsqueeze(1).to_broadcast([n, H, 32])
            sinb = tab[0:n, 32:64].unsqueeze(1).to_broadcast([n, H, 32])
            a = rt1[0:n, 0:H * 32].rearrange("p (h d) -> p h d", d=32)
            b_ = rt2[0:n, 0:H * 32].rearrange("p (h d) -> p h d", d=32)
            x1, x2 = sv[:, :, 0, :], sv[:, :, 1, :]
            kb.tt("dve", a, x1, cosb, ALU.mult, R=[srcB, tab], W=[rt1])
            kb.tt("dve", b_, x2, sinb, ALU.mult, R=[srcB, tab], W=[rt2])
            kb.tt("dve", dv[:, :, 0, :], a, b_, ALU.subtract, R=[rt1, rt2], W=[dstB])
            kb.tt("dve", a, x2, cosb, ALU.mult, R=[srcB, tab, dstB], W=[rt1])
            kb.tt("dve", b_, x1, sinb, ALU.mult, R=[srcB, tab, dstB], W=[rt2])
            kb.tt("dve", dv[:, :, 1, :], a, b_, ALU.add, R=[rt1, rt2], W=[dstB])

        es2 = ExitStack()
        with es2:
            def sb2(name, shape, dt):
                return Buf(es2.enter_context(nc.sbuf_tensor("s2_" + name, list(shape), dt)), name)
            gssm = sb2("gssm", [128, DIN], F32)
            cw = sb2("cw", [128, 24, 4], F32)
            cbias = sb2("cbias", [128, 24], F32)
            dtb = sb2("dtb", [128, NH], F32)
            arep = sb2("arep", [128, NH], F32)
            dsk = sb2("dsk", [128, NH], F32)
            DI = sb2("DI", [128, NH, 128], BF16)
            kb.dma("sp", gssm[:], g_ssm, W=[gssm])
            kb.dma("sp", cw[:], cw_d, W=[cw])
            kb.dma("sp", cbias[:], cb_d, W=[cbias])
            kb.dma("sp", dtb[:], dtb_d, W=[dtb])
            kb.dma("sp", arep[:], alog_d, W=[arep])
            kb.dma("sp", dsk[:], dsk_d, W=[dsk])
            kb.act(arep[:], arep[:], AF.Exp, R=[arep], W=[arep])
            kb.ts("dve", arep[:], arep[:], -1.0, None, ALU.mult, None, R=[arep], W=[arep])
            kb.tt("dve", DI[:], identf[:].unsqueeze(1).to_broadcast([128, NH, 128]),
                  dsk[:].unsqueeze(2).to_broadcast([128, NH, 128]), ALU.mult, R=[identf, dsk], W=[DI])

            halo = sb2("halo", [128, 24, 3], F32)
            kb.memset("pool", halo[:], 0.0, W=[halo])
            cob = [sb2(f"cob{i}", [128, 512], BF16) for i in range(2)]
            x_tok = sb2("x_tok", [128, 4, DIN], BF16)
            B_tok = sb2("B_tok", [128, 4, 512], BF16)
            BT = sb2("BT", [128, 4, 512], BF16)
            CT = sb2("CT", [128, 4, 512], BF16)
            dtt = sb2("dtt", [128, 4, NH], F32)
            dta = sb2("dta", [128, 4, NH], F32)
            acs = sb2("acs", [128, 64], F32)
            nacs = sb2("nacs", [128, NH], F32)
            eacs = sb2("eacs", [128, NH], F32)
            cdec = sb2("cdec", [128, NH], F32)
            dte = sb2("dte", [128, NH], F32)
            w1 = sb2("w1", [128, NH], F32)
            xd = sb2("xd", [128, NH, HP], BF16)
            xdw = sb2("xdw", [128, NH, HP], BF16)
            Gm = sb2("Gm", [128, 4, 128], BF16)
            Dc = [sb2(f"Dc{i}", [128, 512], F32) for i in range(2)]
            Eb = [sb2(f"Eb{i}", [128, 1024], BF16) for i in range(2)]
            Mb = [sb2(f"Mb{i}", [128, 8, 128], BF16) for i in range(2)]
            t1b = [sb2(f"t1b{i}", [128, 512], F32) for i in range(2)]
            y32 = [sb2(f"y32{i}", [128, 512], F32) for i in range(2)]
            yg = sb2("yg", [128, DIN], F32)
            ygs = [Buf(yg.t, f"yg_g{i}") for i in range(4)]
            gms = sb2("gms", [128, 4], F32)
            grs = sb2("grs", [128, 4], F32)
            yn = [sb2(f"yn{i}", [128, 1024], BF16) for i in range(1)]
            ynT = sb2("ynT", [128, 16, 512], BF16)
            h32 = sb2("h32", [128, DIN], F32)
            hbf = sb2("hbf", [128, DIN], BF16)
            kv32 = [t1b[0], y32[0]]
            kro = cacc
            rtab = [sb2(f"rtab{i}", [128, 64], F32) for i in range(4)]
            kb.memset("pool", h32[:], 0.0, W=[h32])
            kb.memset("pool", hbf[:], 0.0, W=[hbf])
            zs = hT.t[:].rearrange("p k t -> p (k t)")[:, 0:8192].rearrange("p (s c) -> p s c", s=4)
            print("phase2 sbuf remaining", nc.sbuf_bytes_remaining)

            for j in range(NTILES):
                for s4 in range(4):
                    kb.dma("sp", xt[:, s4, :], xp[j * 512 + s4 * 128:j * 512 + (s4 + 1) * 128, :], W=[xs_[s4]])
                norm_T(0)
                for b in range(4):
                    wv = wnext("z", 8, 512)
                    for s_ in range(4):
                        p = proj_tok(xnT, 8, wv, 512, s_)
                        kb.act(zs[:, s_, b * 512:(b + 1) * 512], p.t[:, 0:512], AF.Silu, R=[p], W=[hT])
                conv_wv = [None]

                def conv_front(ci):
                    if ci % 4 == 0:
                        conv_wv[0] = wnext("x", 8, 512)
                    p = proj_feat(xnT, 8, conv_wv[0], (ci % 4) * 128, 512)
                    st_ = cst[ci % 2]
                    ca = cacc[ci % 2]
                    kb.cp("act", st_[:, 0:3], halo[:, ci, :], R=[halo], W=[st_])
                    kb.cp("act", st_[:, 3:515], p.t[:, 0:512], R=[p], W=[st_])
                    kb.cp("act", halo[:, ci, :], st_[:, 512:515], R=[st_], W=[halo])
                    kb.act(ca[:], p.t[:, 0:512], AF.Identity, R=[p, cw, cbias], W=[ca],
                           scale=cw[:, ci, 3:4], bias=cbias[:, ci:ci + 1])
                    for tap in (2, 1, 0):
                        kb.stt("dve", ca[:], st_[:, tap:tap + 512], cw[:, ci, tap:tap + 1], ca[:],
                               ALU.mult, ALU.add, R=[st_, cw, ca], W=[ca])

                def conv_back(ci):
                    ca = cacc[ci % 2]
                    if ci < 16:
                        co = cob[ci % 2]
                        kb.act(co[:], ca[:], AF.Silu, R=[ca], W=[co])
                        pt = kb.ps()
                        pbv = pt.t[:].bitcast(BF16)
                        for s_ in range(4):
                            kb.tr(pbv[:, s_ * 128:(s_ + 1) * 128], co[:, s_ * 128:(s_ + 1) * 128], identb[:],
                                  R=[co, identb], W=[pt])
                        kb.cp("dve", x_tok[:, :, ci * 128:(ci + 1) * 128],
                              pbv[:, 0:512].rearrange("p (s c) -> p s c", s=4), R=[pt], W=[x_tok])
                    elif ci < 20:
                        g = ci - 16
                        kb.act(BT[:, g, :], ca[:], AF.Silu, R=[ca], W=[BT])
                        pt = kb.ps()
                        pbv = pt.t[:].bitcast(BF16)
                        for s_ in range(4):
                            kb.tr(pbv[:, s_ * 128:(s_ + 1) * 128], BT[:, g, s_ * 128:(s_ + 1) * 128], identb[:],
                                  R=[BT, identb], W=[pt])
                        kb.cp("dve", B_tok[:, :, g * 128:(g + 1) * 128],
                              pbv[:, 0:512].rearrange("p (s c) -> p s c", s=4), R=[pt], W=[B_tok])
                    else:
                        g = ci - 20
                        kb.act(CT[:, g, :], ca[:], AF.Silu, R=[ca], W=[CT])
                conv_front(0)
                for ci in range(24):
                    if ci + 1 < 24:
                        conv_front(ci + 1)
                    conv_back(ci)
                wv = wnext("dt", 8, 32)
                p = kb.ps()
                for s_ in range(4):
                    proj_tok(xnT, 8, wv, 32, s_, p=p, c0=s_ * 32)
                kb.tt("dve", dtt[:], p.t[:, 0:128].rearrange("p (s h) -> p s h", s=4),
                      dtb[:].unsqueeze(1).to_broadcast([128, 4, NH]), ALU.add, R=[p, dtb], W=[dtt])
                kb.act(dtt[:], dtt[:], AF.Exp, R=[dtt], W=[dtt])
                kb.act(dtt[:], dtt[:], AF.Ln, R=[dtt], W=[dtt], bias=1.0)
                kb.tt("dve", dta[:], dtt[:], arep[:].unsqueeze(1).to_broadcast([128, 4, NH]), ALU.mult,
                      R=[dtt, arep], W=[dta])
                if j == NTILES - 1:
                    for b in range(6):
                        wv = wnext("xc", 8, 512)
                        p = kb.ps()
                        for kc in range(8):
                            kb.mm(p.t[0:3, 0:512], xnT[:, kc, 509:512], wv[:, kc, 0:512], kc == 0, kc == 7,
                                  R=[xnT, wv_buf[0]], W=[p])
                        c3 = cacc[b % 2]
                        kb.cp("act", c3[0:3, :], p.t[0:3, 0:512], R=[p], W=[c3])
                        kb.dma("sp", convp[:, b * 512:(b + 1) * 512], c3[0:3, :], R=[c3])
                for s_ in range(4):
                    gi = j * 4 + s_
                    tsl = slice(s_ * 128, (s_ + 1) * 128)
                    p1 = kb.ps()
                    kb.mm(p1.t[:, 0:32], tri[:], dta[:, s_, :], True, True, R=[tri, dta], W=[p1])
                    kb.mm(p1.t[:, 32:64], ones[:], dta[:, s_, :], True, True, R=[ones, dta], W=[p1])
                    kb.cp("dve", acs[:], p1.t[:, 0:64], R=[p1], W=[acs])
                    kb.ts("dve", nacs[:], acs[:, 0:32], -1.0, None, ALU.mult, None, R=[acs], W=[nacs])
                    kb.act(eacs[:], acs[:, 0:32], AF.Exp, R=[acs], W=[eacs])
                    kb.act(cdec[:], acs[:, 32:64], AF.Exp, R=[acs], W=[cdec])
                    kb.tt("dve", dte[:], acs[:, 32:64], acs[:, 0:32], ALU.subtract, R=[acs], W=[dte])
                    kb.act(dte[:], dte[:], AF.Exp, R=[dte], W=[dte])
                    kb.tt("dve", w1[:], dte[:], dtt[:, s_, :], ALU.mult, R=[dte, dtt], W=[w1])
                    xv = x_tok[:, s_, :].rearrange("p (h d) -> p h d", d=HP)
                    kb.tt("dve", xd[:], xv, dtt[:, s_, :].unsqueeze(2).to_broadcast([128, NH, HP]), ALU.mult,
                          R=[x_tok, dtt], W=[xd])
                    kb.tt("pool", xdw[:], xv, w1[:].unsqueeze(2).to_broadcast([128, NH, HP]), ALU.mult,
                          R=[x_tok, w1], W=[xdw])
                    pG = kb.ps()
                    for g in range(4):
                        kb.mm(pG.t[:, g * 128:(g + 1) * 128], BT[:, g, tsl], CT[:, g, tsl], True, True,
                              R=[BT, CT], W=[pG])
                    kb.tt("dve", Gm[:], pG.t[:, 0:512].rearrange("p (g l) -> p g l", g=4),
                          tri[:].unsqueeze(1).to_broadcast([128, 4, 128]), ALU.mult, R=[pG, tri], W=[Gm])
                    def ssd_front(g):
                        ebb, mbb = Eb[g % 2], Mb[g % 2]
                        for half in range(2):
                            dcb = Dc[half]
                            pS = kb.ps()
                            for hh in range(4):
                                h = g * 8 + half * 4 + hh
                                kb.mm(pS.t[:, hh * 128:(hh + 1) * 128], dta[:, s_, h:h + 1].to_broadcast([128, 128]),
                                      tri[:], True, False, R=[dta, tri], W=[pS])
                                kb.mm(pS.t[:, hh * 128:(hh + 1) * 128], identb[:], mneg[:], False, True,
                                      R=[identb, mneg], W=[pS])
                            h0 = g * 8 + half * 4
                            kb.tt("dve", dcb[:].rearrange("p (h l) -> p h l", h=4),
                                  pS.t[:, 0:512].rearrange("p (h l) -> p h l", h=4),
                                  nacs[:, h0:h0 + 4].unsqueeze(2).to_broadcast([128, 4, 128]), ALU.add,
                                  R=[pS, nacs], W=[dcb])
                            kb.act(ebb[:, half * 512:(half + 1) * 512], dcb[:], AF.Exp, R=[dcb], W=[ebb])
                        kb.tt("dve", mbb[:], ebb[:].rearrange("p (h l) -> p h l", h=8),
                              Gm[:, g, :].unsqueeze(1).to_broadcast([128, 8, 128]), ALU.mult, R=[ebb, Gm], W=[mbb])

                    def ssd_back(g):
                        mbb = Mb[g % 2]
                        pO = kb.ps()
                        kb.mm(pO.t[:, 0:512], CT[:, g, tsl], hbf[:, g * 512:(g + 1) * 512], True, True,
                              R=[CT, hbf], W=[pO])
                        pY = kb.ps()
                        for hh in range(8):
                            h = g * 8 + hh
                            kb.mm(pY.t[:, hh * 64:(hh + 1) * 64], mbb[:, hh, :], xd[:, h, :], True, False,
                                  R=[mbb, xd], W=[pY])
                            kb.mm(pY.t[:, hh * 64:(hh + 1) * 64], DI[:, h, :], x_tok[:, s_, h * 64:(h + 1) * 64],
                                  False, True, R=[DI, x_tok], W=[pY])
                        t1 = t1b[g % 2]
                        yy = y32[g % 2]
                        kb.tt("dve", t1[:].rearrange("p (h d) -> p h d", d=HP),
                              pO.t[:, 0:512].rearrange("p (h d) -> p h d", d=HP),
                              eacs[:, g * 8:(g + 1) * 8].unsqueeze(2).to_broadcast([128, 8, HP]), ALU.mult,
                              R=[pO, eacs], W=[t1])
                        kb.tt("dve", yy[:], pY.t[:, 0:512], t1[:], ALU.add, R=[pY, t1], W=[yy])
                        kb.tt("dve", yg[:, g * 512:(g + 1) * 512], yy[:], zs[:, s_, g * 512:(g + 1) * 512], ALU.mult,
                              R=[yy, hT], W=[ygs[g]])
                        kb.act(junk[:, 0:512], yg[:, g * 512:(g + 1) * 512], AF.Square, R=[ygs[g]], W=[junk, gms],
                               scale=float(512 ** -0.5), accum=gms[:, g:g + 1])
                    ssd_front(0)
                    for g in range(4):
                        if g + 1 < 4:
                            ssd_front(g + 1)
                        ssd_back(g)
                    kb.act(grs[:], gms[:], AF.Ln, R=[gms], W=[grs], bias=EPS)
                    kb.act(grs[:], grs[:], AF.Exp, R=[grs], W=[grs], scale=-0.5)
                    ynb = yn[0]
                    for half in range(2):
                        for g2 in range(2):
                            g = half * 2 + g2
                            kb.stt("dve", ynb[:, g2 * 512:(g2 + 1) * 512], yg[:, g * 512:(g + 1) * 512], grs[:, g:g + 1],
                                   gssm[:, g * 512:(g + 1) * 512], ALU.mult, ALU.mult, R=[ygs[g], grs, gssm], W=[ynb])
                        pt = kb.ps()
                        pbv = pt.t[:].bitcast(BF16)
                        for k8 in range(8):
                            kb.tr(pbv[:, k8 * 128:(k8 + 1) * 128], ynb[:, k8 * 128:(k8 + 1) * 128], identb[:],
                                  R=[ynb, identb], W=[pt])
                        kb.cp("act", ynT[:, half * 8:(half + 1) * 8, tsl],
                              pbv[:, 0:1024].rearrange("p (k t) -> p k t", k=8), R=[pt], W=[ynT])
                    for g in range(4):
                        pT_ = kb.ps()
                        kb.mm(pT_.t[:, 0:512], B_tok[:, s_, g * 128:(g + 1) * 128],
                              xdw[:, g * 8:(g + 1) * 8, :], True, True, R=[B_tok, xdw], W=[pT_])
                        hv = h32[:, g * 512:(g + 1) * 512].rearrange("p (h d) -> p h d", d=HP)
                        if g == 0:
                            hall = h32[:].rearrange("p (h d) -> p h d", d=HP)
                            kb.tt("dve", hall, hall, cdec[:].unsqueeze(2).to_broadcast([128, NH, HP]),
                                  ALU.mult, R=[h32, cdec], W=[h32])
                        kb.tt("dve", hv, hv, pT_.t[:, 0:512].rearrange("p (h d) -> p h d", d=HP), ALU.add,
                              R=[h32, pT_], W=[h32])
                    kb.cp("act", hbf[:], h32[:], R=[h32], W=[hbf])
                if j == NTILES - 1:
                    for q4 in range(4):
                        pt = kb.ps()
                        for k4 in range(4):
                            k = q4 * 4 + k4
                            kb.tr(pt.t[:, k4 * 128:(k4 + 1) * 128], h32[:, k * 128:(k + 1) * 128], identf[:],
                                  R=[h32, identf], W=[pt])
                        kb.cp("act", yg[:, q4 * 512:(q4 + 1) * 512], pt.t[:, 0:512], R=[pt], W=[ygs[q4]])
                    kb.dma("sp", ssmp.rearrange("(k p) n -> p k n", p=128),
                           yg[:].rearrange("p (k n) -> p k n", k=16), R=ygs)
                for hf in range(2):
                    pp = [kb.ps() for _ in range(4)]
                    for kbk in range(2):
                        wv = wnext("out", 8, 512)
                        for s_ in range(4):
                            for kc in range(8):
                                kcg = kbk * 8 + kc
                                kb.mm(pp[s_].t[:, 0:512], ynT[:, kcg, s_ * 128:(s_ + 1) * 128], wv[:, kc, :],
                                      kcg == 0, kcg == 15, R=[ynT, wv_buf[0]], W=[pp[s_]])
                    for s_ in range(4):
                        kb.tt("dve", xt[:, s_, hf * 512:(hf + 1) * 512], xt[:, s_, hf * 512:(hf + 1) * 512],
                              pp[s_].t[:, 0:512], ALU.add, R=[xs_[s_], pp[s_]], W=[xs_[s_]])
                ffn(0, 1)
                kb.dma("sp", x2d_t[j * 512:(j + 1) * 512, :].rearrange("(s p) d -> p s d", p=128), xt[:],
                       R=xs_, W=[x2B])
                norm_T(2)
                wv = wnext("kv", 8, 512)
                for s_ in range(4):
                    kb.dma("sp", rtab[s_][:], c_rope[(j * 4 + s_) * 128:(j * 4 + s_ + 1) * 128, :], W=[rtab[s_]])
                for s_ in range(4):
                    gi = j * 4 + s_
                    p = proj_tok(xnT, 8, wv, 512, s_)
                    kvb, krb, tb = kv32[s_ % 2], kro[s_ % 2], rtab[s_]
                    kb.cp("act", kvb[:], p.t[:, 0:512], R=[p], W=[kvb])
                    rope(krb, krb[:, 0:256], kvb, kvb[:, 0:256], 4, tb)
                    kb.dma("sp", kp[gi * 128:(gi + 1) * 128, :], krb[:, 0:256], R=[krb], W=[kpBs[gi]])
                    kb.dma("sp", vp[gi * 128:(gi + 1) * 128, :], kvb[:, 256:512], R=[kvb], W=[vpBs[gi]])
            S.barrier()
        S.barrier()

        KR = 64 + NBP
        es3 = ExitStack()
        with es3:
            def sb3(name, shape, dt):
                return Buf(es3.enter_context(nc.sbuf_tensor("s3_" + name, list(shape), dt)), name)
            gfin = sb3("gfin", [128, D], F32)
            kb.dma("sp", gfin[:], g_all[5], W=[gfin])
            KT = sb3("KT", [128, 4, NT], BF16)
            Vaug = sb3("Vaug", [128, NSUB, 4, 128], BF16)
            rd = sb3("rd", [64, 512], F32)
            kmT = sb3("kmT", [64, 4, NBP], F32)
            gmask = sb3("gmask", [128, NSTEP * 3 * NBP], F32)
            x2idx = sb3("x2idx", [128, NSTEP], mybir.dt.uint32)
            mk = [sb3(f"mk{i}", [128, 2 * 512], BF16) for i in range(2)]
            trimask = sb3("trimask", [128, 512], BF16)
            kin2 = [sb3(f"kin2_{i}", [128, 2, 512], F32) for i in range(2)]
            kbb = [sb3(f"kbb{i}", [128, 256], BF16) for i in range(2)]
            qrs = [sb3(f"qr{i}", [128, D], F32) for i in range(2)]
            Qaug = sb3("Qaug", [128, NQH, KR], BF16)
            QT32 = sb3("QT32", [64, 4, 128], F32)
            QTaug = sb3("QTaug", [128, NQH, 128], BF16)
            gm = sb3("gm", [128, NQH, NBP], F32)
            m8 = sb3("m8", [128, NQH, 8], F32)
            sel = sb3("sel", [128, NQH, NBP], F32)
            PT = [sb3(f"PT{i}", [128, 512], BF16) for i in range(3)]
            AT = sb3("AT", [128, 8, 512], BF16)
            rtq = [sb3(f"rtq{i}", [128, 64], F32) for i in range(2)]
            print("phase3 sbuf remaining", nc.sbuf_bytes_remaining)
            kb.dma("sp", gmask[:], c_gmask, W=[gmask])
            kb.dma("sp", trimask[:], c_trimask, W=[trimask])
            kb.dma("sp", x2idx[:], c_x2idx, W=[x2idx])
            kb.memset("pool", kmT[:], 0.0, W=[kmT])
            kb.memset("pool", KT[:], 0.0, W=[KT])
            kb.memset("pool", QTaug[:], 0.0, W=[QTaug])
            kb.memset("pool", Vaug[:], 1.0, W=[Vaug])
            kb.NROT = 4
            for kv in range(4):
                kb.dma("sp", KT[64:64 + NBP, kv, :], c_onehot, W=[KT])
            pi_ = 0
            for n in range(NB):
                kin = kin2[n % 2]
                kb.dma("sp", kin[:, :, 0:256], kp[n * 256:(n + 1) * 256, :].rearrange("(s p) d -> p s d", p=128),
                       R=[kpBs[2 * n], kpBs[2 * n + 1]], W=[kin])
                kb.dma("sp", kin[:, :, 256:512], vp[n * 256:(n + 1) * 256, :].rearrange("(s p) d -> p s d", p=128),
                       R=[vpBs[2 * n], vpBs[2 * n + 1]], W=[kin])
                pm = kb.ps()
                for h in range(4):
                    for s2 in range(2):
                        kb.mm(pm.t[0:64, h:h + 1], kin[:, s2, h * 64:(h + 1) * 64], ones[:, 0:1], s2 == 0, s2 == 1,
                              R=[kin, ones], W=[pm])
                kb.act(kmT[:, :, n], pm.t[0:64, 0:4], AF.Copy, R=[pm], W=[kmT], scale=1.0 / 256.0)
                for s2 in range(2):
                    kt = n * 2 + s2
                    kb_ = kbb[kt % 2]
                    kb.cp("dve", kb_[:], kin[:, s2, 0:256], R=[kin], W=[kb_])
                    pt = kb.ps()
                    pbv = pt.t[:].bitcast(BF16)
                    for h in range(4):
                        kb.tr(pbv[0:64, h * 128:(h + 1) * 128], kb_[:, h * 64:(h + 1) * 64], identb[:],
                              R=[kb_, identb], W=[pt])
                    kb.cp("act", KT[0:64, :, kt * 128:(kt + 1) * 128],
                          pbv[0:64, 0:512].rearrange("p (h t) -> p h t", h=4), R=[pt], W=[KT])
                    kb.cp("dve", Vaug[:, kt, :, 0:64], kin[:, s2, 256:512].rearrange("p (h d) -> p h d", h=4),
                          R=[kin], W=[Vaug])

            for j in range(NTILES // 2):
                for s_ in range(4):
                    S.dma("pool", (lambda o_, c_: (lambda e: e.indirect_dma_start(
                        out=o_, out_offset=None, in_=x2d_t,
                        in_offset=bass.IndirectOffsetOnAxis(ap=x2idx[:, c_:c_ + 1], axis=0))))(xt[:, s_, :], j * 4 + s_),
                        R=[x2B, x2idx], W=[xs_[s_]])
                norm_T(3)
                wq = []
                for b in range(2):
                    wv = wnext("q", 8, 512)
                    wq.append((wv, wv_buf[0]))
                def q_front(s_):
                    gi_ = j * 4 + s_
                    tb = rtq[s_ % 2]
                    kb.dma("sp", tb[:], c_ropeq[gi_ * 128:(gi_ + 1) * 128, :], W=[tb])
                    qraw = kin2[s_ % 2]
                    for b in range(2):
                        wv_buf[0] = wq[b][1]
                        p = proj_tok(xnT, 8, wq[b][0], 512, s_)
                        kb.cp("act", qraw[:, b, :], p.t[:, 0:512], R=[p], W=[qraw])
                    qr_ = qrs[s_ % 2]
                    rope(qr_, qr_[:], qraw, qraw[:].rearrange("p b c -> p (b c)"), NQH, tb)
                    return qr_
                qf = q_front(0)
                for s_ in range(4):
                    gi = j * 4 + s_
                    tsl = slice(s_ * 128, (s_ + 1) * 128)
                    qr = qf
                    kb.cp("act", Qaug[:, :, 0:64], qr[:].rearrange("p (h d) -> p h d", d=64), R=[qr], W=[Qaug])
                    pg = kb.psum[4]
                    for q4 in range(4):
                        pt = kb.ps()
                        for k4 in range(4):
                            h = q4 * 4 + k4
                            kb.tr(pt.t[0:64, k4 * 128:(k4 + 1) * 128], qr[:, h * 64:(h + 1) * 64], identf[:],
                                  R=[qr, identf], W=[pt])
                        kb.cp("dve", QT32[:], pt.t[0:64, 0:512].rearrange("p (h t) -> p h t", h=4), R=[pt], W=[QT32])
                        for k4 in range(4):
                            h = q4 * 4 + k4
                            kb.mm(pg.t[:, h * NBP:(h + 1) * NBP], QT32[:, k4, :], kmT[:, h // 4, :], True, True,
                                  R=[QT32, kmT], W=[pg])
                    mo = gi * 3 * NBP
                    kb.tt("dve", gm[:], pg.t[:, 0:NQH * NBP].rearrange("p (h n) -> p h n", h=NQH),
                          gmask[:, mo:mo + NBP].unsqueeze(1).to_broadcast([128, NQH, NBP]), ALU.add,
                          R=[pg, gmask], W=[gm])
                    for h in range(NQH):
                        S.op("dve", (lambda h_: (lambda e: e.max(m8[:, h_, :], gm[:, h_, :])))(h), R=[gm], W=[m8])
                    kb.tt("dve", sel[:], gm[:], m8[:, :, NSEL - 1:NSEL].to_broadcast([128, NQH, NBP]), ALU.is_ge,
                          R=[gm, m8], W=[sel])
                    kb.tt("dve", sel[:], sel[:], gmask[:, mo + NBP:mo + 2 * NBP].unsqueeze(1).to_broadcast([128, NQH, NBP]),
                          ALU.mult, R=[sel, gmask], W=[sel])
                    kb.tt("dve", sel[:], sel[:], gmask[:, mo + 2 * NBP:mo + 3 * NBP].unsqueeze(1).to_broadcast([128, NQH, NBP]),
                          ALU.add, R=[sel, gmask], W=[sel])
                    kb.ts("dve", Qaug[:, :, 64:KR], sel[:], -1.0, -NEG, ALU.add, ALU.mult, R=[sel], W=[Qaug])
                    for kv in range(4):
                        pt = kb.ps()
                        pbv = pt.t[:].bitcast(BF16)
                        for g in range(4):
                            h = kv * 4 + g
                            kb.tr(pbv[0:KR, g * 128:(g + 1) * 128], Qaug[:, h, :], identb[:], R=[Qaug, identb], W=[pt])
                        kb.cp("act", QTaug[0:KR, kv * 4:(kv + 1) * 4, :],
                              pbv[0:KR, 0:512].rearrange("p (g t) -> p g t", g=4), R=[pt], W=[QTaug])
                    if s_ + 1 < 4:
                        qf = q_front(s_ + 1)
                    nkt = 2 * gi + 2
                    mkb = mk[gi % 2]
                    if gi == 0:
                        kb.dma("sp", mkb[:], c_masks[0], W=[mkb])
                    if gi + 1 < NSTEP:
                        kb.dma("sp", mk[(gi + 1) % 2][:], c_masks[gi + 1], W=[mk[(gi + 1) % 2]])
                    for kv in range(4):
                        pO = kb.psum[6 + (kv % 2)]
                        rq = QTaug[:, kv * 4:(kv + 1) * 4, :]

                        def s_mm(kt):
                            pS_ = kb.ps()
                            msk = kt >= 2 * gi
                            kb.mm(pS_.t[:, 0:512], KT[:, kv, kt * 128:(kt + 1) * 128], rq, True, not msk,
                                  R=[KT, QTaug], W=[pS_])
                            if msk:
                                m_ = kt - 2 * gi
                                kb.mm(pS_.t[:, 0:512], identb[:], mkb[:, m_ * 512:(m_ + 1) * 512], False, True,
                                      R=[identb, mkb], W=[pS_])
                            return pS_
                        pS_cur = s_mm(0)
                        for kt in range(nkt):
                            pS_next = s_mm(kt + 1) if kt + 1 < nkt else None
                            ptb = PT[pi_ % 3]
                            pi_ += 1
                            kb.act(ptb[:], pS_cur.t[:, 0:512], AF.Exp, R=[pS_cur], W=[ptb])
                            kb.mm(pO.t[:, 0:512], Vaug[:, kt, kv, :], ptb[:], kt == 0, kt == nkt - 1,
                                  R=[Vaug, ptb], W=[pO])
                            pS_cur = pS_next
                        S.op("dve", lambda e, pO=pO: e.reciprocal(rd[:], pO.t[64:128, 0:512]), R=[pO], W=[rd])
                        pov = pO.t[0:64, 0:512].rearrange("p (gp two t) -> p gp two t", two=2, t=128)
                        rdv = rd[:].rearrange("p (gp two t) -> p gp two t", two=2, t=128)
                        for two in range(2):
                            kb.tt("dve", AT[two * 64:(two + 1) * 64, kv * 2:kv * 2 + 2, tsl], pov[:, :, two, :],
                                  rdv[:, :, two, :], ALU.mult, R=[pO, rd], W=[AT])
                if DEBUG:
                    kb.dma("sp", dbg_at[j], AT[:], R=[AT])
                for b in range(2):
                    wv = wnext("o", 8, 512)
                    for s_ in range(4):
                        p = proj_tok(AT, 8, wv, 512, s_)
                        kb.tt("dve", xt[:, s_, b * 512:(b + 1) * 512], xt[:, s_, b * 512:(b + 1) * 512],
                              p.t[:, 0:512], ALU.add, R=[xs_[s_], p], W=[xs_[s_]])
                ffn(1, 4)
                for s_ in range(4):
                    kb.act(xnb[s_ % 2][:], xt[:, s_, :], AF.Square, R=[xs_[s_]], W=[xnb[s_ % 2], ms], scale=1.0 / 32.0,
                           accum=ms[:, s_:s_ + 1])
                kb.act(rs[:, 0:4], ms[:, 0:4], AF.Ln, R=[ms], W=[rs], bias=EPS)
                kb.act(rs[:, 0:4], rs[:, 0:4], AF.Exp, R=[rs], W=[rs], scale=-0.5)
                for s_ in range(4):
                    gi = j * 4 + s_
                    yb = kin2[s_ % 2]
                    ybv = yb[:].rearrange("p b c -> p (b c)")
                    kb.stt("dve", ybv, xt[:, s_, :], rs[:, s_:s_ + 1], gfin[:], ALU.mult, ALU.mult,
                           R=[xs_[s_], rs, gfin], W=[yb])
                    kb.dma("sp", yp[gi * 128:(gi + 1) * 128, :], ybv, R=[yb])
            S.barrier()
        esP.close()
        if with_sample:
            esS = ExitStack()

            def sbS(name, shape, dt):
                return Buf(esS.enter_context(nc.sbuf_tensor("sS_" + name, list(shape), dt)), name)
            xt = sbS("xt", [16, 1, D], F32)
            xs_ = [xt]
            xnb = [sbS("xnb0", [16, D], BF16)]
            xnT = sbS("xnT", [128, 8, 16], BF16)
            junk = sbS("junk", [16, D], BF16)
            ms = sbS("ms", [16, 8], F32)
            rs = sbS("rs", [16, 8], F32)
            hT = sbS("hT", [128, 22, 16], BF16)
            cst = [sbS(f"cst{i}", [16, 515], F32) for i in range(2)]
            cacc = [sbS(f"cacc{i}", [128, 16], F32) for i in range(2)]
            NPG = NPAST // 128
            NPB = NPAST // 256
            KRS = 64 + NBPS
            NKS = NPAST + 128
            U32 = mybir.dt.uint32
            kb.NROT = 4
            ksB = Buf(ks, "ks_dram")
            vsB = Buf(vs, "vs_dram")
            esA = ExitStack()
            with esA:
                def sbA(name, shape, dt):
                    return Buf(esA.enter_context(nc.sbuf_tensor("sA_" + name, list(shape), dt)), name)
                gssm = sbA("gssm", [16, DIN], F32)
                cw = sbA("cw", [128, 24, 4], F32)
                cbias = sbA("cbias", [128, 24], F32)
                dtb = sbA("dtb", [16, NH], F32)
                arep = sbA("arep", [16, NH], F32)
                dsk = sbA("dsk", [16, NH], F32)
                i16 = sbA("i16", [128, 256], F32)
                selm = sbA("selm", [16, 16 * 128], F32)
                kb.dma("sp", gssm[:], g_ssm[0:16, :], W=[gssm])
                kb.dma("sp", cw[:], cw_d, W=[cw])
                kb.dma("sp", cbias[:], cb_d, W=[cbias])
                kb.dma("sp", dtb[:], dtb_d[0:16, :], W=[dtb])
                kb.dma("sp", arep[:], alog_d[0:16, :], W=[arep])
                kb.dma("sp", dsk[:], dsk_d[0:16, :], W=[dsk])
                kb.dma("sp", i16[:], c_i16, W=[i16])
                kb.dma("sp", selm[:], c_sel, W=[selm])
                kb.act(arep[:], arep[:], AF.Exp, R=[arep], W=[arep])
                kb.ts("dve", arep[:], arep[:], -1.0, None, ALU.mult, None, R=[arep], W=[arep])
                zs_s = sbA("zs_s", [16, DIN], F32)
                sconv_sb = sbA("sconv_sb", [12, CD], F32)
                c16 = [sbA(f"c16_{i}", [16, 512], F32) for i in range(2)]
                stg = [sbA(f"stg{i}", [128, 4, 7], F32) for i in range(2)]
                ca4 = [sbA(f"ca4_{i}", [128, 4, 4], F32) for i in range(2)]
                xf = [sbA(f"xf{i}", [128, 16], F32) for i in range(2)]
                x_tok_s = sbA("x_tok_s", [16, DIN], F32)
                xdt_tok = sbA("xdt_tok", [16, DIN], F32)
                BT_s = sbA("BT_s", [128, 4, 16], F32)
                CT_s = sbA("CT_s", [128, 4, 16], F32)
                CTm = sbA("CTm", [128, 4, 16, 16], F32)
                dtt_s = sbA("dtt_s", [16, NH], F32)
                dA = sbA("dA", [16, NH], F32)
                decb = sbA("decb", [128, NH], F32)
                hs_nat = sbA("hs_nat", [128, 16, 128], F32)
                h32_s = sbA("h32_s", [128, DIN], F32)
                y_s = sbA("y_s", [16, DIN], F32)
                yn_s = sbA("yn_s", [16, DIN], BF16)
                gms_s = sbA("gms_s", [16, 4], F32)
                kvs = sbA("kvs", [16, 512], F32)
                krs = sbA("krs", [16, 256], F32)
                rts = sbA("rts", [16, 64], F32)
                print("sampleA sbuf remaining", nc.sbuf_bytes_remaining)
                kb.dma("sp", xt[0:16, 0, :], xs, W=[xs_[0]])
                kb.dma("sp", sconv_sb[:], sconv, W=[sconv_sb])
                norm_T(0, 1, 16)
                for b in range(4):
                    wv = wnext("z", 8, 512)
                    p = proj_tok(xnT, 8, wv, 512, 0, 16)
                    kb.act(zs_s[:, b * 512:(b + 1) * 512], p.t[0:16, 0:512], AF.Silu, R=[p], W=[zs_s])
                for b in range(6):
                    wv = wnext("x", 8, 512)
                    p = proj_tok(xnT, 8, wv, 512, 0, 16)
                    cc = c16[b % 2]
                    kb.cp("act", cc[:], p.t[0:16, 0:512], R=[p], W=[cc])
                    for sq in range(4):
                        kb.dma("sp", convs[sq * 3:(sq + 1) * 3, b * 512:(b + 1) * 512], cc[sq * 4 + 1:sq * 4 + 4, :], R=[cc])
                    for ct in range(4):
                        ci = b * 4 + ct
                        p = proj_feat(xnT, 8, wv, ct * 128, 16)
                        pst = kb.ps()
                        kb.tr(pst.t[:, 0:12], sconv_sb[0:12, ci * 128:(ci + 1) * 128], identf[0:12, 0:12],
                              R=[sconv_sb, identf], W=[pst])
                        st_ = stg[ci % 2]
                        ca = ca4[ci % 2]
                        kb.cp("dve", st_[:, :, 0:3], pst.t[:, 0:12].rearrange("p (s r) -> p s r", s=4), R=[pst], W=[st_])
                        kb.cp("act", st_[:, :, 3:7], p.t[:, 0:16].rearrange("p (s r) -> p s r", s=4), R=[p], W=[st_])
                        kb.ts("dve", ca[:], st_[:, :, 3:7], cw[:, ci, 3:4], cbias[:, ci:ci + 1], ALU.mult, ALU.add,
                              R=[st_, cw, cbias], W=[ca])
                        for tap in (2, 1, 0):
                            kb.stt("dve", ca[:], st_[:, :, tap:tap + 4], cw[:, ci, tap:tap + 1], ca[:],
                                   ALU.mult, ALU.add, R=[st_, cw, ca], W=[ca])
                        cav = ca[:].rearrange("p s r -> p (s r)")
                        if ci < 16:
                            xf_ = xf[ci % 2]
                            kb.act(xf_[:], cav, AF.Silu, R=[ca], W=[xf_])
                            pt = kb.ps()
                            kb.tr(pt.t[0:16, 0:128], xf_[:], identf[:], R=[xf_, identf], W=[pt])
                            kb.cp("dve", x_tok_s[:, ci * 128:(ci + 1) * 128], pt.t[0:16, 0:128], R=[pt], W=[x_tok_s])
                        elif ci < 20:
                            kb.act(BT_s[:, ci - 16, :], cav, AF.Silu, R=[ca], W=[BT_s])
                        else:
                            kb.act(CT_s[:, ci - 20, :], cav, AF.Silu, R=[ca], W=[CT_s])
                wv = wnext("dt", 8, 32)
                p = proj_tok(xnT, 8, wv, 32, 0, 16)
                kb.tt("dve", dtt_s[:], p.t[0:16, 0:32], dtb[:], ALU.add, R=[p, dtb], W=[dtt_s])
                kb.act(dtt_s[:], dtt_s[:], AF.Exp, R=[dtt_s], W=[dtt_s])
                kb.act(dtt_s[:], dtt_s[:], AF.Ln, R=[dtt_s], W=[dtt_s], bias=1.0)
                kb.tt("dve", dA[:], dtt_s[:], arep[:], ALU.mult, R=[dtt_s, arep], W=[dA])
                kb.tt("dve", xdt_tok[:].rearrange("p (h d) -> p h d", d=HP), x_tok_s[:].rearrange("p (h d) -> p h d", d=HP),
                      dtt_s[:].unsqueeze(2).to_broadcast([16, NH, HP]), ALU.mult, R=[x_tok_s, dtt_s], W=[xdt_tok])
                kb.tt("dve", CTm[:], CT_s[:].unsqueeze(3).to_broadcast([128, 4, 16, 16]),
                      i16[:].rearrange("p (a b) -> p a b", a=16).unsqueeze(1).to_broadcast([128, 4, 16, 16]), ALU.mult,
                      R=[CT_s, i16], W=[CTm])
                pY = [kb.psum[4 + g] for g in range(4)]
                for sq in range(4):
                    kb.dma("sp", hs_nat[:], sssm[sq].rearrange("(k p) n -> p k n", p=128), W=[hs_nat])
                    for q4 in range(4):
                        pt = kb.ps()
                        for k4 in range(4):
                            k = q4 * 4 + k4
                            kb.tr(pt.t[:, k4 * 128:(k4 + 1) * 128], hs_nat[:, k, :], identf[:], R=[hs_nat, identf], W=[pt])
                        kb.cp("act", h32_s[:, q4 * 512:(q4 + 1) * 512], pt.t[:, 0:512], R=[pt], W=[h32_s])
                    for t in range(4):
                        tk = sq * 4 + t
                        sel_tk = selm[:, tk * 128:(tk + 1) * 128]
                        pd = kb.ps()
                        kb.mm(pd.t[:, 0:32], sel_tk, dA[:], True, True, R=[selm, dA], W=[pd])
                        kb.act(decb[:], pd.t[:, 0:32], AF.Exp, R=[pd], W=[decb])
                        for g in range(4):
                            px = kb.ps()
                            kb.mm(px.t[:, 0:512], sel_tk, xdt_tok[:, g * 512:(g + 1) * 512], True, True,
                                  R=[selm, xdt_tok], W=[px])
                            hg = h32_s[:, g * 512:(g + 1) * 512]
                            hv = hg.rearrange("p (h d) -> p h d", d=HP)
                            kb.tt("dve", hv, hv, decb[:, g * 8:(g + 1) * 8].unsqueeze(2).to_broadcast([128, 8, HP]),
                                  ALU.mult, R=[h32_s, decb], W=[h32_s])
                            kb.stt("dve", hg, px.t[:, 0:512], BT_s[:, g, tk:tk + 1], hg, ALU.mult, ALU.add,
                                   R=[px, BT_s, h32_s], W=[h32_s])
                            kb.mm(pY[g].t[0:16, 0:512], CTm[:, g, tk, :], hg, tk == 0, tk == 15,
                                  R=[CTm, h32_s], W=[pY[g]])
                    for q4 in range(4):
                        pt = kb.ps()
                        for k4 in range(4):
                            k = q4 * 4 + k4
                            kb.tr(pt.t[:, k4 * 128:(k4 + 1) * 128], h32_s[:, k * 128:(k + 1) * 128], identf[:],
                                  R=[h32_s, identf], W=[pt])
                        kb.cp("act", hs_nat[:, q4 * 4:(q4 + 1) * 4, :], pt.t[:, 0:512].rearrange("p (k n) -> p k n", k=4),
                              R=[pt], W=[hs_nat])
                    kb.dma("sp", ssms[sq].rearrange("(k p) n -> p k n", p=128), hs_nat[:], R=[hs_nat])
                for g in range(4):
                    gsl = slice(g * 512, (g + 1) * 512)
                    kb.tt("dve", y_s[:, gsl].rearrange("p (h d) -> p h d", d=HP),
                          x_tok_s[:, gsl].rearrange("p (h d) -> p h d", d=HP),
                          dsk[:, g * 8:(g + 1) * 8].unsqueeze(2).to_broadcast([16, 8, HP]), ALU.mult,
                          R=[x_tok_s, dsk], W=[y_s])
                    kb.tt("dve", y_s[:, gsl], y_s[:, gsl], pY[g].t[0:16, 0:512], ALU.add, R=[y_s, pY[g]], W=[y_s])
                    kb.tt("dve", y_s[:, gsl], y_s[:, gsl], zs_s[:, gsl], ALU.mult, R=[y_s, zs_s], W=[y_s])
                    kb.act(junk[0:16, 0:512], y_s[:, gsl], AF.Square, R=[y_s], W=[junk, gms_s],
                           scale=float(512 ** -0.5), accum=gms_s[:, g:g + 1])
                kb.act(gms_s[:], gms_s[:], AF.Ln, R=[gms_s], W=[gms_s], bias=EPS)
                kb.act(gms_s[:], gms_s[:], AF.Exp, R=[gms_s], W=[gms_s], scale=-0.5)
                for g in range(4):
                    gsl = slice(g * 512, (g + 1) * 512)
                    kb.stt("dve", yn_s[:, gsl], y_s[:, gsl], gms_s[:, g:g + 1], gssm[:, gsl], ALU.mult, ALU.mult,
                           R=[y_s, gms_s, gssm], W=[yn_s])
                ynT_s = hT
                for half in range(2):
                    pt = kb.ps()
                    pbv = pt.t[:].bitcast(BF16)
                    for k8 in range(8):
                        kc = half * 8 + k8
                        kb.tr(pbv[:, k8 * 128:k8 * 128 + 16], yn_s[:, kc * 128:(kc + 1) * 128], identb[0:16, 0:16],
                              R=[yn_s, identb], W=[pt])
                    kb.cp("act", ynT_s[:, half * 8:(half + 1) * 8, 0:16],
                          pbv[:, 0:1024].rearrange("p (k t) -> p k t", k=8)[:, :, 0:16], R=[pt], W=[hT])
                for hf in range(2):
                    p = kb.ps()
                    for kbk in range(2):
                        wv = wnext("out", 8, 512)
                        for kc in range(8):
                            kcg = kbk * 8 + kc
                            kb.mm(p.t[0:16, 0:512], ynT_s[:, kcg, 0:16], wv[:, kc, :], kcg == 0, kcg == 15,
                                  R=[hT, wv_buf[0]], W=[p])
                    kb.tt("dve", xt[0:16, 0, hf * 512:(hf + 1) * 512], xt[0:16, 0, hf * 512:(hf + 1) * 512],
                          p.t[0:16, 0:512], ALU.add, R=[xs_[0], p], W=[xs_[0]])
                ffn(0, 1, 1, 16)
                norm_T(2, 1, 16)
                wv = wnext("kv", 8, 512)
                p = proj_tok(xnT, 8, wv, 512, 0, 16)
                kb.dma("sp", rts[:], c_rope_s, W=[rts])
                kb.cp("act", kvs[:], p.t[0:16, 0:512], R=[p], W=[kvs])
                rope(krs, krs[:], kvs, kvs[:, 0:256], 4, rts)
                kb.dma("sp", ks, krs[:], R=[krs], W=[ksB])
                kb.dma("sp", vs, kvs[:, 256:512], R=[kvs], W=[vsB])
                S.barrier()
            S.barrier()
            esB = ExitStack()
            with esB:
                def sbB(name, shape, dt):
                    return Buf(esB.enter_context(nc.sbuf_tensor("sB_" + name, list(shape), dt)), name)
                gfin_s = sbB("gfin_s", [16, D], F32)
                kb.dma("sp", gfin_s[:], g_all[5, 0:16, :], W=[gfin_s])
                KT_s = sbB("KT_s", [KRS, 4, NKS], BF16)
                Vaug_s = sbB("Vaug_s", [128, NPG + 1, 4, 65], BF16)
                kmT_s = sbB("kmT_s", [64, 4, NBPS], F32)
                gmask_s = sbB("gmask_s", [128, 3 * NBPS], F32)
                trimask_s = sbB("trimask_s", [128, 16], BF16)
                ptab_i = sbB("ptab_i", [128, 4 * NPG], I32)
                pidx = sbB("pidx", [128, 1], F32)
                idxf = sbB("idxf", [128, 4 * NPG], F32)
                idxu = sbB("idxu", [128, 4 * NPG], U32)
                kpg = [[sbB(f"kpg{i}_{j}", [128, 256], F32) for j in range(2)] for i in range(4)]
                vpg = [[sbB(f"vpg{i}_{j}", [128, 256], F32) for j in range(2)] for i in range(4)]
                kbb_s = [sbB(f"kbb_s{i}", [128, 256], BF16) for i in range(2)]
                qraw_s = sbB("qraw_s", [16, D], F32)
                qr_s = sbB("qr_s", [16, D], F32)
                rtq_s = sbB("rtq_s", [16, 64], F32)
                q4 = sbB("q4", [4, D], F32)
                knew = sbB("knew", [4, 512], F32)
                knb = sbB("knb", [4, 256], BF16)
                Qaug_s = sbB("Qaug_s", [4, NQH, KRS], BF16)
                QT32_s = sbB("QT32_s", [64, 4, 4], F32)
                QTaug_s = sbB("QTaug_s", [KRS, NQH, 4], BF16)
                gm_s = sbB("gm_s", [4, NQH, NBPS], F32)
                m8_s = sbB("m8_s", [4, NQH, 8], F32)
                sel_s = sbB("sel_s", [4, NQH, NBPS], F32)
                PT_s = [sbB(f"PT_s{i}", [128, 512], BF16) for i in range(2)]
                on_s = sbB("on_s", [64, 16], F32)
                rden_s = sbB("rden_s", [65, 16], F32)
                AT_s = sbB("AT_s", [128, 8, 16], BF16)
                yo_s = sbB("yo_s", [16, D], F32)
                print("sampleB sbuf remaining", nc.sbuf_bytes_remaining)
                kb.dma("sp", gmask_s[:], c_gmask_s, W=[gmask_s])
                kb.dma("sp", trimask_s[:], c_trimask_s, W=[trimask_s])
                kb.dma("sp", pidx[:], c_pidx, W=[pidx])
                kb.dma("sp", ptab_i[:], ptab.rearrange("b j -> (b j)").partition_broadcast(128), W=[ptab_i])
                kb.ts("dve", idxf[:], ptab_i[:], 128.0, pidx[:, 0:1], ALU.mult, ALU.add, R=[ptab_i, pidx], W=[idxf])
                kb.cp("dve", idxu[:], idxf[:], R=[idxf], W=[idxu])
                kb.memset("pool", kmT_s[:], 0.0, W=[kmT_s])
                kb.memset("pool", Vaug_s[:], 1.0, W=[Vaug_s])
                for kv in range(4):
                    kb.dma("sp", KT_s[64:64 + NBPS, kv, :], c_onehot_s, W=[KT_s])
                norm_T(3, 1, 16)
                for b in range(2):
                    wv = wnext("q", 8, 512)
                    p = proj_tok(xnT, 8, wv, 512, 0, 16)
                    kb.cp("act", qraw_s[:, b * 512:(b + 1) * 512], p.t[0:16, 0:512], R=[p], W=[qraw_s])
                kb.dma("sp", rtq_s[:], c_ropeq_s, W=[rtq_s])
                rope(qr_s, qr_s[:], qraw_s, qraw_s[:], NQH, rtq_s)
                pi2 = 0
                for sq in range(4):
                    for n in range(NPB):
                        kin = kpg[(sq * NPB + n) % 4]
                        vin = vpg[(sq * NPB + n) % 4]
                        for s2 in range(2):
                            col = sq * NPG + n * 2 + s2
                            S.dma("pool", (lambda o_, c_: (lambda e: e.indirect_dma_start(
                                out=o_, out_offset=None, in_=cache_k,
                                in_offset=bass.IndirectOffsetOnAxis(ap=idxu[:, c_:c_ + 1], axis=0))))(kin[s2][:], col),
                                R=[idxu], W=[kin[s2]])
                            S.dma("pool", (lambda o_, c_: (lambda e: e.indirect_dma_start(
                                out=o_, out_offset=None, in_=cache_v,
                                in_offset=bass.IndirectOffsetOnAxis(ap=idxu[:, c_:c_ + 1], axis=0))))(vin[s2][:], col),
                                R=[idxu], W=[vin[s2]])
                        pm = kb.ps()
                        for h in range(4):
                            for s2 in range(2):
                                kb.mm(pm.t[0:64, h:h + 1], kin[s2][:, h * 64:(h + 1) * 64], ones[:, 0:1], s2 == 0, s2 == 1,
                                      R=[kin[s2], ones], W=[pm])
                        kb.act(kmT_s[:, :, n], pm.t[0:64, 0:4], AF.Copy, R=[pm], W=[kmT_s], scale=1.0 / 256.0)
                        for s2 in range(2):
                            kt = n * 2 + s2
                            kb_ = kbb_s[kt % 2]
                            kb.cp("dve", kb_[:], kin[s2][:], R=[kin[s2]], W=[kb_])
                            pt = kb.ps()
                            pbv = pt.t[:].bitcast(BF16)
                            for h in range(4):
                                kb.tr(pbv[0:64, h * 128:(h + 1) * 128], kb_[:, h * 64:(h + 1) * 64], identb[:],
                                      R=[kb_, identb], W=[pt])
                            kb.cp("act", KT_s[0:64, :, kt * 128:(kt + 1) * 128],
                                  pbv[0:64, 0:512].rearrange("p (h t) -> p h t", h=4), R=[pt], W=[KT_s])
                            kb.cp("dve", Vaug_s[:, kt, :, 0:64], vin[s2][:].rearrange("p (h d) -> p h d", h=4),
                                  R=[vin[s2]], W=[Vaug_s])
                    kb.dma("sp", knew[:, 0:256], ks[sq * 4:(sq + 1) * 4, :], R=[ksB], W=[knew])
                    kb.dma("sp", knew[:, 256:512], vs[sq * 4:(sq + 1) * 4, :], R=[vsB], W=[knew])
                    kb.cp("dve", knb[:], knew[:, 0:256], R=[knew], W=[knb])
                    pt = kb.ps()
                    pbv = pt.t[:].bitcast(BF16)
                    for h in range(4):
                        kb.tr(pbv[0:64, h * 128:h * 128 + 4], knb[:, h * 64:(h + 1) * 64], identb[0:4, 0:4],
                              R=[knb, identb], W=[pt])
                    kb.cp("act", KT_s[0:64, :, NPAST:NPAST + 4],
                          pbv[0:64, 0:512].rearrange("p (h t) -> p h t", h=4)[:, :, 0:4], R=[pt], W=[KT_s])
                    kb.cp("dve", Vaug_s[0:4, NPG, :, 0:64], knew[:, 256:512].rearrange("p (h d) -> p h d", h=4),
                          R=[knew], W=[Vaug_s])
                    kb.dma("sp", q4[:], qr_s[sq * 4:(sq + 1) * 4, :], R=[qr_s], W=[q4])
                    kb.cp("act", Qaug_s[:, :, 0:64], q4[:].rearrange("p (h d) -> p h d", d=64), R=[q4], W=[Qaug_s])
                    pgs = [kb.psum[4], kb.psum[5]]
                    for q4i in range(4):
                        pt = kb.ps()
                        for k4 in range(4):
                            h = q4i * 4 + k4
                            kb.tr(pt.t[0:64, k4 * 4:(k4 + 1) * 4], q4[:, h * 64:(h + 1) * 64], identf[0:4, 0:4],
                                  R=[q4, identf], W=[pt])
                        kb.cp("dve", QT32_s[:], pt.t[0:64, 0:16].rearrange("p (h t) -> p h t", h=4), R=[pt], W=[QT32_s])
                        for k4 in range(4):
                            h = q4i * 4 + k4
                            pg = pgs[h // 8]
                            kb.mm(pg.t[0:4, (h % 8) * NBPS:(h % 8 + 1) * NBPS], QT32_s[:, k4, :], kmT_s[:, h // 4, :],
                                  True, True, R=[QT32_s, kmT_s], W=[pg])
                    for hf in range(2):
                        kb.tt("dve", gm_s[:, hf * 8:(hf + 1) * 8, :],
                              pgs[hf].t[0:4, 0:8 * NBPS].rearrange("p (h n) -> p h n", h=8),
                              gmask_s[0:4, 0:NBPS].unsqueeze(1).to_broadcast([4, 8, NBPS]), ALU.add,
                              R=[pgs[hf], gmask_s], W=[gm_s])
                    for h in range(NQH):
                        S.op("dve", (lambda h_: (lambda e: e.max(m8_s[:, h_, :], gm_s[:, h_, :])))(h), R=[gm_s], W=[m8_s])
                    kb.tt("dve", sel_s[:], gm_s[:], m8_s[:, :, TOPK - 1:TOPK].to_broadcast([4, NQH, NBPS]), ALU.is_ge,
                          R=[gm_s, m8_s], W=[sel_s])
                    kb.tt("dve", sel_s[:], sel_s[:], gmask_s[0:4, NBPS:2 * NBPS].unsqueeze(1).to_broadcast([4, NQH, NBPS]),
                          ALU.mult, R=[sel_s, gmask_s], W=[sel_s])
                    kb.tt("dve", sel_s[:], sel_s[:], gmask_s[0:4, 2 * NBPS:3 * NBPS].unsqueeze(1).to_broadcast([4, NQH, NBPS]),
                          ALU.add, R=[sel_s, gmask_s], W=[sel_s])
                    kb.ts("dve", Qaug_s[:, :, 64:KRS], sel_s[:], -1.0, -NEG, ALU.add, ALU.mult, R=[sel_s], W=[Qaug_s])
                    pt = kb.ps()
                    pbv = pt.t[:].bitcast(BF16)
                    for h in range(NQH):
                        kb.tr(pbv[0:KRS, h * 4:(h + 1) * 4], Qaug_s[:, h, :], identb[0:4, 0:4], R=[Qaug_s, identb], W=[pt])
                    kb.cp("act", QTaug_s[:], pbv[0:KRS, 0:64].rearrange("p (h t) -> p h t", h=NQH), R=[pt], W=[QTaug_s])
                    for kv in range(4):
                        pO = kb.psum[6 + (kv % 2)]
                        rq = QTaug_s[:, kv * 4:(kv + 1) * 4, :]
                        kt0 = 0
                        while kt0 < NPG:
                            nk = min(32, NPG - kt0)
                            pS = kb.ps()
                            for k_ in range(nk):
                                kt = kt0 + k_
                                kb.mm(pS.t[:, k_ * 16:(k_ + 1) * 16], KT_s[:, kv, kt * 128:(kt + 1) * 128], rq, True, True,
                                      R=[KT_s, QTaug_s], W=[pS])
                            ptb = PT_s[pi2 % 2]
                            pi2 += 1
                            kb.act(ptb[:, 0:nk * 16], pS.t[:, 0:nk * 16], AF.Exp, R=[pS], W=[ptb])
                            for k_ in range(nk):
                                kt = kt0 + k_
                                kb.mm(pO.t[0:65, 0:16], Vaug_s[:, kt, kv, :], ptb[:, k_ * 16:(k_ + 1) * 16], kt == 0, False,
                                      R=[Vaug_s, ptb], W=[pO])
                            kt0 += nk
                        pS = kb.ps()
                        kb.mm(pS.t[0:4, 0:16], KT_s[:, kv, NPAST:NPAST + 4], rq, True, False, R=[KT_s, QTaug_s], W=[pS])
                        kb.mm(pS.t[0:4, 0:16], identb[0:4, 0:4], trimask_s[0:4, :], False, True, R=[identb, trimask_s], W=[pS])
                        ptb = PT_s[pi2 % 2]
                        pi2 += 1
                        kb.act(ptb[0:4, 0:16], pS.t[0:4, 0:16], AF.Exp, R=[pS], W=[ptb])
                        kb.mm(pO.t[0:65, 0:16], Vaug_s[0:4, NPG, kv, :], ptb[0:4, 0:16], False, True, R=[Vaug_s, ptb], W=[pO])
                        kb.cp("act", on_s[:], pO.t[0:64, 0:16], R=[pO], W=[on_s])
                        S.op("dve", lambda e, pO=pO: e.reciprocal(rden_s[64:65, :], pO.t[64:65, 0:16]), R=[pO], W=[rden_s])
                        pB = kb.ps()
                        kb.mm(pB.t[0:64, 0:16], ones[64:65, 0:64], rden_s[64:65, :], True, True, R=[ones, rden_s], W=[pB])
                        onv = on_s[:].rearrange("p (gp two t) -> p gp two t", two=2, t=4)
                        pbv2 = pB.t[0:64, 0:16].rearrange("p (gp two t) -> p gp two t", two=2, t=4)
                        for two in range(2):
                            kb.tt("dve", AT_s[two * 64:(two + 1) * 64, kv * 2:kv * 2 + 2, sq * 4:(sq + 1) * 4],
                                  onv[:, :, two, :], pbv2[:, :, two, :], ALU.mult, R=[on_s, pB], W=[AT_s])
                for b in range(2):
                    wv = wnext("o", 8, 512)
                    p = proj_tok(AT_s, 8, wv, 512, 0, 16)
                    kb.tt("dve", xt[0:16, 0, b * 512:(b + 1) * 512], xt[0:16, 0, b * 512:(b + 1) * 512],
                          p.t[0:16, 0:512], ALU.add, R=[xs_[0], p], W=[xs_[0]])
                ffn(1, 4, 1, 16)
                kb.act(xnb[0][0:16, :], xt[0:16, 0, :], AF.Square, R=[xs_[0]], W=[xnb[0], ms], scale=1.0 / 32.0, accum=ms[0:16, 0:1])
                kb.act(rs[0:16, 0:1], ms[0:16, 0:1], AF.Ln, R=[ms], W=[rs], bias=EPS)
                kb.act(rs[0:16, 0:1], rs[0:16, 0:1], AF.Exp, R=[rs], W=[rs], scale=-0.5)
                kb.stt("dve", yo_s[:], xt[0:16, 0, :], rs[0:16, 0:1], gfin_s[:], ALU.mult, ALU.mult,
                       R=[xs_[0], rs, gfin_s], W=[yo_s])
                kb.dma("sp", ys, yo_s[:], R=[yo_s])
                S.barrier()
            esS.close()
        S.finish()
    return nc


def _blockify(W, col_starts, cols, kp=128):
    K = W.shape[0]
    KC = K // kp
    out = np.empty((len(col_starts), kp, KC * cols), np.float32)
    for b, c0 in enumerate(col_starts):
        blk = W[:, c0:c0 + cols].reshape(KC, kp, cols).transpose(1, 0, 2)
        out[b] = blk.reshape(kp, KC * cols)
    return out


def _rep(v, n=128):
    return np.ascontiguousarray(np.broadcast_to(np.asarray(v, np.float32)[None], (n,) + tuple(np.shape(v))))


def _sub_of(step, role):
    lo, hi = 2 * step, 2 * step + 1
    a = lo if step % 2 == 0 else hi
    return a if role == 0 else (lo + hi - a)


def _consts(NT, NBP, role=0, pos0=0):
    NSUB = NT // 128
    NSTEP = NSUB // 2
    NB = NT // 256
    bf = ml_dtypes.bfloat16
    c = {}
    c["c_identf"] = np.eye(128, dtype=np.float32)
    r = np.arange(128)
    c["c_tri"] = (r[:, None] <= r[None, :]).astype(np.float32)
    c["c_ones"] = np.ones((128, 128), np.float32)
    c["c_mneg"] = np.where(r[None, :] < r[:, None], -1.0e6, 0.0).astype(np.float32).astype(bf)
    c["c_identb"] = np.eye(128).astype(bf)
    tm = np.where(r[:, None] > r[None, :], NEG, 0.0).astype(np.float32)
    c["c_trimask"] = np.tile(tm, (1, 4)).astype(bf)
    oh = np.zeros((NBP, NT), np.float32)
    for n in range(NB):
        oh[n, n * 256:(n + 1) * 256] = 1.0
    c["c_onehot"] = oh.astype(bf)
    half = 32
    inv = (np.float32(10000.0) ** (-np.arange(half, dtype=np.float32) / np.float32(half))).astype(np.float32)
    pos = (pos0 + np.arange(NT)).astype(np.float32)
    ang = (pos[:, None] * inv[None, :]).astype(np.float32)
    tab = np.concatenate([np.cos(ang), np.sin(ang)], axis=1).astype(np.float32)
    c["c_rope"] = tab
    subs = [_sub_of(i, role) for i in range(NSTEP)]
    rows = np.concatenate([np.arange(sb_ * 128, (sb_ + 1) * 128) for sb_ in subs])
    c["c_ropeq"] = np.ascontiguousarray((tab * np.float32(0.125)).astype(np.float32)[rows])
    c["c_x2idx"] = np.ascontiguousarray(rows.reshape(NSTEP, 128).T.astype(np.uint32))
    gmk = np.zeros((NSTEP, 3, NBP), np.float32)
    nn = np.arange(NBP)
    for i in range(NSTEP):
        qb = i
        gmk[i, 0] = np.where(nn < qb, 0.0, -1e30)
        gmk[i, 1] = (nn < qb).astype(np.float32)
        gmk[i, 2] = (nn == qb).astype(np.float32)
    c["c_gmask"] = _rep(gmk.reshape(-1))
    tri4 = np.tile(tm, (1, 4))
    negf = np.full((128, 512), NEG, np.float32)
    zero = np.zeros((128, 512), np.float32)
    mks = np.empty((NSTEP, 128, 1024), np.float32)
    for i in range(NSTEP):
        if subs[i] == 2 * i:
            mks[i, :, 0:512], mks[i, :, 512:1024] = tri4, negf
        else:
            mks[i, :, 0:512], mks[i, :, 512:1024] = zero, tri4
    c["c_masks"] = mks.astype(bf)
    return c


def _weights(inp):
    w = {}
    w_in = np.asarray(inp["w_in_ssm"][0], np.float32)
    w["w_z"] = _blockify(w_in, [0, 512, 1024, 1536], 512)
    w["w_x"] = _blockify(w_in, [2048 + 512 * b for b in range(6)], 512)
    w["w_dt"] = _blockify(w_in, [5120], 32)[0]
    Wout = np.asarray(inp["w_out_ssm"][0], np.float32)
    wo_ = np.empty((4, 128, 4096), np.float32)
    for hf in range(2):
        for kbk in range(2):
            wo_[hf * 2 + kbk] = _blockify(Wout[kbk * 1024:(kbk + 1) * 1024], [hf * 512], 512)[0]
    w["w_out"] = wo_
    gu = np.empty((2, 11, 128, 4096), np.float32)
    dn = np.zeros((2, 6, 128, 4096), np.float32)
    for l in range(2):
        W = np.asarray(inp["w_gu"][l], np.float32)
        for b in range(11):
            blk = np.concatenate([W[:, 256 * b:256 * b + 256], W[:, DFF + 256 * b:DFF + 256 * b + 256]], axis=1)
            gu[l, b] = _blockify(blk, [0], 512)[0]
        Wd = np.asarray(inp["w_down"][l], np.float32)
        for hf in range(2):
            for kbk in range(3):
                nk = 8 if kbk < 2 else 6
                blk = _blockify(Wd[kbk * 1024:kbk * 1024 + nk * 128], [hf * 512], 512)[0]
                dn[l, hf * 3 + kbk, :, 0:nk * 512] = blk
    w["w_gu"] = gu
    w["w_dn"] = dn
    w["w_kv"] = _blockify(np.asarray(inp["w_kv"], np.float32), [0], 512)
    w["w_q"] = _blockify(np.asarray(inp["w_q"][0], np.float32), [0, 512], 512)
    w["w_o"] = _blockify(np.asarray(inp["w_o"][0], np.float32), [0, 512], 512)
    g = np.stack([inp["norm_mix"][0], inp["norm_ffn"][0], inp["norm_kv"], inp["norm_mix"][1],
                  inp["norm_ffn"][1], inp["norm_final"]]).astype(np.float32)
    w["g_all"] = np.ascontiguousarray(np.broadcast_to(g[:, None, :], (6, 128, D)))
    w["g_allT"] = np.ascontiguousarray(g.reshape(6, 8, 128).transpose(2, 0, 1).reshape(128, 48))
    w["g_ssm"] = _rep(inp["norm_ssm"][0])
    cwv = np.asarray(inp["conv_w"][0], np.float32)
    w["cw"] = np.ascontiguousarray(cwv.reshape(4, 24, 128).transpose(2, 1, 0))
    w["cb"] = np.ascontiguousarray(np.asarray(inp["conv_b"][0], np.float32).reshape(24, 128).T)
    w["dtb"] = _rep(inp["dt_bias"][0])
    w["alog"] = _rep(inp["a_log"][0])
    w["dsk"] = _rep(inp["d_skip"][0])
    return w


_NC_CACHE = {}


def _consts_sample(NPAST):
    bf = ml_dtypes.bfloat16
    NPB = NPAST // 256
    NBPS = 8 * ((NPB + 1 + 7) // 8)
    c = {}
    half = 32
    inv = (np.float32(10000.0) ** (-np.arange(half, dtype=np.float32) / np.float32(half))).astype(np.float32)
    pos = (NPAST + (np.arange(16) % 4)).astype(np.float32)
    ang = (pos[:, None] * inv[None, :]).astype(np.float32)
    tab = np.concatenate([np.cos(ang), np.sin(ang)], axis=1).astype(np.float32)
    c["c_rope_s"] = tab
    c["c_ropeq_s"] = (tab * np.float32(0.125)).astype(np.float32)
    c["c_i16"] = np.ascontiguousarray(np.tile(np.eye(16, dtype=np.float32).reshape(1, -1), (128, 1)))
    sel = np.zeros((16, 16, 128), np.float32)
    for t in range(16):
        sel[t, t, :] = 1.0
    c["c_sel"] = sel.reshape(16, 16 * 128)
    oh = np.zeros((NBPS, NPAST + 128), np.float32)
    for n in range(NPB):
        oh[n, n * 256:(n + 1) * 256] = 1.0
    oh[NPB, NPAST:NPAST + 128] = 1.0
    c["c_onehot_s"] = oh.astype(bf)
    nn = np.arange(NBPS)
    gmk = np.stack([np.where(nn < NPB, 0.0, -1e30), (nn < NPB).astype(np.float64), (nn == NPB).astype(np.float64)])
    c["c_gmask_s"] = _rep(gmk.astype(np.float32).reshape(-1))
    r = np.arange(128)
    q = np.arange(16) % 4
    c["c_trimask_s"] = np.where(r[:, None] > q[None, :], NEG, 0.0).astype(np.float32).astype(bf)
    c["c_pidx"] = np.arange(128, dtype=np.float32).reshape(128, 1)
    return c


def run_all(inp, NT, NPAST, n_cores=8):
    NB = NT // 256
    NBP = max(8, NB)
    ck = np.ascontiguousarray(np.asarray(inp["cache_k"], np.float32).reshape(-1, 256))
    cv = np.ascontiguousarray(np.asarray(inp["cache_v"], np.float32).reshape(-1, 256))
    key = (NT, NBP, NPAST, ck.shape[0])
    if key not in _NC_CACHE:
        _NC_CACHE[key] = build(NT, NBP, with_sample=True, NPAST=NPAST, NPOOLROWS=ck.shape[0])
    nc = _NC_CACHE[key]
    shared = {}
    role_c = [_consts(NT, NBP, 0), _consts(NT, NBP, 1)]
    shared.update(_consts_sample(NPAST))
    shared.update(_weights(inp))
    shared["cache_k"] = ck
    shared["cache_v"] = cv
    xpr = np.asarray(inp["x_prompt"], np.float32)
    xsm = np.asarray(inp["x_sample"], np.float32)
    sc = np.asarray(inp["state_conv"], np.float32)
    ssm = np.asarray(inp["state_ssm"], np.float32)
    pt = np.asarray(inp["page_table"], np.int32)
    B = xpr.shape[0]
    in_maps = []
    for c in range(n_cores):
        m = dict(shared)
        m.update(role_c[c // B])
        m["xp"] = np.ascontiguousarray(xpr[c % B])
        sl = slice(4 * c, 4 * c + 4)
        m["xs"] = np.ascontiguousarray(xsm[sl].reshape(16, D))
        m["sconv"] = np.ascontiguousarray(sc[0, sl].reshape(12, CD))
        m["sssm"] = np.ascontiguousarray(ssm[0, sl].reshape(4, DIN, NST))
        m["ptab"] = np.ascontiguousarray(pt[sl])
        in_maps.append(m)
    res = run_bass_kernel_spmd(nc, in_maps, core_ids=list(range(n_cores))).results
    y_prompt = np.empty((B, NT, D), np.float32)
    for b in range(B):
        for role in range(2):
            yc = res[b + B * role]["yp"]
            for i in range(NT // 256):
                sb_ = _sub_of(i, role)
                y_prompt[b, sb_ * 128:(sb_ + 1) * 128] = yc[i * 128:(i + 1) * 128]
    k_prompt = np.stack([res[b]["kp"] for b in range(B)]).reshape(B, NT, 4, 64).astype(np.float32)
    v_prompt = np.stack([res[b]["vp"] for b in range(B)]).reshape(B, NT, 4, 64).astype(np.float32)
    conv_prompt = np.stack([res[b]["convp"] for b in range(B)])[None].astype(np.float32)
    ssm_prompt = np.stack([res[b]["ssmp"] for b in range(B)]).reshape(1, B, NH, HP, NST).astype(np.float32)
    nS = 4 * n_cores
    y_sample = np.concatenate([res[c]["ys"] for c in range(n_cores)]).reshape(nS, 4, D).astype(np.float32)
    conv_sample = np.concatenate([res[c]["convs"] for c in range(n_cores)]).reshape(1, nS, 3, CD).astype(np.float32)
    ssm_sample = np.concatenate([res[c]["ssms"] for c in range(n_cores)]).reshape(1, nS, NH, HP, NST).astype(np.float32)
    k_sample = np.concatenate([res[c]["ks"] for c in range(n_cores)]).reshape(nS, 4, 4, 64).astype(np.float32)
    v_sample = np.concatenate([res[c]["vs"] for c in range(n_cores)]).reshape(nS, 4, 4, 64).astype(np.float32)
    return (y_prompt, y_sample, conv_prompt, ssm_prompt, k_prompt, v_prompt,
            conv_sample, ssm_sample, k_sample, v_sample)


def kernel(**inputs):
    return run_all(inputs, 4096, 8192)
```

```python
import numpy as np
from contextlib import ExitStack
import concourse.bass as bass
import concourse.mybir as mybir
from concourse.bass_utils import run_bass_kernel_spmd
import ml_dtypes

F32 = mybir.dt.float32
BF16 = mybir.dt.bfloat16
I32 = mybir.dt.int32
AF = mybir.ActivationFunctionType
ALU = mybir.AluOpType
AX = mybir.AxisListType

NDS = 6
SEM_LIMIT = 30000


class Buf:
    __slots__ = ("t", "name", "lw", "rd")

    def __init__(self, t, name):
        self.t = t
        self.name = name
        self.lw = None
        self.rd = {}

    def __getitem__(self, k):
        return self.t[k]


class Sched:
    COMPUTE = ("pe", "act", "dve", "pool")
    QUEUES = ("pe", "act", "dve", "pool", "sp")

    def __init__(self, nc, es):
        self.nc, self.es = nc, es
        self.q = {e: [] for e in self.QUEUES}
        self.known = {e: {} for e in self.QUEUES}
        self.csem, self.ccnt = {}, {}
        self.nsem = 0
        self.allsems = []
        for e in self.COMPUTE:
            self._newsem(e)
        self.dsem = {}
        for qn in ("sp", "pool", "act"):
            self.dsem[qn] = []
            for i in range(NDS):
                s = es.enter_context(nc.semaphore(f"d_{qn}_{i}"))
                self.dsem[qn].append([s, 0])
        self.drr = {qn: 0 for qn in self.dsem}
        self.n_ops = 0

    def _newsem(self, e):
        self.nsem += 1
        s = self.es.enter_context(self.nc.semaphore(f"c_{e}_{self.nsem}"))
        if e in self.csem:
            self.allsems.append((self.csem[e], self.ccnt[e]))
        self.csem[e] = s
        self.ccnt[e] = 0

    @staticmethod
    def _deps(R, W):
        deps = {}
        for b in R:
            if b.lw is not None:
                s, v = b.lw
                if deps.get(s, 0) < v:
                    deps[s] = v
        for b in W:
            if b.lw is not None:
                s, v = b.lw
                if deps.get(s, 0) < v:
                    deps[s] = v
            for s, v in b.rd.items():
                if deps.get(s, 0) < v:
                    deps[s] = v
        return deps

    def _wait(self, e, deps, skip=None):
        kn = self.known[e]
        for s, v in deps.items():
            if s is skip:
                continue
            if kn.get(s, 0) >= v:
                continue
            kn[s] = v
            self.q[e].append(("w", s, v))

    @staticmethod
    def _mark(ev, R, W):
        s, v = ev
        for b in W:
            b.lw = ev
            b.rd = {}
        for b in R:
            if any(b is w for w in W):
                continue
            if b.rd.get(s, 0) < v:
                b.rd[s] = v

    def op(self, e, fn, R=(), W=()):
        deps = self._deps(R, W)
        self._wait(e, deps, skip=self.csem[e] if e == "pe" else None)
        if self.ccnt[e] >= SEM_LIMIT:
            self._newsem(e)
        self.ccnt[e] += 1
        ev = (self.csem[e], self.ccnt[e])
        self.q[e].append(("o", fn, ev[0]))
        self._mark(ev, R, W)
        self.n_ops += 1

    def dma(self, qn, fn, R=(), W=()):
        deps = self._deps(R, W)
        self._wait(qn, deps)
        i = self.drr[qn]
        self.drr[qn] = (i + 1) % NDS
        ent = self.dsem[qn][i]
        sem, c = ent
        if c > 0:
            self._wait(qn, {sem: 16 * c})
        ent[1] = c + 1
        ev = (sem, 16 * (c + 1))
        self.q[qn].append(("d", fn, sem))
        self._mark(ev, R, W)
        self.n_ops += 1

    def barrier(self):
        deps = {}
        for e in self.COMPUTE:
            if self.ccnt[e] > 0:
                deps[self.csem[e]] = self.ccnt[e]
        for s, c in self.allsems:
            deps[s] = c
        for qn in self.dsem:
            for s, c in self.dsem[qn]:
                if c > 0:
                    deps[s] = 16 * c
        for e in self.QUEUES:
            self._wait(e, dict(deps), skip=None)

    def finish(self):
        self.barrier()
        nc = self.nc
        q = self.q

        def run(eng, items):
            for it in items:
                if it[0] == "w":
                    eng.wait_ge(it[1], it[2])
                elif it[0] == "o":
                    it[1](eng).then_inc(it[2], 1)
                else:
                    it[1](eng).then_inc(it[2], 16)

        with nc.Block() as block:
            @block.tensor
            def _(e):
                run(e, q["pe"])

            @block.scalar
            def _(e):
                run(e, q["act"])

            @block.vector
            def _(e):
                run(e, q["dve"])

            @block.gpsimd
            def _(e):
                run(e, q["pool"])

            @block.sync
            def _(e):
                run(e, q["sp"])


D = 1024
DIN = 2048
NH = 32
HP = 64
NG = 4
NST = 128
CD = 3072
DFF = 2816
NQH = 16
NKV = 4
HD = 64
EPS = 1e-6
NEG = -30000.0
TOPK = 3
DEBUG = False


class KB:
    def __init__(self, nc, es):
        self.nc, self.es = nc, es
        self.S = Sched(nc, es)
        self.psum = [Buf(es.enter_context(nc.psum_tensor(f"ps{i}", [128, 512], F32)), f"ps{i}") for i in range(8)]
        self.pi = 0
        self.NROT = 6

    def ps(self):
        b = self.psum[self.pi]
        self.pi = (self.pi + 1) % self.NROT
        return b

    def sb(self, name, shape, dt):
        return Buf(self.es.enter_context(self.nc.sbuf_tensor("s_" + name, list(shape), dt)), name)

    def mm(self, out, lhsT, rhs, st, sp, R, W):
        self.S.op("pe", lambda e: e.matmul(out, lhsT, rhs, start=st, stop=sp), R, W)

    def tr(self, out, in_, ident, R, W):
        self.S.op("pe", lambda e: e.transpose(out, in_, ident), R, W)

    def act(self, out, in_, func, R, W, bias=None, scale=None, accum=None):
        kw = {}
        if bias is not None:
            kw["bias"] = bias
        if scale is not None:
            kw["scale"] = scale
        if accum is not None:
            kw["accum_out"] = accum
        self.S.op("act", lambda e: e.activation(out, in_, func, **kw), R, W)

    def tt(self, eng, out, a, b, op, R, W):
        self.S.op(eng, lambda e: e.tensor_tensor(out, a, b, op), R, W)

    def ts(self, eng, out, a, s1, s2, op0, op1, R, W):
        if op1 is None:
            self.S.op(eng, lambda e: e.tensor_scalar(out, a, s1, None, op0), R, W)
        else:
            self.S.op(eng, lambda e: e.tensor_scalar(out, a, s1, s2, op0, op1), R, W)

    def stt(self, eng, out, a, sc, b, op0, op1, R, W):
        self.S.op(eng, lambda e: e.scalar_tensor_tensor(out, a, sc, b, op0, op1), R, W)

    def cp(self, eng, out, in_, R, W):
        if eng == "act":
            self.S.op("act", lambda e: e.activation(out, in_, AF.Copy), R, W)
        else:
            self.S.op(eng, lambda e: e.tensor_copy(out, in_), R, W)

    def memset(self, eng, out, val, W):
        self.S.op(eng, lambda e: e.memset(out, val), (), W)

    def dma(self, qn, out, in_, R=(), W=()):
        self.S.dma(qn, lambda e: e.dma_start(out=out, in_=in_), R, W)


class WStream:
    def __init__(self, kb, nbuf=3, elems=4096):
        self.kb = kb
        self.bufs = [kb.sb(f"wbuf{i}", [128, elems], BF16) for i in range(nbuf)]
        self.plan = []
        self.issued = 0
        self.taken = 0

    def add(self, ap, parts, elems, tag):
        self.plan.append((ap, parts, elems, tag))

    def _issue(self):
        ap, parts, elems, tag = self.plan[self.issued]
        b = self.bufs[self.issued % len(self.bufs)]
        self.kb.dma("pool", b[0:parts, 0:elems], ap, W=[b])
        self.issued += 1

    def next(self, tag):
        n = len(self.bufs)
        while self.issued < len(self.plan) and self.issued < self.taken + n - 1:
            self._issue()
        ap, parts, elems, t = self.plan[self.taken]
        assert t == tag, (t, tag)
        b = self.bufs[self.taken % n]
        self.taken += 1
        return b


def build(NT, NBP, with_sample=False, NPAST=8192, NPOOLROWS=2560 * 128):
    NTILES = NT // 512
    NSUB = NT // 128
    NB = NT // 256
    NSEL = min(TOPK, NB - 1)
    nc = bass.Bass("TRN2", target_bir_lowering=False)
    es = ExitStack()

    def din(name, shape, dt=F32):
        return nc.dram_tensor(name, list(shape), dt, kind="ExternalInput").ap()

    def dout(name, shape, dt=F32):
        return nc.dram_tensor(name, list(shape), dt, kind="ExternalOutput").ap()

    xp = din("xp", [NT, D])
    c_identf = din("c_identf", [128, 128])
    c_tri = din("c_tri", [128, 128])
    c_ones = din("c_ones", [128, 128])
    c_mneg = din("c_mneg", [128, 128], BF16)
    c_identb = din("c_identb", [128, 128], BF16)
    c_trimask = din("c_trimask", [128, 512], BF16)
    c_onehot = din("c_onehot", [NBP, NT], BF16)
    c_rope = din("c_rope", [NT, 64])
    c_ropeq = din("c_ropeq", [NT // 2, 64])
    NSTEP = NSUB // 2
    NT3 = NT // 2
    c_gmask = din("c_gmask", [128, NSTEP * 3 * NBP])
    c_x2idx = din("c_x2idx", [128, NSTEP], mybir.dt.uint32)
    c_masks = din("c_masks", [NSTEP, 128, 2 * 512], BF16)
    g_all = din("g_all", [6, 128, D])
    g_allT = din("g_allT", [128, 6 * 8])
    g_ssm = din("g_ssm", [128, DIN])
    cw_d = din("cw", [128, 24, 4])
    cb_d = din("cb", [128, 24])
    dtb_d = din("dtb", [128, NH])
    alog_d = din("alog", [128, NH])
    dsk_d = din("dsk", [128, NH])
    w_z = din("w_z", [4, 128, 4096])
    w_x = din("w_x", [6, 128, 4096])
    w_dt = din("w_dt", [128, 8 * 32])
    w_out = din("w_out", [4, 128, 4096])
    w_gu = din("w_gu", [2, 11, 128, 4096])
    w_dn = din("w_dn", [2, 6, 128, 4096])
    w_kv = din("w_kv", [1, 128, 4096])
    w_q = din("w_q", [2, 128, 4096])
    w_o = din("w_o", [2, 128, 4096])

    yp = dout("yp", [NT // 2, D])
    kp = dout("kp", [NT, 256])
    vp = dout("vp", [NT, 256])
    convp = dout("convp", [3, CD])
    ssmp = dout("ssmp", [DIN, NST])
    x2d_t = nc.dram_tensor("x2d", [NT, D], F32, kind="Internal").ap()
    if with_sample:
        NBPS = 8 * ((NPAST // 256 + 1 + 7) // 8)
        xs = din("xs", [16, D])
        sconv = din("sconv", [12, CD])
        sssm = din("sssm", [4, DIN, NST])
        cache_k = din("cache_k", [NPOOLROWS, 256])
        cache_v = din("cache_v", [NPOOLROWS, 256])
        ptab = din("ptab", [4, NPAST // 128], I32)
        c_rope_s = din("c_rope_s", [16, 64])
        c_ropeq_s = din("c_ropeq_s", [16, 64])
        c_i16 = din("c_i16", [128, 256])
        c_sel = din("c_sel", [16, 16 * 128])
        c_onehot_s = din("c_onehot_s", [NBPS, NPAST + 128], BF16)
        c_gmask_s = din("c_gmask_s", [128, 3 * NBPS])
        c_trimask_s = din("c_trimask_s", [128, 16], BF16)
        c_pidx = din("c_pidx", [128, 1])
        ys = dout("ys", [16, D])
        convs = dout("convs", [12, CD])
        ssms = dout("ssms", [4, DIN, NST])
        ks = dout("ks", [16, 256])
        vs = dout("vs", [16, 256])
    dbg_at = dout("dbg_at", [NTILES // 2, 128, 8, 512], BF16) if DEBUG else None

    with es:
        kb = KB(nc, es)
        S = kb.S
        sb = kb.sb
        kpBs = [Buf(kp, f"kp_dram{i}") for i in range(NSUB)]
        vpBs = [Buf(vp, f"vp_dram{i}") for i in range(NSUB)]
        x2B = Buf(x2d_t, "x2_dram")

        identf = sb("identf", [128, 128], F32)
        tri = sb("tri", [128, 128], F32)
        ones = sb("ones", [128, 128], F32)
        identb = sb("identb", [128, 128], BF16)
        gT = sb("gT", [128, 6, 8], F32)
        kb.dma("sp", identf[:], c_identf, W=[identf])
        kb.dma("sp", tri[:], c_tri, W=[tri])
        kb.dma("sp", ones[:], c_ones, W=[ones])
        mneg = sb("mneg", [128, 128], BF16)
        kb.dma("sp", mneg[:], c_mneg, W=[mneg])
        kb.dma("sp", identb[:], c_identb, W=[identb])
        kb.dma("sp", gT[:].rearrange("p a b -> p (a b)"), g_allT, W=[gT])

        ws = WStream(kb)
        for j in range(NTILES):
            for b in range(4):
                ws.add(w_z[b], 128, 4096, "z")
            for b in range(6):
                ws.add(w_x[b], 128, 4096, "x")
            ws.add(w_dt, 128, 256, "dt")
            if j == NTILES - 1:
                for b in range(6):
                    ws.add(w_x[b], 128, 4096, "xc")
            for b in range(4):
                ws.add(w_out[b], 128, 4096, "out")
            for b in range(11):
                ws.add(w_gu[0, b], 128, 4096, "gu")
            for b in range(6):
                ws.add(w_dn[0, b][:, 0:(4096 if b % 3 < 2 else 3072)], 128, 4096 if b % 3 < 2 else 3072, "dn")
            ws.add(w_kv[0], 128, 4096, "kv")
        for j in range(NTILES // 2):
            for b in range(2):
                ws.add(w_q[b], 128, 4096, "q")
            for b in range(2):
                ws.add(w_o[b], 128, 4096, "o")
            for b in range(11):
                ws.add(w_gu[1, b], 128, 4096, "gu")
            for b in range(6):
                ws.add(w_dn[1, b][:, 0:(4096 if b % 3 < 2 else 3072)], 128, 4096 if b % 3 < 2 else 3072, "dn")

        if with_sample:
            for b in range(4):
                ws.add(w_z[b], 128, 4096, "z")
            for b in range(6):
                ws.add(w_x[b], 128, 4096, "x")
            ws.add(w_dt, 128, 256, "dt")
            for b in range(4):
                ws.add(w_out[b], 128, 4096, "out")
            for b in range(11):
                ws.add(w_gu[0, b], 128, 4096, "gu")
            for b in range(6):
                ws.add(w_dn[0, b][:, 0:(4096 if b % 3 < 2 else 3072)], 128, 4096 if b % 3 < 2 else 3072, "dn")
            ws.add(w_kv[0], 128, 4096, "kv")
            for b in range(2):
                ws.add(w_q[b], 128, 4096, "q")
            for b in range(2):
                ws.add(w_o[b], 128, 4096, "o")
            for b in range(11):
                ws.add(w_gu[1, b], 128, 4096, "gu")
            for b in range(6):
                ws.add(w_dn[1, b][:, 0:(4096 if b % 3 < 2 else 3072)], 128, 4096 if b % 3 < 2 else 3072, "dn")
        esP = ExitStack()

        def sbP(name, shape, dt):
            return Buf(esP.enter_context(nc.sbuf_tensor("sP_" + name, list(shape), dt)), name)
        xt = sbP("xt", [128, 4, D], F32)
        xs_ = [Buf(xt.t, f"xt_sub{i}") for i in range(4)]
        xnb = [sbP(f"xnb{i}", [128, D], BF16) for i in range(2)]
        xnT = sbP("xnT", [128, 8, 512], BF16)
        junk = sbP("junk", [128, 512], BF16)
        ms = sbP("ms", [128, 8], F32)
        rs = sbP("rs", [128, 8], F32)
        hT = sbP("hT", [128, 22, 512], BF16)
        cst = [sbP(f"cst{i}", [128, 515], F32) for i in range(2)]
        cacc = [sbP(f"cacc{i}", [128, 512], F32) for i in range(2)]

        def norm_T(gidx, nsub=4, ntok=128):
            for s_ in range(nsub):
                jb = xnb[(s_ + 1) % len(xnb)]
                kb.act(jb[0:ntok, :], xt[0:ntok, s_, :], AF.Square, R=[xs_[s_]], W=[jb, ms],
                       scale=1.0 / 32.0, accum=ms[0:ntok, s_:s_ + 1])
            kb.act(rs[0:ntok, 0:nsub], ms[0:ntok, 0:nsub], AF.Ln, R=[ms], W=[rs], bias=EPS)
            kb.act(rs[0:ntok, 0:nsub], rs[0:ntok, 0:nsub], AF.Exp, R=[rs], W=[rs], scale=-0.5)
            for s_ in range(nsub):
                xb_ = xnb[s_ % len(xnb)]
                kb.ts("dve", xb_[0:ntok, :], xt[0:ntok, s_, :], rs[0:ntok, s_:s_ + 1], None, ALU.mult, None,
                      R=[xs_[s_], rs], W=[xb_])
                p = kb.ps()
                pbv = p.t[:].bitcast(BF16)
                for kc in range(8):
                    kb.tr(pbv[:, kc * 128:kc * 128 + ntok], xb_[0:ntok, kc * 128:(kc + 1) * 128],
                          identb[0:ntok, 0:ntok], R=[xb_, identb], W=[p])
                kb.tt("dve", xnT[:, :, s_ * 128:s_ * 128 + ntok],
                      pbv[:, 0:1024].rearrange("p (k t) -> p k t", k=8)[:, :, 0:ntok],
                      gT[:, gidx, :].unsqueeze(2).to_broadcast([128, 8, ntok]), ALU.mult, R=[p, gT], W=[xnT])

        def proj_tok(xT_, KC, wv, cols, s_, ntok=128, p=None, c0=0):
            p = p or kb.ps()
            for kc in range(KC):
                kb.mm(p.t[0:ntok, c0:c0 + cols], xT_[:, kc, s_ * 128:s_ * 128 + ntok], wv[:, kc, 0:cols],
                      kc == 0, kc == KC - 1, R=[xT_, wv_buf[0]], W=[p])
            return p

        def proj_feat(xT_, KC, wv, col0, ntot):
            p = kb.ps()
            for kc in range(KC):
                kb.mm(p.t[:, 0:ntot], wv[:, kc, col0:col0 + 128], xT_[:, kc, 0:ntot],
                      kc == 0, kc == KC - 1, R=[xT_, wv_buf[0]], W=[p])
            return p

        wv_buf = [None]

        def wnext(tag, KC, cols, parts=128):
            b = ws.next(tag)
            wv_buf[0] = b
            return b.t[0:parts, 0:KC * cols].rearrange("p (k c) -> p k c", k=KC)

        def ffn(layer, gidx, nsub=4, ntok=128):
            ntot = (nsub - 1) * 128 + ntok
            norm_T(gidx, nsub, ntok)
            for b in range(11):
                wv = wnext("gu", 8, 512)
                for t2 in range(2):
                    fft = b * 2 + t2
                    pg = proj_feat(xnT, 8, wv, t2 * 128, ntot)
                    pu = proj_feat(xnT, 8, wv, 256 + t2 * 128, ntot)
                    sg_ = cacc[fft % 2]
                    kb.act(sg_[:, 0:ntot], pg.t[:, 0:ntot], AF.Silu, R=[pg], W=[sg_])
                    kb.tt("dve", hT[:, fft, 0:ntot], sg_[:, 0:ntot], pu.t[:, 0:ntot], ALU.mult, R=[sg_, pu], W=[hT])
            for hf in range(2):
                pp = [kb.ps() for _ in range(nsub)]
                for kbk in range(3):
                    nk = 8 if kbk < 2 else 6
                    wv = wnext("dn", nk, 512)
                    for s_ in range(nsub):
                        for kc in range(nk):
                            kcg = kbk * 8 + kc
                            kb.mm(pp[s_].t[0:ntok, 0:512], hT[:, kcg, s_ * 128:s_ * 128 + ntok], wv[:, kc, :],
                                  kcg == 0, kcg == 21, R=[hT, wv_buf[0]], W=[pp[s_]])
                for s_ in range(nsub):
                    kb.tt("dve", xt[0:ntok, s_, hf * 512:(hf + 1) * 512], xt[0:ntok, s_, hf * 512:(hf + 1) * 512],
                          pp[s_].t[0:ntok, 0:512], ALU.add, R=[xs_[s_], pp[s_]], W=[xs_[s_]])


        def rope(dstB, dst, srcB, src, H, tab):
            n = src.shape[0]
            rt1, rt2 = cst[0], cst[1]
            sv = src.rearrange("p (h two d) -> p h two d", two=2, d=32)
            dv = dst.rearrange("p (h two d) -> p h two d", two=2, d=32)
            cosb = tab[0:n, 0:32].unsqueeze(1).to_broadcast([n, H, 32])
            sinb = tab[0:n, 32:64].unsqueeze(1).to_broadcast([n, H, 32])
            a = rt1[0:n, 0:H * 32].rearrange("p (h d) -> p h d", d=32)
            b_ = rt2[0:n, 0:H * 32].rearrange("p (h d) -> p h d", d=32)
            x1, x2 = sv[:, :, 0, :], sv[:, :, 1, :]
            kb.tt("dve", a, x1, cosb, ALU.mult, R=[srcB, tab], W=[rt1])
            kb.tt("dve", b_, x2, sinb, ALU.mult, R=[srcB, tab], W=[rt2])
            kb.tt("dve", dv[:, :, 0, :], a, b_, ALU.subtract, R=[rt1, rt2], W=[dstB])
            kb.tt("dve", a, x2, cosb, ALU.mult, R=[srcB, tab, dstB], W=[rt1])
            kb.tt("dve", b_, x1, sinb, ALU.mult, R=[srcB, tab, dstB], W=[rt2])
            kb.tt("dve", dv[:, :, 1, :], a, b_, ALU.add, R=[rt1, rt2], W=[dstB])

        es2 = ExitStack()
        with es2:
            def sb2(name, shape, dt):
                return Buf(es2.enter_context(nc.sbuf_tensor("s2_" + name, list(shape), dt)), name)
            gssm = sb2("gssm", [128, DIN], F32)
            cw = sb2("cw", [128, 24, 4], F32)
            cbias = sb2("cbias", [128, 24], F32)
            dtb = sb2("dtb", [128, NH], F32)
            arep = sb2("arep", [128, NH], F32)
            dsk = sb2("dsk", [128, NH], F32)
            DI = sb2("DI", [128, NH, 128], BF16)
            kb.dma("sp", gssm[:], g_ssm, W=[gssm])
            kb.dma("sp", cw[:], cw_d, W=[cw])
            kb.dma("sp", cbias[:], cb_d, W=[cbias])
            kb.dma("sp", dtb[:], dtb_d, W=[dtb])
            kb.dma("sp", arep[:], alog_d, W=[arep])
            kb.dma("sp", dsk[:], dsk_d, W=[dsk])
            kb.act(arep[:], arep[:], AF.Exp, R=[arep], W=[arep])
            kb.ts("dve", arep[:], arep[:], -1.0, None, ALU.mult, None, R=[arep], W=[arep])
            kb.tt("dve", DI[:], identf[:].unsqueeze(1).to_broadcast([128, NH, 128]),
                  dsk[:].unsqueeze(2).to_broadcast([128, NH, 128]), ALU.mult, R=[identf, dsk], W=[DI])

            halo = sb2("halo", [128, 24, 3], F32)
            kb.memset("pool", halo[:], 0.0, W=[halo])
            cob = [sb2(f"cob{i}", [128, 512], BF16) for i in range(2)]
            x_tok = sb2("x_tok", [128, 4, DIN], BF16)
            B_tok = sb2("B_tok", [128, 4, 512], BF16)
            BT = sb2("BT", [128, 4, 512], BF16)
            CT = sb2("CT", [128, 4, 512], BF16)
            dtt = sb2("dtt", [128, 4, NH], F32)
            dta = sb2("dta", [128, 4, NH], F32)
            acs = sb2("acs", [128, 64], F32)
            nacs = sb2("nacs", [128, NH], F32)
            eacs = sb2("eacs", [128, NH], F32)
            cdec = sb2("cdec", [128, NH], F32)
            dte = sb2("dte", [128, NH], F32)
            w1 = sb2("w1", [128, NH], F32)
            xd = sb2("xd", [128, NH, HP], BF16)
            xdw = sb2("xdw", [128, NH, HP], BF16)
            Gm = sb2("Gm", [128, 4, 128], BF16)
            Dc = [sb2(f"Dc{i}", [128, 512], F32) for i in range(2)]
            Eb = [sb2(f"Eb{i}", [128, 1024], BF16) for i in range(2)]
            Mb = [sb2(f"Mb{i}", [128, 8, 128], BF16) for i in range(2)]
            t1b = [sb2(f"t1b{i}", [128, 512], F32) for i in range(2)]
            y32 = [sb2(f"y32{i}", [128, 512], F32) for i in range(2)]
            yg = sb2("yg", [128, DIN], F32)
            ygs = [Buf(yg.t, f"yg_g{i}") for i in range(4)]
            gms = sb2("gms", [128, 4], F32)
            grs = sb2("grs", [128, 4], F32)
            yn = [sb2(f"yn{i}", [128, 1024], BF16) for i in range(1)]
            ynT = sb2("ynT", [128, 16, 512], BF16)
            h32 = sb2("h32", [128, DIN], F32)
            hbf = sb2("hbf", [128, DIN], BF16)
            kv32 = [t1b[0], y32[0]]
            kro = cacc
            rtab = [sb2(f"rtab{i}", [128, 64], F32) for i in range(4)]
            kb.memset("pool", h32[:], 0.0, W=[h32])
            kb.memset("pool", hbf[:], 0.0, W=[hbf])
            zs = hT.t[:].rearrange("p k t -> p (k t)")[:, 0:8192].rearrange("p (s c) -> p s c", s=4)
            print("phase2 sbuf remaining", nc.sbuf_bytes_remaining)

            for j in range(NTILES):
                for s4 in range(4):
                    kb.dma("sp", xt[:, s4, :], xp[j * 512 + s4 * 128:j * 512 + (s4 + 1) * 128, :], W=[xs_[s4]])
                norm_T(0)
                for b in range(4):
                    wv = wnext("z", 8, 512)
                    for s_ in range(4):
                        p = proj_tok(xnT, 8, wv, 512, s_)
                        kb.act(zs[:, s_, b * 512:(b + 1) * 512], p.t[:, 0:512], AF.Silu, R=[p], W=[hT])
                conv_wv = [None]

                def conv_front(ci):
                    if ci % 4 == 0:
                        conv_wv[0] = wnext("x", 8, 512)
                    p = proj_feat(xnT, 8, conv_wv[0], (ci % 4) * 128, 512)
                    st_ = cst[ci % 2]
                    ca = cacc[ci % 2]
                    kb.cp("act", st_[:, 0:3], halo[:, ci, :], R=[halo], W=[st_])
                    kb.cp("act", st_[:, 3:515], p.t[:, 0:512], R=[p], W=[st_])
                    kb.cp("act", halo[:, ci, :], st_[:, 512:515], R=[st_], W=[halo])
                    kb.act(ca[:], p.t[:, 0:512], AF.Identity, R=[p, cw, cbias], W=[ca],
                           scale=cw[:, ci, 3:4], bias=cbias[:, ci:ci + 1])
                    for tap in (2, 1, 0):
                        kb.stt("dve", ca[:], st_[:, tap:tap + 512], cw[:, ci, tap:tap + 1], ca[:],
                               ALU.mult, ALU.add, R=[st_, cw, ca], W=[ca])

                def conv_back(ci):
                    ca = cacc[ci % 2]
                    if ci < 16:
                        co = cob[ci % 2]
                        kb.act(co[:], ca[:], AF.Silu, R=[ca], W=[co])
                        pt = kb.ps()
                        pbv = pt.t[:].bitcast(BF16)
                        for s_ in range(4):
                            kb.tr(pbv[:, s_ * 128:(s_ + 1) * 128], co[:, s_ * 128:(s_ + 1) * 128], identb[:],
                                  R=[co, identb], W=[pt])
                        kb.cp("dve", x_tok[:, :, ci * 128:(ci + 1) * 128],
                              pbv[:, 0:512].rearrange("p (s c) -> p s c", s=4), R=[pt], W=[x_tok])
                    elif ci < 20:
                        g = ci - 16
                        kb.act(BT[:, g, :], ca[:], AF.Silu, R=[ca], W=[BT])
                        pt = kb.ps()
                        pbv = pt.t[:].bitcast(BF16)
                        for s_ in range(4):
                            kb.tr(pbv[:, s_ * 128:(s_ + 1) * 128], BT[:, g, s_ * 128:(s_ + 1) * 128], identb[:],
                                  R=[BT, identb], W=[pt])
                        kb.cp("dve", B_tok[:, :, g * 128:(g + 1) * 128],
                              pbv[:, 0:512].rearrange("p (s c) -> p s c", s=4), R=[pt], W=[B_tok])
                    else:
                        g = ci - 20
                        kb.act(CT[:, g, :], ca[:], AF.Silu, R=[ca], W=[CT])
                conv_front(0)
                for ci in range(24):
                    if ci + 1 < 24:
                        conv_front(ci + 1)
                    conv_back(ci)
                wv = wnext("dt", 8, 32)
                p = kb.ps()
                for s_ in range(4):
                    proj_tok(xnT, 8, wv, 32, s_, p=p, c0=s_ * 32)
                kb.tt("dve", dtt[:], p.t[:, 0:128].rearrange("p (s h) -> p s h", s=4),
                      dtb[:].unsqueeze(1).to_broadcast([128, 4, NH]), ALU.add, R=[p, dtb], W=[dtt])
                kb.act(dtt[:], dtt[:], AF.Exp, R=[dtt], W=[dtt])
                kb.act(dtt[:], dtt[:], AF.Ln, R=[dtt], W=[dtt], bias=1.0)
                kb.tt("dve", dta[:], dtt[:], arep[:].unsqueeze(1).to_broadcast([128, 4, NH]), ALU.mult,
                      R=[dtt, arep], W=[dta])
                if j == NTILES - 1:
                    for b in range(6):
                        wv = wnext("xc", 8, 512)
                        p = kb.ps()
                        for kc in range(8):
                            kb.mm(p.t[0:3, 0:512], xnT[:, kc, 509:512], wv[:, kc, 0:512], kc == 0, kc == 7,
                                  R=[xnT, wv_buf[0]], W=[p])
                        c3 = cacc[b % 2]
                        kb.cp("act", c3[0:3, :], p.t[0:3, 0:512], R=[p], W=[c3])
                        kb.dma("sp", convp[:, b * 512:(b + 1) * 512], c3[0:3, :], R=[c3])
                for s_ in range(4):
                    gi = j * 4 + s_
                    tsl = slice(s_ * 128, (s_ + 1) * 128)
                    p1 = kb.ps()
                    kb.mm(p1.t[:, 0:32], tri[:], dta[:, s_, :], True, True, R=[tri, dta], W=[p1])
                    kb.mm(p1.t[:, 32:64], ones[:], dta[:, s_, :], True, True, R=[ones, dta], W=[p1])
                    kb.cp("dve", acs[:], p1.t[:, 0:64], R=[p1], W=[acs])
                    kb.ts("dve", nacs[:], acs[:, 0:32], -1.0, None, ALU.mult, None, R=[acs], W=[nacs])
                    kb.act(eacs[:], acs[:, 0:32], AF.Exp, R=[acs], W=[eacs])
                    kb.act(cdec[:], acs[:, 32:64], AF.Exp, R=[acs], W=[cdec])
                    kb.tt("dve", dte[:], acs[:, 32:64], acs[:, 0:32], ALU.subtract, R=[acs], W=[dte])
                    kb.act(dte[:], dte[:], AF.Exp, R=[dte], W=[dte])
                    kb.tt("dve", w1[:], dte[:], dtt[:, s_, :], ALU.mult, R=[dte, dtt], W=[w1])
                    xv = x_tok[:, s_, :].rearrange("p (h d) -> p h d", d=HP)
                    kb.tt("dve", xd[:], xv, dtt[:, s_, :].unsqueeze(2).to_broadcast([128, NH, HP]), ALU.mult,
                          R=[x_tok, dtt], W=[xd])
                    kb.tt("pool", xdw[:], xv, w1[:].unsqueeze(2).to_broadcast([128, NH, HP]), ALU.mult,
                          R=[x_tok, w1], W=[xdw])
                    pG = kb.ps()
                    for g in range(4):
                        kb.mm(pG.t[:, g * 128:(g + 1) * 128], BT[:, g, tsl], CT[:, g, tsl], True, True,
                              R=[BT, CT], W=[pG])
                    kb.tt("dve", Gm[:], pG.t[:, 0:512].rearrange("p (g l) -> p g l", g=4),
                          tri[:].unsqueeze(1).to_broadcast([128, 4, 128]), ALU.mult, R=[pG, tri], W=[Gm])
                    def ssd_front(g):
                        ebb, mbb = Eb[g % 2], Mb[g % 2]
                        for half in range(2):
                            dcb = Dc[half]
                            pS = kb.ps()
                            for hh in range(4):
                                h = g * 8 + half * 4 + hh
                                kb.mm(pS.t[:, hh * 128:(hh + 1) * 128], dta[:, s_, h:h + 1].to_broadcast([128, 128]),
                                      tri[:], True, False, R=[dta, tri], W=[pS])
                                kb.mm(pS.t[:, hh * 128:(hh + 1) * 128], identb[:], mneg[:], False, True,
                                      R=[identb, mneg], W=[pS])
                            h0 = g * 8 + half * 4
                            kb.tt("dve", dcb[:].rearrange("p (h l) -> p h l", h=4),
                                  pS.t[:, 0:512].rearrange("p (h l) -> p h l", h=4),
                                  nacs[:, h0:h0 + 4].unsqueeze(2).to_broadcast([128, 4, 128]), ALU.add,
                                  R=[pS, nacs], W=[dcb])
                            kb.act(ebb[:, half * 512:(half + 1) * 512], dcb[:], AF.Exp, R=[dcb], W=[ebb])
                        kb.tt("dve", mbb[:], ebb[:].rearrange("p (h l) -> p h l", h=8),
                              Gm[:, g, :].unsqueeze(1).to_broadcast([128, 8, 128]), ALU.mult, R=[ebb, Gm], W=[mbb])

                    def ssd_back(g):
                        mbb = Mb[g % 2]
                        pO = kb.ps()
                        kb.mm(pO.t[:, 0:512], CT[:, g, tsl], hbf[:, g * 512:(g + 1) * 512], True, True,
                              R=[CT, hbf], W=[pO])
                        pY = kb.ps()
                        for hh in range(8):
                            h = g * 8 + hh
                            kb.mm(pY.t[:, hh * 64:(hh + 1) * 64], mbb[:, hh, :], xd[:, h, :], True, False,
                                  R=[mbb, xd], W=[pY])
                            kb.mm(pY.t[:, hh * 64:(hh + 1) * 64], DI[:, h, :], x_tok[:, s_, h * 64:(h + 1) * 64],
                                  False, True, R=[DI, x_tok], W=[pY])
                        t1 = t1b[g % 2]
                        yy = y32[g % 2]
                        kb.tt("dve", t1[:].rearrange("p (h d) -> p h d", d=HP),
                              pO.t[:, 0:512].rearrange("p (h d) -> p h d", d=HP),
                              eacs[:, g * 8:(g + 1) * 8].unsqueeze(2).to_broadcast([128, 8, HP]), ALU.mult,
                              R=[pO, eacs], W=[t1])
                        kb.tt("dve", yy[:], pY.t[:, 0:512], t1[:], ALU.add, R=[pY, t1], W=[yy])
                        kb.tt("dve", yg[:, g * 512:(g + 1) * 512], yy[:], zs[:, s_, g * 512:(g + 1) * 512], ALU.mult,
                              R=[yy, hT], W=[ygs[g]])
                        kb.act(junk[:, 0:512], yg[:, g * 512:(g + 1) * 512], AF.Square, R=[ygs[g]], W=[junk, gms],
                               scale=float(512 ** -0.5), accum=gms[:, g:g + 1])
                    ssd_front(0)
                    for g in range(4):
                        if g + 1 < 4:
                            ssd_front(g + 1)
                        ssd_back(g)
                    kb.act(grs[:], gms[:], AF.Ln, R=[gms], W=[grs], bias=EPS)
                    kb.act(grs[:], grs[:], AF.Exp, R=[grs], W=[grs], scale=-0.5)
                    ynb = yn[0]
                    for half in range(2):
                        for g2 in range(2):
                            g = half * 2 + g2
                            kb.stt("dve", ynb[:, g2 * 512:(g2 + 1) * 512], yg[:, g * 512:(g + 1) * 512], grs[:, g:g + 1],
                                   gssm[:, g * 512:(g + 1) * 512], ALU.mult, ALU.mult, R=[ygs[g], grs, gssm], W=[ynb])
                        pt = kb.ps()
                        pbv = pt.t[:].bitcast(BF16)
                        for k8 in range(8):
                            kb.tr(pbv[:, k8 * 128:(k8 + 1) * 128], ynb[:, k8 * 128:(k8 + 1) * 128], identb[:],
                                  R=[ynb, identb], W=[pt])
                        kb.cp("act", ynT[:, half * 8:(half + 1) * 8, tsl],
                              pbv[:, 0:1024].rearrange("p (k t) -> p k t", k=8), R=[pt], W=[ynT])
                    for g in range(4):
                        pT_ = kb.ps()
                        kb.mm(pT_.t[:, 0:512], B_tok[:, s_, g * 128:(g + 1) * 128],
                              xdw[:, g * 8:(g + 1) * 8, :], True, True, R=[B_tok, xdw], W=[pT_])
                        hv = h32[:, g * 512:(g + 1) * 512].rearrange("p (h d) -> p h d", d=HP)
                        if g == 0:
                            hall = h32[:].rearrange("p (h d) -> p h d", d=HP)
                            kb.tt("dve", hall, hall, cdec[:].unsqueeze(2).to_broadcast([128, NH, HP]),
                                  ALU.mult, R=[h32, cdec], W=[h32])
                        kb.tt("dve", hv, hv, pT_.t[:, 0:512].rearrange("p (h d) -> p h d", d=HP), ALU.add,
                              R=[h32, pT_], W=[h32])
                    kb.cp("act", hbf[:], h32[:], R=[h32], W=[hbf])
                if j == NTILES - 1:
                    for q4 in range(4):
                        pt = kb.ps()
                        for k4 in range(4):
                            k = q4 * 4 + k4
                            kb.tr(pt.t[:, k4 * 128:(k4 + 1) * 128], h32[:, k * 128:(k + 1) * 128], identf[:],
                                  R=[h32, identf], W=[pt])
                        kb.cp("act", yg[:, q4 * 512:(q4 + 1) * 512], pt.t[:, 0:512], R=[pt], W=[ygs[q4]])
                    kb.dma("sp", ssmp.rearrange("(k p) n -> p k n", p=128),
                           yg[:].rearrange("p (k n) -> p k n", k=16), R=ygs)
                for hf in range(2):
                    pp = [kb.ps() for _ in range(4)]
                    for kbk in range(2):
                        wv = wnext("out", 8, 512)
                        for s_ in range(4):
                            for kc in range(8):
                                kcg = kbk * 8 + kc
                                kb.mm(pp[s_].t[:, 0:512], ynT[:, kcg, s_ * 128:(s_ + 1) * 128], wv[:, kc, :],
                                      kcg == 0, kcg == 15, R=[ynT, wv_buf[0]], W=[pp[s_]])
                    for s_ in range(4):
                        kb.tt("dve", xt[:, s_, hf * 512:(hf + 1) * 512], xt[:, s_, hf * 512:(hf + 1) * 512],
                              pp[s_].t[:, 0:512], ALU.add, R=[xs_[s_], pp[s_]], W=[xs_[s_]])
                ffn(0, 1)
                kb.dma("sp", x2d_t[j * 512:(j + 1) * 512, :].rearrange("(s p) d -> p s d", p=128), xt[:],
                       R=xs_, W=[x2B])
                norm_T(2)
                wv = wnext("kv", 8, 512)
                for s_ in range(4):
                    kb.dma("sp", rtab[s_][:], c_rope[(j * 4 + s_) * 128:(j * 4 + s_ + 1) * 128, :], W=[rtab[s_]])
                for s_ in range(4):
                    gi = j * 4 + s_
                    p = proj_tok(xnT, 8, wv, 512, s_)
                    kvb, krb, tb = kv32[s_ % 2], kro[s_ % 2], rtab[s_]
                    kb.cp("act", kvb[:], p.t[:, 0:512], R=[p], W=[kvb])
                    rope(krb, krb[:, 0:256], kvb, kvb[:, 0:256], 4, tb)
                    kb.dma("sp", kp[gi * 128:(gi + 1) * 128, :], krb[:, 0:256], R=[krb], W=[kpBs[gi]])
                    kb.dma("sp", vp[gi * 128:(gi + 1) * 128, :], kvb[:, 256:512], R=[kvb], W=[vpBs[gi]])
            S.barrier()
        S.barrier()

        KR = 64 + NBP
        es3 = ExitStack()
        with es3:
            def sb3(name, shape, dt):
                return Buf(es3.enter_context(nc.sbuf_tensor("s3_" + name, list(shape), dt)), name)
            gfin = sb3("gfin", [128, D], F32)
            kb.dma("sp", gfin[:], g_all[5], W=[gfin])
            KT = sb3("KT", [128, 4, NT], BF16)
            Vaug = sb3("Vaug", [128, NSUB, 4, 128], BF16)
            rd = sb3("rd", [64, 512], F32)
            kmT = sb3("kmT", [64, 4, NBP], F32)
            gmask = sb3("gmask", [128, NSTEP * 3 * NBP], F32)
            x2idx = sb3("x2idx", [128, NSTEP], mybir.dt.uint32)
            mk = [sb3(f"mk{i}", [128, 2 * 512], BF16) for i in range(2)]
            trimask = sb3("trimask", [128, 512], BF16)
            kin2 = [sb3(f"kin2_{i}", [128, 2, 512], F32) for i in range(2)]
            kin2K = [Buf(kin2[i].t, f"kin2K{i}") for i in range(2)]
            kin2V = [Buf(kin2[i].t, f"kin2V{i}") for i in range(2)]
            kbb = [sb3(f"kbb{i}", [128, 256], BF16) for i in range(2)]
            qrs = [sb3(f"qr{i}", [128, D], F32) for i in range(2)]
            Qaug = sb3("Qaug", [128, NQH, KR], BF16)
            QT32 = sb3("QT32", [64, 4, 128], F32)
            QTaug = sb3("QTaug", [128, NQH, 128], BF16)
            gm = sb3("gm", [128, NQH, NBP], F32)
            m8 = sb3("m8", [128, NQH, 8], F32)
            sel = sb3("sel", [128, NQH, NBP], F32)
            PT = [sb3(f"PT{i}", [128, 512], BF16) for i in range(3)]
            AT = sb3("AT", [128, 8, 512], BF16)
            rtq = [sb3(f"rtq{i}", [128, 64], F32) for i in range(2)]
            print("phase3 sbuf remaining", nc.sbuf_bytes_remaining)
            kb.dma("sp", gmask[:], c_gmask, W=[gmask])
            kb.dma("sp", trimask[:], c_trimask, W=[trimask])
            kb.dma("sp", x2idx[:], c_x2idx, W=[x2idx])
            kb.memset("pool", kmT[:], 0.0, W=[kmT])
            kb.memset("pool", KT[:], 0.0, W=[KT])
            kb.memset("pool", QTaug[:], 0.0, W=[QTaug])
            kb.memset("pool", Vaug[:], 1.0, W=[Vaug])
            kb.NROT = 4
            for kv in range(4):
                kb.dma("sp", KT[64:64 + NBP, kv, :], c_onehot, W=[KT])
            pi_ = 0
            for n in range(NB):
                kin = kin2[n % 2]
                kinK, kinV = kin2K[n % 2], kin2V[n % 2]
                kb.dma("sp", kin[:, :, 0:256], kp[n * 256:(n + 1) * 256, :].rearrange("(s p) d -> p s d", p=128),
                       R=[kpBs[2 * n], kpBs[2 * n + 1]], W=[kinK])
                kb.dma("sp", kin[:, :, 256:512], vp[n * 256:(n + 1) * 256, :].rearrange("(s p) d -> p s d", p=128),
                       R=[vpBs[2 * n], vpBs[2 * n + 1]], W=[kinV])
                pm = kb.ps()
                for h in range(4):
                    for s2 in range(2):
                        kb.mm(pm.t[0:64, h:h + 1], kin[:, s2, h * 64:(h + 1) * 64], ones[:, 0:1], s2 == 0, s2 == 1,
                              R=[kinK, ones], W=[pm])
                kb.act(kmT[:, :, n], pm.t[0:64, 0:4], AF.Copy, R=[pm], W=[kmT], scale=1.0 / 256.0)
                for s2 in range(2):
                    kt = n * 2 + s2
                    kb_ = kbb[kt % 2]
                    kb.cp("dve", kb_[:], kin[:, s2, 0:256], R=[kinK], W=[kb_])
                    pt = kb.ps()
                    pbv = pt.t[:].bitcast(BF16)
                    for h in range(4):
                        kb.tr(pbv[0:64, h * 128:(h + 1) * 128], kb_[:, h * 64:(h + 1) * 64], identb[:],
                              R=[kb_, identb], W=[pt])
                    kb.cp("act", KT[0:64, :, kt * 128:(kt + 1) * 128],
                          pbv[0:64, 0:512].rearrange("p (h t) -> p h t", h=4), R=[pt], W=[KT])
                    kb.cp("dve", Vaug[:, kt, :, 0:64], kin[:, s2, 256:512].rearrange("p (h d) -> p h d", h=4),
                          R=[kinV], W=[Vaug])

            S.barrier()
            for j in range(NTILES // 2):
                for s_ in range(4):
                    S.dma("pool", (lambda o_, c_: (lambda e: e.indirect_dma_start(
                        out=o_, out_offset=None, in_=x2d_t,
                        in_offset=bass.IndirectOffsetOnAxis(ap=x2idx[:, c_:c_ + 1], axis=0))))(xt[:, s_, :], j * 4 + s_),
                        R=[x2B, x2idx], W=[xs_[s_]])
                norm_T(3)
                wq = []
                for b in range(2):
                    wv = wnext("q", 8, 512)
                    wq.append((wv, wv_buf[0]))
                def q_front(s_):
                    gi_ = j * 4 + s_
                    tb = rtq[s_ % 2]
                    kb.dma("sp", tb[:], c_ropeq[gi_ * 128:(gi_ + 1) * 128, :], W=[tb])
                    qraw = kin2[s_ % 2]
                    for b in range(2):
                        wv_buf[0] = wq[b][1]
                        p = proj_tok(xnT, 8, wq[b][0], 512, s_)
                        kb.cp("act", qraw[:, b, :], p.t[:, 0:512], R=[p], W=[qraw])
                    qr_ = qrs[s_ % 2]
                    rope(qr_, qr_[:], qraw, qraw[:].rearrange("p b c -> p (b c)"), NQH, tb)
                    return qr_
                qf = q_front(0)
                for s_ in range(4):
                    gi = j * 4 + s_
                    tsl = slice(s_ * 128, (s_ + 1) * 128)
                    qr = qf
                    kb.cp("act", Qaug[:, :, 0:64], qr[:].rearrange("p (h d) -> p h d", d=64), R=[qr], W=[Qaug])
                    pg = kb.psum[4]
                    for q4 in range(4):
                        pt = kb.ps()
                        for k4 in range(4):
                            h = q4 * 4 + k4
                            kb.tr(pt.t[0:64, k4 * 128:(k4 + 1) * 128], qr[:, h * 64:(h + 1) * 64], identf[:],
                                  R=[qr, identf], W=[pt])
                        kb.cp("dve", QT32[:], pt.t[0:64, 0:512].rearrange("p (h t) -> p h t", h=4), R=[pt], W=[QT32])
                        for k4 in range(4):
                            h = q4 * 4 + k4
                            kb.mm(pg.t[:, h * NBP:(h + 1) * NBP], QT32[:, k4, :], kmT[:, h // 4, :], True, True,
                                  R=[QT32, kmT], W=[pg])
                    mo = gi * 3 * NBP
                    kb.tt("dve", gm[:], pg.t[:, 0:NQH * NBP].rearrange("p (h n) -> p h n", h=NQH),
                          gmask[:, mo:mo + NBP].unsqueeze(1).to_broadcast([128, NQH, NBP]), ALU.add,
                          R=[pg, gmask], W=[gm])
                    for h in range(NQH):
                        S.op("dve", (lambda h_: (lambda e: e.max(m8[:, h_, :], gm[:, h_, :])))(h), R=[gm], W=[m8])
                    kb.tt("dve", sel[:], gm[:], m8[:, :, NSEL - 1:NSEL].to_broadcast([128, NQH, NBP]), ALU.is_ge,
                          R=[gm, m8], W=[sel])
                    kb.tt("dve", sel[:], sel[:], gmask[:, mo + NBP:mo + 2 * NBP].unsqueeze(1).to_broadcast([128, NQH, NBP]),
                          ALU.mult, R=[sel, gmask], W=[sel])
                    kb.tt("dve", sel[:], sel[:], gmask[:, mo + 2 * NBP:mo + 3 * NBP].unsqueeze(1).to_broadcast([128, NQH, NBP]),
                          ALU.add, R=[sel, gmask], W=[sel])
                    kb.ts("dve", Qaug[:, :, 64:KR], sel[:], -1.0, -NEG, ALU.add, ALU.mult, R=[sel], W=[Qaug])
                    for kv in range(4):
                        pt = kb.ps()
                        pbv = pt.t[:].bitcast(BF16)
                        for g in range(4):
                            h = kv * 4 + g
                            kb.tr(pbv[0:KR, g * 128:(g + 1) * 128], Qaug[:, h, :], identb[:], R=[Qaug, identb], W=[pt])
                        kb.cp("act", QTaug[0:KR, kv * 4:(kv + 1) * 4, :],
                              pbv[0:KR, 0:512].rearrange("p (g t) -> p g t", g=4), R=[pt], W=[QTaug])
                    if s_ + 1 < 4:
                        qf = q_front(s_ + 1)
                    nkt = 2 * gi + 2
                    mkb = mk[gi % 2]
                    if gi == 0:
                        kb.dma("sp", mkb[:], c_masks[0], W=[mkb])
                    if gi + 1 < NSTEP:
                        kb.dma("sp", mk[(gi + 1) % 2][:], c_masks[gi + 1], W=[mk[(gi + 1) % 2]])
                    for kv in range(4):
                        pO = kb.psum[6 + (kv % 2)]
                        rq = QTaug[:, kv * 4:(kv + 1) * 4, :]

                        def s_mm(kt):
                            pS_ = kb.ps()
                            msk = kt >= 2 * gi
                            kb.mm(pS_.t[:, 0:512], KT[:, kv, kt * 128:(kt + 1) * 128], rq, True, not msk,
                                  R=[KT, QTaug], W=[pS_])
                            if msk:
                                m_ = kt - 2 * gi
                                kb.mm(pS_.t[:, 0:512], identb[:], mkb[:, m_ * 512:(m_ + 1) * 512], False, True,
                                      R=[identb, mkb], W=[pS_])
                            return pS_
                        pS_cur = s_mm(0)
                        for kt in range(nkt):
                            pS_next = s_mm(kt + 1) if kt + 1 < nkt else None
                            ptb = PT[pi_ % 3]
                            pi_ += 1
                            kb.act(ptb[:], pS_cur.t[:, 0:512], AF.Exp, R=[pS_cur], W=[ptb])
                            kb.mm(pO.t[:, 0:512], Vaug[:, kt, kv, :], ptb[:], kt == 0, kt == nkt - 1,
                                  R=[Vaug, ptb], W=[pO])
                            pS_cur = pS_next
                        S.op("dve", lambda e, pO=pO: e.reciprocal(rd[:], pO.t[64:128, 0:512]), R=[pO], W=[rd])
                        pov = pO.t[0:64, 0:512].rearrange("p (gp two t) -> p gp two t", two=2, t=128)
                        rdv = rd[:].rearrange("p (gp two t) -> p gp two t", two=2, t=128)
                        for two in range(2):
                            kb.tt("dve", AT[two * 64:(two + 1) * 64, kv * 2:kv * 2 + 2, tsl], pov[:, :, two, :],
                                  rdv[:, :, two, :], ALU.mult, R=[pO, rd], W=[AT])
                if DEBUG:
                    kb.dma("sp", dbg_at[j], AT[:], R=[AT])
                for b in range(2):
                    wv = wnext("o", 8, 512)
                    for s_ in range(4):
                        p = proj_tok(AT, 8, wv, 512, s_)
                        kb.tt("dve", xt[:, s_, b * 512:(b + 1) * 512], xt[:, s_, b * 512:(b + 1) * 512],
                              p.t[:, 0:512], ALU.add, R=[xs_[s_], p], W=[xs_[s_]])
                ffn(1, 4)
                for s_ in range(4):
                    kb.act(xnb[s_ % 2][:], xt[:, s_, :], AF.Square, R=[xs_[s_]], W=[xnb[s_ % 2], ms], scale=1.0 / 32.0,
                           accum=ms[:, s_:s_ + 1])
                kb.act(rs[:, 0:4], ms[:, 0:4], AF.Ln, R=[ms], W=[rs], bias=EPS)
                kb.act(rs[:, 0:4], rs[:, 0:4], AF.Exp, R=[rs], W=[rs], scale=-0.5)
                for s_ in range(4):
                    gi = j * 4 + s_
                    yb = kin2[s_ % 2]
                    ybv = yb[:].rearrange("p b c -> p (b c)")
                    kb.stt("dve", ybv, xt[:, s_, :], rs[:, s_:s_ + 1], gfin[:], ALU.mult, ALU.mult,
                           R=[xs_[s_], rs, gfin], W=[yb])
                    kb.dma("sp", yp[gi * 128:(gi + 1) * 128, :], ybv, R=[yb])
            S.barrier()
        esP.close()
        if with_sample:
            esS = ExitStack()

            def sbS(name, shape, dt):
                return Buf(esS.enter_context(nc.sbuf_tensor("sS_" + name, list(shape), dt)), name)
            xt = sbS("xt", [16, 1, D], F32)
            xs_ = [xt]
            xnb = [sbS("xnb0", [16, D], BF16)]
            xnT = sbS("xnT", [128, 8, 16], BF16)
            junk = sbS("junk", [16, D], BF16)
            ms = sbS("ms", [16, 8], F32)
            rs = sbS("rs", [16, 8], F32)
            hT = sbS("hT", [128, 22, 16], BF16)
            cst = [sbS(f"cst{i}", [16, 515], F32) for i in range(2)]
            cacc = [sbS(f"cacc{i}", [128, 16], F32) for i in range(2)]
            NPG = NPAST // 128
            NPB = NPAST // 256
            KRS = 64 + NBPS
            NKS = NPAST + 128
            U32 = mybir.dt.uint32
            kb.NROT = 4
            ksB = Buf(ks, "ks_dram")
            vsB = Buf(vs, "vs_dram")
            esA = ExitStack()
            with esA:
                def sbA(name, shape, dt):
                    return Buf(esA.enter_context(nc.sbuf_tensor("sA_" + name, list(shape), dt)), name)
                gssm = sbA("gssm", [16, DIN], F32)
                cw = sbA("cw", [128, 24, 4], F32)
                cbias = sbA("cbias", [128, 24], F32)
                dtb = sbA("dtb", [16, NH], F32)
                arep = sbA("arep", [16, NH], F32)
                dsk = sbA("dsk", [16, NH], F32)
                i16 = sbA("i16", [128, 256], F32)
                selm = sbA("selm", [16, 16 * 128], F32)
                kb.dma("sp", gssm[:], g_ssm[0:16, :], W=[gssm])
                kb.dma("sp", cw[:], cw_d, W=[cw])
                kb.dma("sp", cbias[:], cb_d, W=[cbias])
                kb.dma("sp", dtb[:], dtb_d[0:16, :], W=[dtb])
                kb.dma("sp", arep[:], alog_d[0:16, :], W=[arep])
                kb.dma("sp", dsk[:], dsk_d[0:16, :], W=[dsk])
                kb.dma("sp", i16[:], c_i16, W=[i16])
                kb.dma("sp", selm[:], c_sel, W=[selm])
                kb.act(arep[:], arep[:], AF.Exp, R=[arep], W=[arep])
                kb.ts("dve", arep[:], arep[:], -1.0, None, ALU.mult, None, R=[arep], W=[arep])
                zs_s = sbA("zs_s", [16, DIN], F32)
                sconv_sb = sbA("sconv_sb", [12, CD], F32)
                c16 = [sbA(f"c16_{i}", [16, 512], F32) for i in range(2)]
                stg = [sbA(f"stg{i}", [128, 4, 7], F32) for i in range(2)]
                ca4 = [sbA(f"ca4_{i}", [128, 4, 4], F32) for i in range(2)]
                xf = [sbA(f"xf{i}", [128, 16], F32) for i in range(2)]
                x_tok_s = sbA("x_tok_s", [16, DIN], F32)
                xdt_tok = sbA("xdt_tok", [16, DIN], F32)
                BT_s = sbA("BT_s", [128, 4, 16], F32)
                CT_s = sbA("CT_s", [128, 4, 16], F32)
                CTm = sbA("CTm", [128, 4, 16, 16], F32)
                dtt_s = sbA("dtt_s", [16, NH], F32)
                dA = sbA("dA", [16, NH], F32)
                decb = sbA("decb", [128, NH], F32)
                hs_nat = sbA("hs_nat", [128, 16, 128], F32)
                h32_s = sbA("h32_s", [128, DIN], F32)
                y_s = sbA("y_s", [16, DIN], F32)
                yn_s = sbA("yn_s", [16, DIN], BF16)
                gms_s = sbA("gms_s", [16, 4], F32)
                kvs = sbA("kvs", [16, 512], F32)
                krs = sbA("krs", [16, 256], F32)
                rts = sbA("rts", [16, 64], F32)
                print("sampleA sbuf remaining", nc.sbuf_bytes_remaining)
                kb.dma("sp", xt[0:16, 0, :], xs, W=[xs_[0]])
                kb.dma("sp", sconv_sb[:], sconv, W=[sconv_sb])
                norm_T(0, 1, 16)
                for b in range(4):
                    wv = wnext("z", 8, 512)
                    p = proj_tok(xnT, 8, wv, 512, 0, 16)
                    kb.act(zs_s[:, b * 512:(b + 1) * 512], p.t[0:16, 0:512], AF.Silu, R=[p], W=[zs_s])
                for b in range(6):
                    wv = wnext("x", 8, 512)
                    p = proj_tok(xnT, 8, wv, 512, 0, 16)
                    cc = c16[b % 2]
                    kb.cp("act", cc[:], p.t[0:16, 0:512], R=[p], W=[cc])
                    for sq in range(4):
                        kb.dma("sp", convs[sq * 3:(sq + 1) * 3, b * 512:(b + 1) * 512], cc[sq * 4 + 1:sq * 4 + 4, :], R=[cc])
                    for ct in range(4):
                        ci = b * 4 + ct
                        p = proj_feat(xnT, 8, wv, ct * 128, 16)
                        pst = kb.ps()
                        kb.tr(pst.t[:, 0:12], sconv_sb[0:12, ci * 128:(ci + 1) * 128], identf[0:12, 0:12],
                              R=[sconv_sb, identf], W=[pst])
                        st_ = stg[ci % 2]
                        ca = ca4[ci % 2]
                        kb.cp("dve", st_[:, :, 0:3], pst.t[:, 0:12].rearrange("p (s r) -> p s r", s=4), R=[pst], W=[st_])
                        kb.cp("act", st_[:, :, 3:7], p.t[:, 0:16].rearrange("p (s r) -> p s r", s=4), R=[p], W=[st_])
                        kb.ts("dve", ca[:], st_[:, :, 3:7], cw[:, ci, 3:4], cbias[:, ci:ci + 1], ALU.mult, ALU.add,
                              R=[st_, cw, cbias], W=[ca])
                        for tap in (2, 1, 0):
                            kb.stt("dve", ca[:], st_[:, :, tap:tap + 4], cw[:, ci, tap:tap + 1], ca[:],
                                   ALU.mult, ALU.add, R=[st_, cw, ca], W=[ca])
                        cav = ca[:].rearrange("p s r -> p (s r)")
                        if ci < 16:
                            xf_ = xf[ci % 2]
                            kb.act(xf_[:], cav, AF.Silu, R=[ca], W=[xf_])
                            pt = kb.ps()
                            kb.tr(pt.t[0:16, 0:128], xf_[:], identf[:], R=[xf_, identf], W=[pt])
                            kb.cp("dve", x_tok_s[:, ci * 128:(ci + 1) * 128], pt.t[0:16, 0:128], R=[pt], W=[x_tok_s])
                        elif ci < 20:
                            kb.act(BT_s[:, ci - 16, :], cav, AF.Silu, R=[ca], W=[BT_s])
                        else:
                            kb.act(CT_s[:, ci - 20, :], cav, AF.Silu, R=[ca], W=[CT_s])
                wv = wnext("dt", 8, 32)
                p = proj_tok(xnT, 8, wv, 32, 0, 16)
                kb.tt("dve", dtt_s[:], p.t[0:16, 0:32], dtb[:], ALU.add, R=[p, dtb], W=[dtt_s])
                kb.act(dtt_s[:], dtt_s[:], AF.Exp, R=[dtt_s], W=[dtt_s])
                kb.act(dtt_s[:], dtt_s[:], AF.Ln, R=[dtt_s], W=[dtt_s], bias=1.0)
                kb.tt("dve", dA[:], dtt_s[:], arep[:], ALU.mult, R=[dtt_s, arep], W=[dA])
                kb.tt("dve", xdt_tok[:].rearrange("p (h d) -> p h d", d=HP), x_tok_s[:].rearrange("p (h d) -> p h d", d=HP),
                      dtt_s[:].unsqueeze(2).to_broadcast([16, NH, HP]), ALU.mult, R=[x_tok_s, dtt_s], W=[xdt_tok])
                kb.tt("dve", CTm[:], CT_s[:].unsqueeze(3).to_broadcast([128, 4, 16, 16]),
                      i16[:].rearrange("p (a b) -> p a b", a=16).unsqueeze(1).to_broadcast([128, 4, 16, 16]), ALU.mult,
                      R=[CT_s, i16], W=[CTm])
                pY = [kb.psum[4 + g] for g in range(4)]
                for sq in range(4):
                    kb.dma("sp", hs_nat[:], sssm[sq].rearrange("(k p) n -> p k n", p=128), W=[hs_nat])
                    for q4 in range(4):
                        pt = kb.ps()
                        for k4 in range(4):
                            k = q4 * 4 + k4
                            kb.tr(pt.t[:, k4 * 128:(k4 + 1) * 128], hs_nat[:, k, :], identf[:], R=[hs_nat, identf], W=[pt])
                        kb.cp("act", h32_s[:, q4 * 512:(q4 + 1) * 512], pt.t[:, 0:512], R=[pt], W=[h32_s])
                    for t in range(4):
                        tk = sq * 4 + t
                        sel_tk = selm[:, tk * 128:(tk + 1) * 128]
                        pd = kb.ps()
                        kb.mm(pd.t[:, 0:32], sel_tk, dA[:], True, True, R=[selm, dA], W=[pd])
                        kb.act(decb[:], pd.t[:, 0:32], AF.Exp, R=[pd], W=[decb])
                        for g in range(4):
                            px = kb.ps()
                            kb.mm(px.t[:, 0:512], sel_tk, xdt_tok[:, g * 512:(g + 1) * 512], True, True,
                                  R=[selm, xdt_tok], W=[px])
                            hg = h32_s[:, g * 512:(g + 1) * 512]
                            hv = hg.rearrange("p (h d) -> p h d", d=HP)
                            kb.tt("dve", hv, hv, decb[:, g * 8:(g + 1) * 8].unsqueeze(2).to_broadcast([128, 8, HP]),
                                  ALU.mult, R=[h32_s, decb], W=[h32_s])
                            kb.stt("dve", hg, px.t[:, 0:512], BT_s[:, g, tk:tk + 1], hg, ALU.mult, ALU.add,
                                   R=[px, BT_s, h32_s], W=[h32_s])
                            kb.mm(pY[g].t[0:16, 0:512], CTm[:, g, tk, :], hg, tk == 0, tk == 15,
                                  R=[CTm, h32_s], W=[pY[g]])
                    for q4 in range(4):
                        pt = kb.ps()
                        for k4 in range(4):
                            k = q4 * 4 + k4
                            kb.tr(pt.t[:, k4 * 128:(k4 + 1) * 128], h32_s[:, k * 128:(k + 1) * 128], identf[:],
                                  R=[h32_s, identf], W=[pt])
                        kb.cp("act", hs_nat[:, q4 * 4:(q4 + 1) * 4, :], pt.t[:, 0:512].rearrange("p (k n) -> p k n", k=4),
                              R=[pt], W=[hs_nat])
                    kb.dma("sp", ssms[sq].rearrange("(k p) n -> p k n", p=128), hs_nat[:], R=[hs_nat])
                for g in range(4):
                    gsl = slice(g * 512, (g + 1) * 512)
                    kb.tt("dve", y_s[:, gsl].rearrange("p (h d) -> p h d", d=HP),
                          x_tok_s[:, gsl].rearrange("p (h d) -> p h d", d=HP),
                          dsk[:, g * 8:(g + 1) * 8].unsqueeze(2).to_broadcast([16, 8, HP]), ALU.mult,
                          R=[x_tok_s, dsk], W=[y_s])
                    kb.tt("dve", y_s[:, gsl], y_s[:, gsl], pY[g].t[0:16, 0:512], ALU.add, R=[y_s, pY[g]], W=[y_s])
                    kb.tt("dve", y_s[:, gsl], y_s[:, gsl], zs_s[:, gsl], ALU.mult, R=[y_s, zs_s], W=[y_s])
                    kb.act(junk[0:16, 0:512], y_s[:, gsl], AF.Square, R=[y_s], W=[junk, gms_s],
                           scale=float(512 ** -0.5), accum=gms_s[:, g:g + 1])
                kb.act(gms_s[:], gms_s[:], AF.Ln, R=[gms_s], W=[gms_s], bias=EPS)
                kb.act(gms_s[:], gms_s[:], AF.Exp, R=[gms_s], W=[gms_s], scale=-0.5)
                for g in range(4):
                    gsl = slice(g * 512, (g + 1) * 512)
                    kb.stt("dve", yn_s[:, gsl], y_s[:, gsl], gms_s[:, g:g + 1], gssm[:, gsl], ALU.mult, ALU.mult,
                           R=[y_s, gms_s, gssm], W=[yn_s])
                ynT_s = hT
                for half in range(2):
                    pt = kb.ps()
                    pbv = pt.t[:].bitcast(BF16)
                    for k8 in range(8):
                        kc = half * 8 + k8
                        kb.tr(pbv[:, k8 * 128:k8 * 128 + 16], yn_s[:, kc * 128:(kc + 1) * 128], identb[0:16, 0:16],
                              R=[yn_s, identb], W=[pt])
                    kb.cp("act", ynT_s[:, half * 8:(half + 1) * 8, 0:16],
                          pbv[:, 0:1024].rearrange("p (k t) -> p k t", k=8)[:, :, 0:16], R=[pt], W=[hT])
                for hf in range(2):
                    p = kb.ps()
                    for kbk in range(2):
                        wv = wnext("out", 8, 512)
                        for kc in range(8):
                            kcg = kbk * 8 + kc
                            kb.mm(p.t[0:16, 0:512], ynT_s[:, kcg, 0:16], wv[:, kc, :], kcg == 0, kcg == 15,
                                  R=[hT, wv_buf[0]], W=[p])
                    kb.tt("dve", xt[0:16, 0, hf * 512:(hf + 1) * 512], xt[0:16, 0, hf * 512:(hf + 1) * 512],
                          p.t[0:16, 0:512], ALU.add, R=[xs_[0], p], W=[xs_[0]])
                ffn(0, 1, 1, 16)
                norm_T(2, 1, 16)
                wv = wnext("kv", 8, 512)
                p = proj_tok(xnT, 8, wv, 512, 0, 16)
                kb.dma("sp", rts[:], c_rope_s, W=[rts])
                kb.cp("act", kvs[:], p.t[0:16, 0:512], R=[p], W=[kvs])
                rope(krs, krs[:], kvs, kvs[:, 0:256], 4, rts)
                kb.dma("sp", ks, krs[:], R=[krs], W=[ksB])
                kb.dma("sp", vs, kvs[:, 256:512], R=[kvs], W=[vsB])
                S.barrier()
            S.barrier()
            esB = ExitStack()
            with esB:
                def sbB(name, shape, dt):
                    return Buf(esB.enter_context(nc.sbuf_tensor("sB_" + name, list(shape), dt)), name)
                gfin_s = sbB("gfin_s", [16, D], F32)
                kb.dma("sp", gfin_s[:], g_all[5, 0:16, :], W=[gfin_s])
                KT_s = sbB("KT_s", [KRS, 4, NKS], BF16)
                Vaug_s = sbB("Vaug_s", [128, NPG + 1, 4, 65], BF16)
                kmT_s = sbB("kmT_s", [64, 4, NBPS], F32)
                gmask_s = sbB("gmask_s", [128, 3 * NBPS], F32)
                trimask_s = sbB("trimask_s", [128, 16], BF16)
                ptab_i = sbB("ptab_i", [128, 4 * NPG], I32)
                pidx = sbB("pidx", [128, 1], F32)
                idxf = sbB("idxf", [128, 4 * NPG], F32)
                idxu = sbB("idxu", [128, 4 * NPG], U32)
                kpg = [[sbB(f"kpg{i}_{j}", [128, 256], F32) for j in range(2)] for i in range(4)]
                vpg = [[sbB(f"vpg{i}_{j}", [128, 256], F32) for j in range(2)] for i in range(4)]
                kbb_s = [sbB(f"kbb_s{i}", [128, 256], BF16) for i in range(2)]
                qraw_s = sbB("qraw_s", [16, D], F32)
                qr_s = sbB("qr_s", [16, D], F32)
                rtq_s = sbB("rtq_s", [16, 64], F32)
                q4 = sbB("q4", [4, D], F32)
                knew = sbB("knew", [4, 512], F32)
                knb = sbB("knb", [4, 256], BF16)
                Qaug_s = sbB("Qaug_s", [4, NQH, KRS], BF16)
                QT32_s = sbB("QT32_s", [64, 4, 4], F32)
                QTaug_s = sbB("QTaug_s", [KRS, NQH, 4], BF16)
                gm_s = sbB("gm_s", [4, NQH, NBPS], F32)
                m8_s = sbB("m8_s", [4, NQH, 8], F32)
                sel_s = sbB("sel_s", [4, NQH, NBPS], F32)
                PT_s = [sbB(f"PT_s{i}", [128, 512], BF16) for i in range(2)]
                on_s = sbB("on_s", [64, 16], F32)
                rden_s = sbB("rden_s", [65, 16], F32)
                AT_s = sbB("AT_s", [128, 8, 16], BF16)
                yo_s = sbB("yo_s", [16, D], F32)
                print("sampleB sbuf remaining", nc.sbuf_bytes_remaining)
                kb.dma("sp", gmask_s[:], c_gmask_s, W=[gmask_s])
                kb.dma("sp", trimask_s[:], c_trimask_s, W=[trimask_s])
                kb.dma("sp", pidx[:], c_pidx, W=[pidx])
                kb.dma("sp", ptab_i[:], ptab.rearrange("b j -> (b j)").partition_broadcast(128), W=[ptab_i])
                kb.ts("dve", idxf[:], ptab_i[:], 128.0, pidx[:, 0:1], ALU.mult, ALU.add, R=[ptab_i, pidx], W=[idxf])
                kb.cp("dve", idxu[:], idxf[:], R=[idxf], W=[idxu])
                kb.memset("pool", kmT_s[:], 0.0, W=[kmT_s])
                kb.memset("pool", Vaug_s[:], 1.0, W=[Vaug_s])
                for kv in range(4):
                    kb.dma("sp", KT_s[64:64 + NBPS, kv, :], c_onehot_s, W=[KT_s])
                norm_T(3, 1, 16)
                for b in range(2):
                    wv = wnext("q", 8, 512)
                    p = proj_tok(xnT, 8, wv, 512, 0, 16)
                    kb.cp("act", qraw_s[:, b * 512:(b + 1) * 512], p.t[0:16, 0:512], R=[p], W=[qraw_s])
                kb.dma("sp", rtq_s[:], c_ropeq_s, W=[rtq_s])
                rope(qr_s, qr_s[:], qraw_s, qraw_s[:], NQH, rtq_s)
                pi2 = 0
                for sq in range(4):
                    for n in range(NPB):
                        kin = kpg[(sq * NPB + n) % 4]
                        vin = vpg[(sq * NPB + n) % 4]
                        for s2 in range(2):
                            col = sq * NPG + n * 2 + s2
                            S.dma("pool", (lambda o_, c_: (lambda e: e.indirect_dma_start(
                                out=o_, out_offset=None, in_=cache_k,
                                in_offset=bass.IndirectOffsetOnAxis(ap=idxu[:, c_:c_ + 1], axis=0))))(kin[s2][:], col),
                                R=[idxu], W=[kin[s2]])
                            S.dma("pool", (lambda o_, c_: (lambda e: e.indirect_dma_start(
                                out=o_, out_offset=None, in_=cache_v,
                                in_offset=bass.IndirectOffsetOnAxis(ap=idxu[:, c_:c_ + 1], axis=0))))(vin[s2][:], col),
                                R=[idxu], W=[vin[s2]])
                        pm = kb.ps()
                        for h in range(4):
                            for s2 in range(2):
                                kb.mm(pm.t[0:64, h:h + 1], kin[s2][:, h * 64:(h + 1) * 64], ones[:, 0:1], s2 == 0, s2 == 1,
                                      R=[kin[s2], ones], W=[pm])
                        kb.act(kmT_s[:, :, n], pm.t[0:64, 0:4], AF.Copy, R=[pm], W=[kmT_s], scale=1.0 / 256.0)
                        for s2 in range(2):
                            kt = n * 2 + s2
                            kb_ = kbb_s[kt % 2]
                            kb.cp("dve", kb_[:], kin[s2][:], R=[kin[s2]], W=[kb_])
                            pt = kb.ps()
                            pbv = pt.t[:].bitcast(BF16)
                            for h in range(4):
                                kb.tr(pbv[0:64, h * 128:(h + 1) * 128], kb_[:, h * 64:(h + 1) * 64], identb[:],
                                      R=[kb_, identb], W=[pt])
                            kb.cp("act", KT_s[0:64, :, kt * 128:(kt + 1) * 128],
                                  pbv[0:64, 0:512].rearrange("p (h t) -> p h t", h=4), R=[pt], W=[KT_s])
                            kb.cp("dve", Vaug_s[:, kt, :, 0:64], vin[s2][:].rearrange("p (h d) -> p h d", h=4),
                                  R=[vin[s2]], W=[Vaug_s])
                    kb.dma("sp", knew[:, 0:256], ks[sq * 4:(sq + 1) * 4, :], R=[ksB], W=[knew])
                    kb.dma("sp", knew[:, 256:512], vs[sq * 4:(sq + 1) * 4, :], R=[vsB], W=[knew])
                    kb.cp("dve", knb[:], knew[:, 0:256], R=[knew], W=[knb])
                    pt = kb.ps()
                    pbv = pt.t[:].bitcast(BF16)
                    for h in range(4):
                        kb.tr(pbv[0:64, h * 128:h * 128 + 4], knb[:, h * 64:(h + 1) * 64], identb[0:4, 0:4],
                              R=[knb, identb], W=[pt])
                    kb.cp("act", KT_s[0:64, :, NPAST:NPAST + 4],
                          pbv[0:64, 0:512].rearrange("p (h t) -> p h t", h=4)[:, :, 0:4], R=[pt], W=[KT_s])
                    kb.cp("dve", Vaug_s[0:4, NPG, :, 0:64], knew[:, 256:512].rearrange("p (h d) -> p h d", h=4),
                          R=[knew], W=[Vaug_s])
                    kb.dma("sp", q4[:], qr_s[sq * 4:(sq + 1) * 4, :], R=[qr_s], W=[q4])
                    kb.cp("act", Qaug_s[:, :, 0:64], q4[:].rearrange("p (h d) -> p h d", d=64), R=[q4], W=[Qaug_s])
                    pgs = [kb.psum[4], kb.psum[5]]
                    for q4i in range(4):
                        pt = kb.ps()
                        for k4 in range(4):
                            h = q4i * 4 + k4
                            kb.tr(pt.t[0:64, k4 * 4:(k4 + 1) * 4], q4[:, h * 64:(h + 1) * 64], identf[0:4, 0:4],
                                  R=[q4, identf], W=[pt])
                        kb.cp("dve", QT32_s[:], pt.t[0:64, 0:16].rearrange("p (h t) -> p h t", h=4), R=[pt], W=[QT32_s])
                        for k4 in range(4):
                            h = q4i * 4 + k4
                            pg = pgs[h // 8]
                            kb.mm(pg.t[0:4, (h % 8) * NBPS:(h % 8 + 1) * NBPS], QT32_s[:, k4, :], kmT_s[:, h // 4, :],
                                  True, True, R=[QT32_s, kmT_s], W=[pg])
                    for hf in range(2):
                        kb.tt("dve", gm_s[:, hf * 8:(hf + 1) * 8, :],
                              pgs[hf].t[0:4, 0:8 * NBPS].rearrange("p (h n) -> p h n", h=8),
                              gmask_s[0:4, 0:NBPS].unsqueeze(1).to_broadcast([4, 8, NBPS]), ALU.add,
                              R=[pgs[hf], gmask_s], W=[gm_s])
                    for h in range(NQH):
                        S.op("dve", (lambda h_: (lambda e: e.max(m8_s[:, h_, :], gm_s[:, h_, :])))(h), R=[gm_s], W=[m8_s])
                    kb.tt("dve", sel_s[:], gm_s[:], m8_s[:, :, TOPK - 1:TOPK].to_broadcast([4, NQH, NBPS]), ALU.is_ge,
                          R=[gm_s, m8_s], W=[sel_s])
                    kb.tt("dve", sel_s[:], sel_s[:], gmask_s[0:4, NBPS:2 * NBPS].unsqueeze(1).to_broadcast([4, NQH, NBPS]),
                          ALU.mult, R=[sel_s, gmask_s], W=[sel_s])
                    kb.tt("dve", sel_s[:], sel_s[:], gmask_s[0:4, 2 * NBPS:3 * NBPS].unsqueeze(1).to_broadcast([4, NQH, NBPS]),
                          ALU.add, R=[sel_s, gmask_s], W=[sel_s])
                    kb.ts("dve", Qaug_s[:, :, 64:KRS], sel_s[:], -1.0, -NEG, ALU.add, ALU.mult, R=[sel_s], W=[Qaug_s])
                    pt = kb.ps()
                    pbv = pt.t[:].bitcast(BF16)
                    for h in range(NQH):
                        kb.tr(pbv[0:KRS, h * 4:(h + 1) * 4], Qaug_s[:, h, :], identb[0:4, 0:4], R=[Qaug_s, identb], W=[pt])
                    kb.cp("act", QTaug_s[:], pbv[0:KRS, 0:64].rearrange("p (h t) -> p h t", h=NQH), R=[pt], W=[QTaug_s])
                    for kv in range(4):
                        pO = kb.psum[6 + (kv % 2)]
                        rq = QTaug_s[:, kv * 4:(kv + 1) * 4, :]
                        kt0 = 0
                        while kt0 < NPG:
                            nk = min(32, NPG - kt0)
                            pS = kb.ps()
                            for k_ in range(nk):
                                kt = kt0 + k_
                                kb.mm(pS.t[:, k_ * 16:(k_ + 1) * 16], KT_s[:, kv, kt * 128:(kt + 1) * 128], rq, True, True,
                                      R=[KT_s, QTaug_s], W=[pS])
                            ptb = PT_s[pi2 % 2]
                            pi2 += 1
                            kb.act(ptb[:, 0:nk * 16], pS.t[:, 0:nk * 16], AF.Exp, R=[pS], W=[ptb])
                            for k_ in range(nk):
                                kt = kt0 + k_
                                kb.mm(pO.t[0:65, 0:16], Vaug_s[:, kt, kv, :], ptb[:, k_ * 16:(k_ + 1) * 16], kt == 0, False,
                                      R=[Vaug_s, ptb], W=[pO])
                            kt0 += nk
                        pS = kb.ps()
                        kb.mm(pS.t[0:4, 0:16], KT_s[:, kv, NPAST:NPAST + 4], rq, True, False, R=[KT_s, QTaug_s], W=[pS])
                        kb.mm(pS.t[0:4, 0:16], identb[0:4, 0:4], trimask_s[0:4, :], False, True, R=[identb, trimask_s], W=[pS])
                        ptb = PT_s[pi2 % 2]
                        pi2 += 1
                        kb.act(ptb[0:4, 0:16], pS.t[0:4, 0:16], AF.Exp, R=[pS], W=[ptb])
                        kb.mm(pO.t[0:65, 0:16], Vaug_s[0:4, NPG, kv, :], ptb[0:4, 0:16], False, True, R=[Vaug_s, ptb], W=[pO])
                        kb.cp("act", on_s[:], pO.t[0:64, 0:16], R=[pO], W=[on_s])
                        S.op("dve", lambda e, pO=pO: e.reciprocal(rden_s[64:65, :], pO.t[64:65, 0:16]), R=[pO], W=[rden_s])
                        pB = kb.ps()
                        kb.mm(pB.t[0:64, 0:16], ones[64:65, 0:64], rden_s[64:65, :], True, True, R=[ones, rden_s], W=[pB])
                        onv = on_s[:].rearrange("p (gp two t) -> p gp two t", two=2, t=4)
                        pbv2 = pB.t[0:64, 0:16].rearrange("p (gp two t) -> p gp two t", two=2, t=4)
                        for two in range(2):
                            kb.tt("dve", AT_s[two * 64:(two + 1) * 64, kv * 2:kv * 2 + 2, sq * 4:(sq + 1) * 4],
                                  onv[:, :, two, :], pbv2[:, :, two, :], ALU.mult, R=[on_s, pB], W=[AT_s])
                for b in range(2):
                    wv = wnext("o", 8, 512)
                    p = proj_tok(AT_s, 8, wv, 512, 0, 16)
                    kb.tt("dve", xt[0:16, 0, b * 512:(b + 1) * 512], xt[0:16, 0, b * 512:(b + 1) * 512],
                          p.t[0:16, 0:512], ALU.add, R=[xs_[0], p], W=[xs_[0]])
                ffn(1, 4, 1, 16)
                kb.act(xnb[0][0:16, :], xt[0:16, 0, :], AF.Square, R=[xs_[0]], W=[xnb[0], ms], scale=1.0 / 32.0, accum=ms[0:16, 0:1])
                kb.act(rs[0:16, 0:1], ms[0:16, 0:1], AF.Ln, R=[ms], W=[rs], bias=EPS)
                kb.act(rs[0:16, 0:1], rs[0:16, 0:1], AF.Exp, R=[rs], W=[rs], scale=-0.5)
                kb.stt("dve", yo_s[:], xt[0:16, 0, :], rs[0:16, 0:1], gfin_s[:], ALU.mult, ALU.mult,
                       R=[xs_[0], rs, gfin_s], W=[yo_s])
                kb.dma("sp", ys, yo_s[:], R=[yo_s])
                S.barrier()
            esS.close()
        S.finish()
    return nc


def _blockify(W, col_starts, cols, kp=128):
    K = W.shape[0]
    KC = K // kp
    out = np.empty((len(col_starts), kp, KC * cols), np.float32)
    for b, c0 in enumerate(col_starts):
        blk = W[:, c0:c0 + cols].reshape(KC, kp, cols).transpose(1, 0, 2)
        out[b] = blk.reshape(kp, KC * cols)
    return out


def _rep(v, n=128):
    return np.ascontiguousarray(np.broadcast_to(np.asarray(v, np.float32)[None], (n,) + tuple(np.shape(v))))


def _sub_of(step, role):
    lo, hi = 2 * step, 2 * step + 1
    a = lo if step % 2 == 0 else hi
    return a if role == 0 else (lo + hi - a)


def _consts(NT, NBP, role=0, pos0=0):
    NSUB = NT // 128
    NSTEP = NSUB // 2
    NB = NT // 256
    bf = ml_dtypes.bfloat16
    c = {}
    c["c_identf"] = np.eye(128, dtype=np.float32)
    r = np.arange(128)
    c["c_tri"] = (r[:, None] <= r[None, :]).astype(np.float32)
    c["c_ones"] = np.ones((128, 128), np.float32)
    c["c_mneg"] = np.where(r[None, :] < r[:, None], -1.0e6, 0.0).astype(np.float32).astype(bf)
    c["c_identb"] = np.eye(128).astype(bf)
    tm = np.where(r[:, None] > r[None, :], NEG, 0.0).astype(np.float32)
    c["c_trimask"] = np.tile(tm, (1, 4)).astype(bf)
    oh = np.zeros((NBP, NT), np.float32)
    for n in range(NB):
        oh[n, n * 256:(n + 1) * 256] = 1.0
    c["c_onehot"] = oh.astype(bf)
    half = 32
    inv = (np.float32(10000.0) ** (-np.arange(half, dtype=np.float32) / np.float32(half))).astype(np.float32)
    pos = (pos0 + np.arange(NT)).astype(np.float32)
    ang = (pos[:, None] * inv[None, :]).astype(np.float32)
    tab = np.concatenate([np.cos(ang), np.sin(ang)], axis=1).astype(np.float32)
    c["c_rope"] = tab
    subs = [_sub_of(i, role) for i in range(NSTEP)]
    rows = np.concatenate([np.arange(sb_ * 128, (sb_ + 1) * 128) for sb_ in subs])
    c["c_ropeq"] = np.ascontiguousarray((tab * np.float32(0.125)).astype(np.float32)[rows])
    c["c_x2idx"] = np.ascontiguousarray(rows.reshape(NSTEP, 128).T.astype(np.uint32))
    gmk = np.zeros((NSTEP, 3, NBP), np.float32)
    nn = np.arange(NBP)
    for i in range(NSTEP):
        qb = i
        gmk[i, 0] = np.where(nn < qb, 0.0, -1e30)
        gmk[i, 1] = (nn < qb).astype(np.float32)
        gmk[i, 2] = (nn == qb).astype(np.float32)
    c["c_gmask"] = _rep(gmk.reshape(-1))
    tri4 = np.tile(tm, (1, 4))
    negf = np.full((128, 512), NEG, np.float32)
    zero = np.zeros((128, 512), np.float32)
    mks = np.empty((NSTEP, 128, 1024), np.float32)
    for i in range(NSTEP):
        if subs[i] == 2 * i:
            mks[i, :, 0:512], mks[i, :, 512:1024] = tri4, negf
        else:
            mks[i, :, 0:512], mks[i, :, 512:1024] = zero, tri4
    c["c_masks"] = mks.astype(bf)
    return c


def _weights(inp):
    w = {}
    w_in = np.asarray(inp["w_in_ssm"][0], np.float32)
    w["w_z"] = _blockify(w_in, [0, 512, 1024, 1536], 512)
    w["w_x"] = _blockify(w_in, [2048 + 512 * b for b in range(6)], 512)
    w["w_dt"] = _blockify(w_in, [5120], 32)[0]
    Wout = np.asarray(inp["w_out_ssm"][0], np.float32)
    wo_ = np.empty((4, 128, 4096), np.float32)
    for hf in range(2):
        for kbk in range(2):
            wo_[hf * 2 + kbk] = _blockify(Wout[kbk * 1024:(kbk + 1) * 1024], [hf * 512], 512)[0]
    w["w_out"] = wo_
    gu = np.empty((2, 11, 128, 4096), np.float32)
    dn = np.zeros((2, 6, 128, 4096), np.float32)
    for l in range(2):
        W = np.asarray(inp["w_gu"][l], np.float32)
        for b in range(11):
            blk = np.concatenate([W[:, 256 * b:256 * b + 256], W[:, DFF + 256 * b:DFF + 256 * b + 256]], axis=1)
            gu[l, b] = _blockify(blk, [0], 512)[0]
        Wd = np.asarray(inp["w_down"][l], np.float32)
        for hf in range(2):
            for kbk in range(3):
                nk = 8 if kbk < 2 else 6
                blk = _blockify(Wd[kbk * 1024:kbk * 1024 + nk * 128], [hf * 512], 512)[0]
                dn[l, hf * 3 + kbk, :, 0:nk * 512] = blk
    w["w_gu"] = gu
    w["w_dn"] = dn
    w["w_kv"] = _blockify(np.asarray(inp["w_kv"], np.float32), [0], 512)
    w["w_q"] = _blockify(np.asarray(inp["w_q"][0], np.float32), [0, 512], 512)
    w["w_o"] = _blockify(np.asarray(inp["w_o"][0], np.float32), [0, 512], 512)
    g = np.stack([inp["norm_mix"][0], inp["norm_ffn"][0], inp["norm_kv"], inp["norm_mix"][1],
                  inp["norm_ffn"][1], inp["norm_final"]]).astype(np.float32)
    w["g_all"] = np.ascontiguousarray(np.broadcast_to(g[:, None, :], (6, 128, D)))
    w["g_allT"] = np.ascontiguousarray(g.reshape(6, 8, 128).transpose(2, 0, 1).reshape(128, 48))
    w["g_ssm"] = _rep(inp["norm_ssm"][0])
    cwv = np.asarray(inp["conv_w"][0], np.float32)
    w["cw"] = np.ascontiguousarray(cwv.reshape(4, 24, 128).transpose(2, 1, 0))
    w["cb"] = np.ascontiguousarray(np.asarray(inp["conv_b"][0], np.float32).reshape(24, 128).T)
    w["dtb"] = _rep(inp["dt_bias"][0])
    w["alog"] = _rep(inp["a_log"][0])
    w["dsk"] = _rep(inp["d_skip"][0])
    return w


_NC_CACHE = {}


def _consts_sample(NPAST):
    bf = ml_dtypes.bfloat16
    NPB = NPAST // 256
    NBPS = 8 * ((NPB + 1 + 7) // 8)
    c = {}
    half = 32
    inv = (np.float32(10000.0) ** (-np.arange(half, dtype=np.float32) / np.float32(half))).astype(np.float32)
    pos = (NPAST + (np.arange(16) % 4)).astype(np.float32)
    ang = (pos[:, None] * inv[None, :]).astype(np.float32)
    tab = np.concatenate([np.cos(ang), np.sin(ang)], axis=1).astype(np.float32)
    c["c_rope_s"] = tab
    c["c_ropeq_s"] = (tab * np.float32(0.125)).astype(np.float32)
    c["c_i16"] = np.ascontiguousarray(np.tile(np.eye(16, dtype=np.float32).reshape(1, -1), (128, 1)))
    sel = np.zeros((16, 16, 128), np.float32)
    for t in range(16):
        sel[t, t, :] = 1.0
    c["c_sel"] = sel.reshape(16, 16 * 128)
    oh = np.zeros((NBPS, NPAST + 128), np.float32)
    for n in range(NPB):
        oh[n, n * 256:(n + 1) * 256] = 1.0
    oh[NPB, NPAST:NPAST + 128] = 1.0
    c["c_onehot_s"] = oh.astype(bf)
    nn = np.arange(NBPS)
    gmk = np.stack([np.where(nn < NPB, 0.0, -1e30), (nn < NPB).astype(np.float64), (nn == NPB).astype(np.float64)])
    c["c_gmask_s"] = _rep(gmk.astype(np.float32).reshape(-1))
    r = np.arange(128)
    q = np.arange(16) % 4
    c["c_trimask_s"] = np.where(r[:, None] > q[None, :], NEG, 0.0).astype(np.float32).astype(bf)
    c["c_pidx"] = np.arange(128, dtype=np.float32).reshape(128, 1)
    return c


def run_all(inp, NT, NPAST, n_cores=8):
    NB = NT // 256
    NBP = max(8, NB)
    ck = np.ascontiguousarray(np.asarray(inp["cache_k"], np.float32).reshape(-1, 256))
    cv = np.ascontiguousarray(np.asarray(inp["cache_v"], np.float32).reshape(-1, 256))
    key = (NT, NBP, NPAST, ck.shape[0])
    if key not in _NC_CACHE:
        _NC_CACHE[key] = build(NT, NBP, with_sample=True, NPAST=NPAST, NPOOLROWS=ck.shape[0])
    nc = _NC_CACHE[key]
    shared = {}
    role_c = [_consts(NT, NBP, 0), _consts(NT, NBP, 1)]
    shared.update(_consts_sample(NPAST))
    shared.update(_weights(inp))
    shared["cache_k"] = ck
    shared["cache_v"] = cv
    xpr = np.asarray(inp["x_prompt"], np.float32)
    xsm = np.asarray(inp["x_sample"], np.float32)
    sc = np.asarray(inp["state_conv"], np.float32)
    ssm = np.asarray(inp["state_ssm"], np.float32)
    pt = np.asarray(inp["page_table"], np.int32)
    B = xpr.shape[0]
    in_maps = []
    for c in range(n_cores):
        m = dict(shared)
        m.update(role_c[c // B])
        m["xp"] = np.ascontiguousarray(xpr[c % B])
        sl = slice(4 * c, 4 * c + 4)
        m["xs"] = np.ascontiguousarray(xsm[sl].reshape(16, D))
        m["sconv"] = np.ascontiguousarray(sc[0, sl].reshape(12, CD))
        m["sssm"] = np.ascontiguousarray(ssm[0, sl].reshape(4, DIN, NST))
        m["ptab"] = np.ascontiguousarray(pt[sl])
        in_maps.append(m)
    res = run_bass_kernel_spmd(nc, in_maps, core_ids=list(range(n_cores))).results
    y_prompt = np.empty((B, NT, D), np.float32)
    for b in range(B):
        for role in range(2):
            yc = res[b + B * role]["yp"]
            for i in range(NT // 256):
                sb_ = _sub_of(i, role)
                y_prompt[b, sb_ * 128:(sb_ + 1) * 128] = yc[i * 128:(i + 1) * 128]
    k_prompt = np.stack([res[b]["kp"] for b in range(B)]).reshape(B, NT, 4, 64).astype(np.float32)
    v_prompt = np.stack([res[b]["vp"] for b in range(B)]).reshape(B, NT, 4, 64).astype(np.float32)
    conv_prompt = np.stack([res[b]["convp"] for b in range(B)])[None].astype(np.float32)
    ssm_prompt = np.stack([res[b]["ssmp"] for b in range(B)]).reshape(1, B, NH, HP, NST).astype(np.float32)
    nS = 4 * n_cores
    y_sample = np.concatenate([res[c]["ys"] for c in range(n_cores)]).reshape(nS, 4, D).astype(np.float32)
    conv_sample = np.concatenate([res[c]["convs"] for c in range(n_cores)]).reshape(1, nS, 3, CD).astype(np.float32)
    ssm_sample = np.concatenate([res[c]["ssms"] for c in range(n_cores)]).reshape(1, nS, NH, HP, NST).astype(np.float32)
    k_sample = np.concatenate([res[c]["ks"] for c in range(n_cores)]).reshape(nS, 4, 4, 64).astype(np.float32)
    v_sample = np.concatenate([res[c]["vs"] for c in range(n_cores)]).reshape(nS, 4, 4, 64).astype(np.float32)
    return (y_prompt, y_sample, conv_prompt, ssm_prompt, k_prompt, v_prompt,
            conv_sample, ssm_sample, k_sample, v_sample)


def kernel(**inputs):
    return run_all(inputs, 4096, 8192)
```
